# Optimizing a Trainium2 kernel written in Bass

```python
import math
import jax, jax.numpy as jnp
from jax import lax
import numpy as np

D_MODEL = 2048
BATCH = 2
SEQ = 4096
DEPTH = 1

HEAD_DIM = 128
N_Q_HEADS = 16
N_KV_HEADS = 4
GROUP = N_Q_HEADS // N_KV_HEADS
WINDOW = 128
BLK = 128
ROT_DIM = HEAD_DIM // 4
ROPE_THETA = 500000.0
D_RNN = ((4 * D_MODEL // 3 + 255) // 256) * 256
N_RNN_BLOCKS = 16
RNN_BW = D_RNN // N_RNN_BLOCKS
CONV_W = 4
LRU_C = 8.0
D_FF = ((8 * D_MODEL // 3 + 255) // 256) * 256
EPS = 1e-6
NEG = -1e30

Q_W = N_Q_HEADS * HEAD_DIM
KV_W = N_KV_HEADS * HEAD_DIM
SPLITS = np.cumsum([Q_W, KV_W, KV_W, D_RNN, D_RNN, D_MODEL]).tolist()
IN_W = SPLITS[-1] + D_MODEL

kernel_name = "hybrid_swa_sink_rglru_swiglu"


def rms_norm(x, g):
    xf = x.astype(jnp.float32)
    y = xf * lax.rsqrt(jnp.mean(xf * xf, axis=-1, keepdims=True) + EPS)
    return (y * g.astype(jnp.float32)).astype(x.dtype)


def partial_rope(x, cos, sin):
    xf = x.astype(jnp.float32)
    half = ROT_DIM // 2
    x1, x2, rest = xf[..., :half], xf[..., half:ROT_DIM], xf[..., ROT_DIM:]
    out = jnp.concatenate([x1 * cos - x2 * sin, x2 * cos + x1 * sin, rest], axis=-1)
    return out.astype(x.dtype)


def window_attention(q, k, v, sinks):
    B, S = q.shape[0], q.shape[1]
    nb = S // BLK
    qb = q.reshape(B, nb, BLK, N_KV_HEADS, GROUP, HEAD_DIM)

    def with_prev(t):
        tb = t.reshape(B, nb, BLK, N_KV_HEADS, HEAD_DIM)
        prev = jnp.pad(tb, ((0, 0), (1, 0), (0, 0), (0, 0), (0, 0)))[:, :-1]
        return jnp.concatenate([prev, tb], axis=2)

    kw, vw = with_prev(k), with_prev(v)
    scale = 1.0 / math.sqrt(HEAD_DIM)
    s = jnp.einsum('bnqkgd,bnjkd->bnkgqj', qb, kw).astype(jnp.float32) * scale
    qi = jnp.arange(BLK)[:, None]
    kj = jnp.arange(2 * BLK)[None, :]
    rel = qi + BLK - kj
    band = (rel >= 0) & (rel < WINDOW)
    real = (jnp.arange(nb)[:, None, None] > 0) | (kj >= BLK)[None]
    mask = (band[None] & real)[None, :, None, None]
    s = jnp.where(mask, s, NEG)
    sink = jnp.broadcast_to(
        sinks.astype(jnp.float32).reshape(N_KV_HEADS, GROUP)[None, None, :, :, None, None],
        s.shape[:-1] + (1,))
    p = jax.nn.softmax(jnp.concatenate([s, sink], axis=-1), axis=-1)[..., :-1]
    o = jnp.einsum('bnkgqj,bnjkd->bnqkgd', p.astype(v.dtype), vw)
    return o.reshape(B, S, Q_W)


def causal_depthwise_conv(u, w, b):
    S = u.shape[1]
    up = jnp.pad(u, ((0, 0), (CONV_W - 1, 0), (0, 0)))
    y = b
    for tap in range(CONV_W):
        y = y + w[tap] * up[:, tap:tap + S]
    return y


def rg_lru(u, w_r, b_r, w_i, b_i, lam):
    B, S = u.shape[0], u.shape[1]
    uf = u.astype(jnp.float32)
    ub = uf.reshape(B, S, N_RNN_BLOCKS, RNN_BW)
    r = jax.nn.sigmoid(jnp.einsum('bsnc,ncd->bsnd', ub, w_r.astype(jnp.float32)).reshape(B, S, D_RNN)
                       + b_r.astype(jnp.float32))
    i = jax.nn.sigmoid(jnp.einsum('bsnc,ncd->bsnd', ub, w_i.astype(jnp.float32)).reshape(B, S, D_RNN)
                       + b_i.astype(jnp.float32))
    log_a = -LRU_C * r * jax.nn.softplus(-lam.astype(jnp.float32))
    a = jnp.exp(log_a)
    bterm = jnp.sqrt(jnp.maximum(-jnp.expm1(2.0 * log_a), 0.0)) * (i * uf)

    def combine(left, right):
        a1, b1 = left
        a2, b2 = right
        return a1 * a2, a2 * b1 + b2

    _, h = lax.associative_scan(combine, (a, bterm), axis=1)
    return h.astype(u.dtype)


def setup_inputs(seed: int = 0) -> dict:
    key = jax.random.key(seed)
    ks = jax.random.split(key, 24)
    f32 = jnp.float32
    nrm = lambda k, shape, fan: jax.random.normal(k, shape, f32) * (fan ** -0.5)
    L = DEPTH
    x = jax.random.normal(ks[0], (BATCH, SEQ, D_MODEL), f32)
    offs = jax.random.randint(ks[1], (BATCH, 1), 0, 1024, dtype=jnp.int32)
    positions = offs + jnp.arange(SEQ, dtype=jnp.int32)[None, :]
    a0 = jax.random.uniform(ks[2], (L, D_RNN), f32, 0.9, 0.999)
    return {
        "x": x,
        "positions": positions,
        "norm1_g": 1.0 + 0.02 * jax.random.normal(ks[3], (L, D_MODEL), f32),
        "w_in": nrm(ks[4], (L, D_MODEL, IN_W), D_MODEL),
        "b_gates": 0.02 * jax.random.normal(ks[5], (L, 2 * D_MODEL), f32),
        "q_norm_g": 1.0 + 0.02 * jax.random.normal(ks[6], (L, HEAD_DIM), f32),
        "k_norm_g": 1.0 + 0.02 * jax.random.normal(ks[7], (L, HEAD_DIM), f32),
        "sinks": 0.5 * jax.random.normal(ks[8], (L, N_Q_HEADS), f32),
        "conv_w": nrm(ks[9], (L, CONV_W, D_RNN), CONV_W),
        "conv_b": 0.02 * jax.random.normal(ks[10], (L, D_RNN), f32),
        "w_rgate": nrm(ks[11], (L, N_RNN_BLOCKS, RNN_BW, RNN_BW), RNN_BW),
        "b_rgate": 0.02 * jax.random.normal(ks[12], (L, D_RNN), f32),
        "w_igate": nrm(ks[13], (L, N_RNN_BLOCKS, RNN_BW, RNN_BW), RNN_BW),
        "b_igate": 0.02 * jax.random.normal(ks[14], (L, D_RNN), f32),
        "lru_lambda": jnp.log(a0) - jnp.log1p(-a0),
        "w_attn_proj": nrm(ks[15], (L, Q_W, D_MODEL), Q_W),
        "w_lru_proj": nrm(ks[16], (L, D_RNN, D_MODEL), D_RNN),
        "w_out": nrm(ks[17], (L, D_MODEL, D_MODEL), D_MODEL),
        "norm2_g": 1.0 + 0.02 * jax.random.normal(ks[18], (L, D_MODEL), f32),
        "w_ffn_gate": nrm(ks[19], (L, D_MODEL, D_FF), D_MODEL),
        "w_ffn_up": nrm(ks[20], (L, D_MODEL, D_FF), D_MODEL),
        "w_ffn_down": nrm(ks[21], (L, D_FF, D_MODEL), D_FF),
    }


def reference(x, positions, norm1_g, w_in, b_gates, q_norm_g, k_norm_g, sinks,
              conv_w, conv_b, w_rgate, b_rgate, w_igate, b_igate, lru_lambda,
              w_attn_proj, w_lru_proj, w_out, norm2_g, w_ffn_gate, w_ffn_up, w_ffn_down):
    B, S = x.shape[0], x.shape[1]
    inv_freq = ROPE_THETA ** (-jnp.arange(0, ROT_DIM, 2, dtype=jnp.float32) / ROT_DIM)
    ang = positions.astype(jnp.float32)[..., None] * inv_freq
    cos, sin = jnp.cos(ang)[:, :, None, :], jnp.sin(ang)[:, :, None, :]

    h = x
    for l in range(DEPTH):
        xn = rms_norm(h, norm1_g[l])
        z = xn @ w_in[l]
        q, k, v, u, gr, ga, gl = jnp.split(z, SPLITS, axis=-1)
        g_attn = jax.nn.sigmoid(ga + b_gates[l][:D_MODEL])
        g_lru = jax.nn.sigmoid(gl + b_gates[l][D_MODEL:])

        q = rms_norm(q.reshape(B, S, N_Q_HEADS, HEAD_DIM), q_norm_g[l])
        k = rms_norm(k.reshape(B, S, N_KV_HEADS, HEAD_DIM), k_norm_g[l])
        q = partial_rope(q, cos, sin)
        k = partial_rope(k, cos, sin)
        v = v.reshape(B, S, N_KV_HEADS, HEAD_DIM)
        attn = window_attention(q, k, v, sinks[l])

        uc = causal_depthwise_conv(u, conv_w[l], conv_b[l])
        rec = rg_lru(uc, w_rgate[l], b_rgate[l], w_igate[l], b_igate[l], lru_lambda[l])
        rec = rec * jax.nn.gelu(gr)

        merged = g_attn * (attn @ w_attn_proj[l]) + g_lru * (rec @ w_lru_proj[l])
        h = h + merged @ w_out[l]

        hn = rms_norm(h, norm2_g[l])
        ff = (jax.nn.silu(hn @ w_ffn_gate[l]) * (hn @ w_ffn_up[l])) @ w_ffn_down[l]
        h = h + ff
    return h
```

```python
import contextlib
import math
import numpy as np
import concourse.bass as bass
import concourse.mybir as mybir
from concourse.bass_utils import run_bass_kernel_spmd

F32 = mybir.dt.float32
BF16 = mybir.dt.bfloat16
I32 = mybir.dt.int32
AF = mybir.ActivationFunctionType
ALU = mybir.AluOpType

D = 2048
S = 4096
T = 1024
TE = 1152
HD = 128
NQ = 16
NKV = 4
DR = 2816
NB_RNN = 16
BW = 176
DFF = 5632
KC = 16
RC = 22
FC = 44
EPS = 1e-6
THETA = 500000.0
Q_W = 2048
KV_W = 512
SPL = np.cumsum([Q_W, KV_W, KV_W, DR, DR, D]).tolist()

ENGS = ("pe", "act", "dve", "pool", "sp")
SAME_ENGINE_SYNC = True
NRING = 4
SLAB = 4096


class Res:
    __slots__ = ("name", "w", "r")

    def __init__(self, name=""):
        self.name = name
        self.w = None
        self.r = {}


class DSem:
    __slots__ = ("sem", "cnt")

    def __init__(self, sem):
        self.sem = sem
        self.cnt = 0


class Prog:
    def __init__(self, nc, stack):
        self.nc = nc
        self.stack = stack
        self.sem = {e: stack.enter_context(nc.semaphore("s_" + e)) for e in ENGS}
        self.cnt = {e: 0 for e in ENGS}
        self.ops = {e: [] for e in ENGS}
        self.waited = {e: {} for e in ENGS}
        self.pending = {e: [] for e in ENGS}
        self.ndsem = 0
        self.dma_evs = []

    def dsem(self):
        self.ndsem += 1
        return DSem(self.stack.enter_context(self.nc.semaphore("d%d" % self.ndsem)))

    def _collect(self, e, reads, writes):
        evs = []
        for r in reads:
            if r.w is not None:
                evs.append(r.w)
        for w in writes:
            for pe_ in ENGS:
                if pe_ != e and any(w is p for p in self.pending[pe_]):
                    raise RuntimeError("write to %s with pending unsignaled reads on %s" % (w.name, pe_))
            if w.w is not None:
                evs.append(w.w)
            for k, ev in w.r.items():
                if k == e and e != "pe":
                    continue
                evs.append(ev)
        waits = []
        wd = self.waited[e]
        for (sem, val, src) in evs:
            if src == e and (e == "pe" or not SAME_ENGINE_SYNC):
                continue
            if wd.get(sem, 0) >= val:
                continue
            wd[sem] = val
            waits.append((sem, val))
        return waits

    def op(self, e, fn, reads=(), writes=(), signal=True):
        waits = self._collect(e, reads, writes)
        ev = None
        if signal:
            self.cnt[e] += 1
            ev = (self.sem[e], self.cnt[e], e)
            for r in self.pending[e]:
                r.r[e] = ev
            self.pending[e] = []
            for r in reads:
                r.r[e] = ev
            for w in writes:
                w.w = ev
                w.r = {}
            self.ops[e].append((waits, fn, (self.sem[e], 1)))
        else:
            for r in reads:
                self.pending[e].append(r)
            self.ops[e].append((waits, fn, None))
        return ev

    def dma(self, q, fn, ds, reads=(), writes=(), inc=16):
        waits = self._collect(q, reads, writes)
        ds.cnt += inc
        ev = (ds.sem, ds.cnt, "dma")
        self.ops[q].append((waits, fn, (ds.sem, inc)))
        for r in reads:
            r.r[("dma", id(ds))] = ev
        for w in writes:
            w.w = ev
            w.r = {}
        self.dma_evs.append(ev)
        return ev

    def wait(self, e, evs):
        waits = []
        wd = self.waited[e]
        for (sem, val, src) in evs:
            if wd.get(sem, 0) >= val:
                continue
            wd[sem] = val
            waits.append((sem, val))
        if waits:
            self.ops[e].append((waits, None, None))

    def barrier(self, group=("pe", "act", "dve", "sp")):
        for e in group:
            assert not self.pending[e], "barrier with pending unsignaled ops on " + e
        evs = [(self.sem[o], self.cnt[o], o) for o in group if self.cnt[o] > 0]
        last = {}
        for ev in self.dma_evs:
            last[ev[0]] = ev
        self.dma_evs = list(last.values())
        for e in group:
            self.wait(e, [ev for ev in evs if ev[2] != e] + self.dma_evs)


    def check(self):
        val = {}
        ptr = {e: 0 for e in ENGS}
        total = sum(len(v) for v in self.ops.values())
        done = 0
        while done < total:
            prog = False
            for e in ENGS:
                lst = self.ops[e]
                while ptr[e] < len(lst):
                    waits, fn, inc = lst[ptr[e]]
                    if any(val.get(s, 0) < v for s, v in waits):
                        break
                    if inc is not None:
                        val[inc[0]] = val.get(inc[0], 0) + inc[1]
                    ptr[e] += 1
                    done += 1
                    prog = True
            if not prog:
                msg = []
                for e in ENGS:
                    if ptr[e] < len(self.ops[e]):
                        waits, fn, inc = self.ops[e][ptr[e]]
                        msg.append("%s@%d waits %s" % (e, ptr[e], [(s.name, v, val.get(s, 0)) for s, v in waits if val.get(s, 0) < v]))
                raise RuntimeError("DEADLOCK: " + "; ".join(msg))
        return True

    def emit(self):
        self.check()
        nc = self.nc
        ops = self.ops

        def run(eng, lst):
            for waits, fn, inc in lst:
                for sem, val in waits:
                    eng.wait_ge(sem, val)
                if fn is None:
                    continue
                ins = fn(eng)
                if inc is not None:
                    ins.then_inc(inc[0], inc[1])

        with nc.Block() as block:
            @block.sync
            def _(eng):
                run(eng, ops["sp"])

            @block.scalar
            def _(eng):
                run(eng, ops["act"])

            @block.vector
            def _(eng):
                run(eng, ops["dve"])

            @block.gpsimd
            def _(eng):
                run(eng, ops["pool"])

            @block.tensor
            def _(eng):
                run(eng, ops["pe"])


def gate_klist(c):
    n_lo = (128 * c) // BW
    n_hi = (128 * c + 127) // BW
    k_lo = (BW * n_lo) // 128
    k_hi = (BW * (n_hi + 1) - 1) // 128
    return list(range(k_lo, k_hi + 1))


CC = {}
_o = 0
for _name, _n in [("qg", 1), ("kg", 1), ("bga", 16), ("bgl", 16), ("cw", 88), ("cb", 22), ("br", 22),
                  ("bi", 22), ("lam", 22), ("invf", 1), ("sinks", 16), ("sel", 8)]:
    CC[_name] = _o
    _o += _n
NCONST = _o


def host_consts(inp, core):
    c = np.zeros((128, NCONST), np.float32)
    j = core % 4
    b = core // 4
    c[:, CC["qg"]] = inp["q_norm_g"][0]
    c[:, CC["kg"]] = inp["k_norm_g"][0]
    c[:, CC["bga"]:CC["bga"] + 16] = inp["b_gates"][0][:D].reshape(16, 128).T
    c[:, CC["bgl"]:CC["bgl"] + 16] = inp["b_gates"][0][D:].reshape(16, 128).T
    for tap in range(4):
        c[:, CC["cw"] + tap * 22:CC["cw"] + (tap + 1) * 22] = inp["conv_w"][0][tap].reshape(22, 128).T
    c[:, CC["cb"]:CC["cb"] + 22] = inp["conv_b"][0].reshape(22, 128).T
    c[:, CC["br"]:CC["br"] + 22] = inp["b_rgate"][0].reshape(22, 128).T
    c[:, CC["bi"]:CC["bi"] + 22] = inp["b_igate"][0].reshape(22, 128).T
    c[:, CC["lam"]:CC["lam"] + 22] = inp["lru_lambda"][0].reshape(22, 128).T
    invf = (np.float32(THETA) ** (-np.arange(0, 32, 2, dtype=np.float32) / np.float32(32))).astype(np.float32)
    c[0:16, CC["invf"]] = invf
    c[16:32, CC["invf"]] = invf
    c[:, CC["sinks"]:CC["sinks"] + 16] = inp["sinks"][0][None, :]
    for k in range(8):
        c[:, CC["sel"] + k] = 1.0 if (k // 4 == b and k < core) else 0.0
    return c


def host_cmat(core):
    j = core % 4
    m = np.zeros((128, 2 * 128 + 3 * 512 + 32), np.float32)
    m[:, 0:128] = np.eye(128, dtype=np.float32)
    m[:, 128:256] = 1.0
    p = np.arange(128)[:, None]
    f = np.arange(128)[None, :]
    cur = (p <= f).astype(np.float32)
    prev = (p > f).astype(np.float32)
    prev0 = prev if j > 0 else np.zeros_like(prev)
    m[:, 256:768] = np.tile(cur, (1, 4))
    m[:, 768:1280] = np.tile(prev, (1, 4))
    m[:, 1280:1792] = np.tile(prev0, (1, 4))
    rot = np.zeros((128, 32), np.float32)
    for mm_ in range(16):
        rot[mm_ + 16, mm_] = -1.0
        rot[mm_, mm_ + 16] = 1.0
    m[:, 1792:1824] = rot
    return m


NCMAT = 1824


STAGES = ("A", "L", "ML", "AT", "MA", "C", "D")


def build_nc(debug=(), stop_after=None):
    nc = bass.Bass("TRN2", target_bir_lowering=False)
    specs = []
    W = {"off": 0, "i": 0, "ap": None}

    x_own = nc.dram_tensor("x_own", [T, D], F32, kind="ExternalInput").ap()
    x_halo = nc.dram_tensor("x_halo", [128, D], F32, kind="ExternalInput").ap()
    pos_d = nc.dram_tensor("pos", [32, TE], I32, kind="ExternalInput").ap()
    consts_d = nc.dram_tensor("consts", [128, NCONST], F32, kind="ExternalInput").ap()
    cmat_d = nc.dram_tensor("cmat", [128, NCMAT], F32, kind="ExternalInput").ap()
    g12_d = nc.dram_tensor("g12", [2, 128, D], F32, kind="ExternalInput").ap()
    out_d = nc.dram_tensor("out", [T, D], F32, kind="ExternalOutput").ap()
    r1_d = nc.dram_tensor("r1_spill", [RC, 128, T], BF16).ap()
    cc_in = nc.dram_tensor("cc_in", [128, 64], F32).ap()
    cc_out = nc.dram_tensor("cc_out", [8 * 128, 64], F32).ap()
    dbg_d = {}
    for name, shape in (("xn", [128, KC * TE]), ("rec", [128, RC * T]), ("mrg", [128, KC * T]),
                        ("att", [128, KC * T]), ("hh", [128, 8 * D]), ("misc", [128, 4096])):
        if name in debug:
            dbg_d[name] = nc.dram_tensor("dbg_" + name, shape, F32, kind="ExternalOutput").ap()

    with contextlib.ExitStack() as st:
        P = Prog(nc, st)
        big = nc.alloc_sbuf_tensor("big", [128, 65536], BF16)
        xn_t = nc.alloc_sbuf_tensor("xn", [128, KC * TE], BF16)
        ring = [nc.alloc_sbuf_tensor("ring%d" % i, [128, SLAB], BF16) for i in range(NRING)]
        ring_r = [Res("ring%d" % i) for i in range(NRING)]
        ring_s = [P.dsem() for _ in range(NRING)]
        cst = nc.alloc_sbuf_tensor("cst", [128, NCONST], F32)
        cmat = nc.alloc_sbuf_tensor("cmatb", [128, NCMAT], BF16)
        sml = nc.alloc_sbuf_tensor("sml", [128, 320], F32)
        psum = [nc.alloc_psum_tensor("ps%d" % i, [128, 512], F32) for i in range(8)]
        ps_r = [Res("ps%d" % i) for i in range(8)]
        cst_r = Res("cst")
        cmat_r = Res("cmat")
        ds_c = P.dsem()

        def view(off, shape, dt=BF16):
            esz = 2 if dt == BF16 else 4
            n = int(np.prod(shape[1:]))
            a = big[:, off // 2: off // 2 + n * esz // 2]
            if dt != BF16:
                a = a.bitcast(dt)
            if len(shape) == 3:
                a = a.rearrange("p (a b) -> p a b", a=shape[1])
            elif len(shape) == 4:
                a = a.rearrange("p (a b c) -> p a b c", a=shape[1], b=shape[2])
            return a

        xn = xn_t[:, :].rearrange("p (k t) -> p k t", k=KC)
        xn_r = [[Res("xn%d_%d" % (k, t)) for t in range(9)] for k in range(KC)]
        ident = cmat[:, 0:128]
        ones_b = cmat[:, 128:256]
        mask_cur = cmat[:, 256:768]
        mask_prev = cmat[:, 768:1280]
        mask_prev0 = cmat[:, 1280:1792]
        rotm = cmat[0:32, 1792:1824]

        def cc(name, i=0):
            return cst[:, CC[name] + i: CC[name] + i + 1]

        SS, RSTD, NSP8, NSP16, ESINK, CCS, HS, TMPA, TMPB, CHA, CHB = 0, 16, 32, 54, 76, 256, 136, 158, 180, 202, 224
        sml_r = Res("sml")

        def wload(n_el, spec):
            b = W["i"] % NRING
            W["i"] += 1
            off = W["off"]
            W["off"] += 128 * n_el
            specs.append((n_el, spec))

            def fn(e, b=b, off=off, n_el=n_el):
                src = W["ap"][off: off + 128 * n_el].rearrange("(p n) -> p n", p=128)
                return e.dma_start(out=ring[b][:, 0:n_el], in_=src, max_dma_last_dim=8192)
            P.dma("pool", fn, ring_s[b], writes=[ring_r[b]])
            return ring[b], ring_r[b]

        def mm(out, lhsT, rhs, start, stop, reads, wres, signal=None):
            if signal is None:
                signal = stop
            P.op("pe", lambda e: e.matmul(out, lhsT, rhs, start=start, stop=stop),
                 reads=reads, writes=[wres], signal=signal)

        def dbg_dump(name, src_ap, res_list, width):
            if name not in dbg_d:
                return
            P.barrier()
            soff = {"xn": 98304, "rec": 114688, "mrg": 8192, "att": 8192, "hh": 73728}[name]
            ds = [P.dsem(), P.dsem()]
            nchunk = (width + 2047) // 2048
            tv = [view(soff, [128, 2048], F32), view(soff + 8192, [128, 2048], F32)]
            tr = [Res("dbgt0"), Res("dbgt1")]
            for i in range(nchunk):
                lo = i * 2048
                hi = min(width, lo + 2048)
                P.op("dve", lambda e, i=i, lo=lo, hi=hi: e.tensor_copy(out=tv[i % 2][:, 0:hi - lo], in_=src_ap[:, lo:hi]),
                     reads=res_list, writes=[tr[i % 2]])
                P.dma("sp", lambda e, i=i, lo=lo, hi=hi: e.dma_start(out=dbg_d[name][:, lo:hi], in_=tv[i % 2][:, 0:hi - lo]),
                      ds[i % 2], reads=[tr[i % 2]])
            P.barrier()

        P.dma("sp", lambda e: e.dma_start(out=cst[:, :], in_=consts_d), ds_c, writes=[cst_r])
        P.dma("pool", lambda e: e.dma_start(out=cmat[:, :], in_=cmat_d, max_dma_last_dim=4096), P.dsem(), writes=[cmat_r])
        P.op("act", lambda e: e.activation(out=sml[:, TMPA:TMPA + RC], in_=cst[:, CC["lam"]:CC["lam"] + RC], func=AF.Exp, scale=-1.0),
             reads=[cst_r], writes=[sml_r])
        P.op("act", lambda e: e.activation(out=sml[:, TMPB:TMPB + RC], in_=sml[:, TMPA:TMPA + RC], func=AF.Ln, bias=1.0),
             reads=[sml_r], writes=[sml_r])
        P.op("dve", lambda e: e.tensor_scalar(out=sml[:, NSP8:NSP8 + RC], in0=sml[:, TMPB:TMPB + RC], scalar1=-8.0, scalar2=None, op0=ALU.mult),
             reads=[sml_r], writes=[sml_r])
        P.op("dve", lambda e: e.tensor_scalar(out=sml[:, NSP16:NSP16 + RC], in0=sml[:, TMPB:TMPB + RC], scalar1=-16.0, scalar2=None, op0=ALU.mult),
             reads=[sml_r], writes=[sml_r])
        P.op("act", lambda e: e.activation(out=sml[:, ESINK:ESINK + 16], in_=cst[:, CC["sinks"]:CC["sinks"] + 16], func=AF.Exp),
             reads=[cst_r], writes=[sml_r])

        def norm_transpose(src_tile_fn, ntiles, gsel, dst, dst_r, dst_col0, a_off):
            xsb = [view(a_off, [128, D]), view(a_off + 4096, [128, D])]
            xsb_r = [Res("xsb0"), Res("xsb1")]
            sq = view(a_off + 8192, [128, D])
            sq_r = Res("sq")
            gb = view(a_off + 12288, [128, D], F32)
            gb_r = Res("gb")
            P.dma("sp", lambda e: e.dma_start(out=gb, in_=g12_d[gsel]), P.dsem(), writes=[gb_r])
            for i in range(ntiles):
                src, src_r = src_tile_fn(i)
                P.op("act", lambda e, src=src, i=i: e.activation(out=sq, in_=src, func=AF.Square, accum_out=sml[:, SS + i:SS + i + 1]),
                     reads=[src_r], writes=[sq_r, sml_r])
                P.op("dve", lambda e, i=i: e.tensor_scalar(out=sml[:, TMPA + i:TMPA + i + 1], in0=sml[:, SS + i:SS + i + 1],
                                                         scalar1=1.0 / D, scalar2=EPS, op0=ALU.mult, op1=ALU.add),
                     reads=[sml_r], writes=[sml_r])
                P.op("act", lambda e, i=i: e.activation(out=sml[:, TMPB + i:TMPB + i + 1], in_=sml[:, TMPA + i:TMPA + i + 1], func=AF.Sqrt),
                     reads=[sml_r], writes=[sml_r])
                P.op("dve", lambda e, i=i: e.reciprocal(out=sml[:, RSTD + i:RSTD + i + 1], in_=sml[:, TMPB + i:TMPB + i + 1]),
                     reads=[sml_r], writes=[sml_r])
                b = i % 2
                P.op("dve", lambda e, src=src, i=i, b=b: e.scalar_tensor_tensor(out=xsb[b], in0=src, scalar=sml[:, RSTD + i:RSTD + i + 1],
                                                                              in1=gb, op0=ALU.mult, op1=ALU.mult),
                     reads=[src_r, sml_r, gb_r], writes=[xsb_r[b]])
                for hh in range(2):
                    bank = (2 * i + hh) % 4
                    pb = psum[bank][:, :].bitcast(BF16).rearrange("p (a b) -> p a b", a=8)
                    for k8 in range(8):
                        kc = hh * 8 + k8
                        P.op("pe", lambda e, pb=pb, k8=k8, kc=kc, b=b: e.transpose(pb[:, k8, :], xsb[b][:, kc * 128:(kc + 1) * 128], ident),
                             reads=[xsb_r[b], cmat_r], writes=[ps_r[bank]], signal=(k8 == 7))
                    c0 = dst_col0 + i * 128
                    eng = "act" if hh == 0 else "dve"
                    if eng == "act":
                        P.op("act", lambda e, pb=pb, hh=hh, c0=c0: e.activation(out=dst[:, hh * 8:(hh + 1) * 8, c0:c0 + 128], in_=pb, func=AF.Copy),
                             reads=[ps_r[bank]], writes=[dst_r[k][i] for k in range(hh * 8, hh * 8 + 8)])
                    else:
                        P.op("dve", lambda e, pb=pb, hh=hh, c0=c0: e.tensor_copy(out=dst[:, hh * 8:(hh + 1) * 8, c0:c0 + 128], in_=pb),
                             reads=[ps_r[bank]], writes=[dst_r[k][i] for k in range(hh * 8, hh * 8 + 8)])

        xt = [view(0, [128, D], F32), view(8192, [128, D], F32)]
        xt_r = [Res("xt0"), Res("xt1")]
        xt_s = [P.dsem(), P.dsem()]

        def a_src(i):
            b = i % 2
            if i == 0:
                P.dma("sp", lambda e: e.dma_start(out=xt[0], in_=x_halo), xt_s[0], writes=[xt_r[0]])
            else:
                P.dma("sp", lambda e, i=i, b=b: e.dma_start(out=xt[b], in_=x_own[(i - 1) * 128:i * 128, :]), xt_s[b], writes=[xt_r[b]])
            return xt[b], xt_r[b]
        norm_transpose(a_src, 9, 0, xn, xn_r, 0, 16384)
        dbg_dump("xn", xn_t[:, :], [r for k in range(KC) for r in xn_r[k]], KC * TE)
        if stop_after == "A":
            return finish(nc, P, W, specs, out_d, None)

        def xn_reads(tok0, n):
            t0, t1 = tok0 // 128, (tok0 + n - 1) // 128
            return lambda kc: [xn_r[kc][t] for t in range(t0, t1 + 1)]

        P.barrier()
        R0 = view(0, [128, RC, T])
        R0_r = [Res("R0_%d" % c) for c in range(RC)]
        UC = view(45056, [128, 8, T])
        UC_r = [Res("UC%d" % i) for i in range(8)]
        GT = view(61440, [128, 4, T])
        GT_r = [Res("GT%d" % i) for i in range(4)]
        UP = [view(69632, [128, 1040], F32), view(69632 + 4160, [128, 1040], F32)]
        UP_r = [Res("UP0"), Res("UP1")]
        f32t = {}
        for i, nm in enumerate(("acc", "rt", "it", "at", "st", "bt", "ht", "cum", "zeros")):
            f32t[nm] = (view(77952 + 4096 * i, [128, T], F32), Res(nm))
        R1T = [view(114816, [128, T]), view(114816 + 2048, [128, T])]
        R1T_r = [Res("R1T0"), Res("R1T1")]
        r1_s = [P.dsem(), P.dsem()]
        ccs = sml[:, CCS:CCS + 64]
        P.op("dve", lambda e: e.memset(ccs, 0.0), writes=[sml_r])
        P.op("dve", lambda e: e.memset(f32t["zeros"][0], 0.0), writes=[f32t["zeros"][1]])

        def w_in_cols(lo, n):
            return lambda inp: inp["w_in"][0][:, lo:lo + n]

        def spec_k16(colfn_list):
            def fn(inp):
                parts = []
                for wname, lo in colfn_list:
                    wmat = inp[wname][0]
                    parts.append(wmat[:, lo:lo + 128].reshape(KC, 128, 128).transpose(1, 0, 2).reshape(128, KC * 128))
                return np.concatenate(parts, axis=1)
            return fn

        def gate_spec(c):
            kl = gate_klist(c)

            def fn(inp):
                parts = []
                for wname in ("w_rgate", "w_igate"):
                    full = np.zeros((len(kl) * 128, 128), np.float32)
                    wg = inp[wname][0]
                    for n in range(NB_RNN):
                        r0, r1 = n * BW, (n + 1) * BW
                        ro0, ro1 = max(r0, kl[0] * 128), min(r1, (kl[-1] + 1) * 128)
                        co0, co1 = max(r0, c * 128), min(r1, (c + 1) * 128)
                        if ro0 < ro1 and co0 < co1:
                            full[ro0 - kl[0] * 128:ro1 - kl[0] * 128, co0 - c * 128:co1 - c * 128] = wg[n][ro0 - r0:ro1 - r0, co0 - r0:co1 - r0]
                    parts.append(full.reshape(len(kl), 128, 128).transpose(1, 0, 2).reshape(128, len(kl) * 128))
                return np.concatenate(parts, axis=1)
            return fn

        def lru_gates(c):
            kl = gate_klist(c)
            nk = len(kl)
            slab, slab_r = wload(2 * nk * 128, gate_spec(c))
            sv = slab[:, 0:2 * nk * 128].rearrange("p (g k m) -> p g k m", g=2, k=nk)
            for gi in range(2):
                for hf in range(2):
                    bank = 4 + gi * 2 + hf
                    for ki, k in enumerate(kl):
                        mm(psum[bank][:, :], sv[:, gi, ki, :], UC[:, k % 8, hf * 512:(hf + 1) * 512],
                           ki == 0, ki == nk - 1, [slab_r, UC_r[k % 8]], ps_r[bank])
            rt, rt_r = f32t["rt"]
            it, it_r = f32t["it"]
            at, at_r = f32t["at"]
            s_t, st_r = f32t["st"]
            bt, bt_r = f32t["bt"]
            ht, ht_r = f32t["ht"]
            cum, cum_r = f32t["cum"]
            zt, zt_r = f32t["zeros"]
            for hf in range(2):
                sl = slice(hf * 512, (hf + 1) * 512)
                P.op("act", lambda e, hf=hf, sl=sl: e.activation(out=rt[:, sl], in_=psum[4 + hf][:, :], func=AF.Sigmoid, bias=cc("br", c)),
                     reads=[ps_r[4 + hf], cst_r], writes=[rt_r])
                P.op("act", lambda e, hf=hf, sl=sl: e.activation(out=it[:, sl], in_=psum[6 + hf][:, :], func=AF.Sigmoid, bias=cc("bi", c)),
                     reads=[ps_r[6 + hf], cst_r], writes=[it_r])
            P.op("act", lambda e: e.activation(out=at, in_=rt, func=AF.Exp, scale=sml[:, NSP8 + c:NSP8 + c + 1]),
                 reads=[rt_r, sml_r], writes=[at_r])
            P.op("act", lambda e: e.activation(out=s_t, in_=rt, func=AF.Exp, scale=sml[:, NSP16 + c:NSP16 + c + 1]),
                 reads=[rt_r, sml_r], writes=[st_r])
            P.op("act", lambda e: e.activation(out=s_t, in_=s_t, func=AF.Sqrt, scale=-1.0, bias=1.0),
                 reads=[st_r], writes=[st_r])
            P.op("pool", lambda e: e.tensor_tensor(out=bt, in0=it, in1=UC[:, c % 8, :], op=ALU.mult),
                 reads=[it_r, UC_r[c % 8]], writes=[bt_r])
            P.op("pool", lambda e: e.tensor_tensor(out=bt, in0=bt, in1=s_t, op=ALU.mult),
                 reads=[bt_r, st_r], writes=[bt_r])
            P.op("dve", lambda e: e.tensor_tensor_scan(out=ht, data0=at, data1=bt, initial=0.0, op0=ALU.mult, op1=ALU.add),
                 reads=[at_r, bt_r], writes=[ht_r])
            P.op("dve", lambda e: e.tensor_tensor_scan(out=cum, data0=at, data1=zt, initial=1.0, op0=ALU.mult, op1=ALU.add),
                 reads=[at_r, zt_r], writes=[cum_r])
            P.op("dve", lambda e: e.tensor_tensor(out=R0[:, c, :], in0=ht, in1=GT[:, c % 4, :], op=ALU.mult),
                 reads=[ht_r, GT_r[c % 4]], writes=[R0_r[c]])
            b = c % 2
            P.op("dve", lambda e, b=b: e.tensor_tensor(out=R1T[b], in0=cum, in1=GT[:, c % 4, :], op=ALU.mult),
                 reads=[cum_r, GT_r[c % 4]], writes=[R1T_r[b]])
            P.dma("sp", lambda e, b=b: e.dma_start(out=r1_d[c], in_=R1T[b]), r1_s[b], reads=[R1T_r[b]])
            P.op("act", lambda e: e.activation(out=sml[:, CCS + c:CCS + c + 1], in_=cum[:, T - 1:T], func=AF.Copy),
                 reads=[cum_r], writes=[sml_r])
            P.op("act", lambda e: e.activation(out=sml[:, CCS + RC + c:CCS + RC + c + 1], in_=ht[:, T - 1:T], func=AF.Copy),
                 reads=[ht_r], writes=[sml_r])

        ug = {}

        def ug_load(c):
            ug[c] = wload(4096, spec_k16([("w_in", SPL[2] + c * 128), ("w_in", SPL[3] + c * 128)]))
        ug_load(0)

        def lru_chunk(c):
            if c + 1 < RC:
                ug_load(c + 1)
            slab, slab_r = ug[c]
            sv = slab[:, :].rearrange("p (g k m) -> p g k m", g=2, k=KC)
            for hf in range(2):
                rd = xn_reads(128 + hf * 512, 512)
                for k in range(KC):
                    mm(psum[hf][:, :], sv[:, 0, k, :], xn[:, k, 128 + hf * 512:128 + (hf + 1) * 512],
                       k == 0, k == KC - 1, [slab_r] + rd(k), ps_r[hf])
            for k in range(KC):
                mm(psum[2][:, 0:8], sv[:, 0, k, :], xn[:, k, 120:128], k == 0, k == KC - 1, [slab_r, xn_r[k][0]], ps_r[2])
            up, up_r = UP[c % 2], UP_r[c % 2]
            P.op("act", lambda e, up=up: e.activation(out=up[:, 0:8], in_=psum[2][:, 0:8], func=AF.Copy), reads=[ps_r[2]], writes=[up_r])
            P.op("act", lambda e, up=up: e.activation(out=up[:, 8:520], in_=psum[0][:, :], func=AF.Copy), reads=[ps_r[0]], writes=[up_r])
            P.op("act", lambda e, up=up: e.activation(out=up[:, 520:1032], in_=psum[1][:, :], func=AF.Copy), reads=[ps_r[1]], writes=[up_r])
            for hf in range(2):
                rd = xn_reads(128 + hf * 512, 512)
                for k in range(KC):
                    mm(psum[3][:, :], sv[:, 1, k, :], xn[:, k, 128 + hf * 512:128 + (hf + 1) * 512],
                       k == 0, k == KC - 1, [slab_r] + rd(k), ps_r[3])
                P.op("act", lambda e, hf=hf: e.activation(out=GT[:, c % 4, hf * 512:(hf + 1) * 512], in_=psum[3][:, :], func=AF.Gelu_apprx_tanh),
                     reads=[ps_r[3]], writes=[GT_r[c % 4]])
            acc, acc_r = f32t["acc"]
            P.op("act", lambda e, up=up: e.activation(out=acc, in_=up[:, 5:5 + T], func=AF.Identity, scale=cc("cw", 0 * 22 + c), bias=cc("cb", c)),
                 reads=[up_r, cst_r], writes=[acc_r])
            for tap in (1, 2):
                P.op("dve", lambda e, up=up, tap=tap: e.scalar_tensor_tensor(out=acc, in0=up[:, 5 + tap:5 + tap + T], scalar=cc("cw", tap * 22 + c),
                                                                             in1=acc, op0=ALU.mult, op1=ALU.add),
                     reads=[up_r, cst_r, acc_r], writes=[acc_r])
            P.op("dve", lambda e, up=up: e.scalar_tensor_tensor(out=UC[:, c % 8, :], in0=up[:, 8:8 + T], scalar=cc("cw", 3 * 22 + c),
                                                               in1=acc, op0=ALU.mult, op1=ALU.add),
                 reads=[up_r, cst_r, acc_r], writes=[UC_r[c % 8]])
        gates_done = 0
        for c in range(RC):
            lru_chunk(c)
            while gates_done < RC and gate_klist(gates_done)[-1] <= c:
                lru_gates(gates_done)
                gates_done += 1

        assert gates_done == RC

        if stop_after == "L0":
            return finish(nc, P, W, specs, out_d, None)
        ccin_r, ccout_r = Res("ccin"), Res("ccout")
        P.dma("pool", lambda e: e.dma_start(out=cc_in, in_=ccs), P.dsem(), reads=[sml_r], writes=[ccin_r])
        ds_cc = P.dsem()
        waits = P._collect("pool", [ccin_r], [ccout_r])
        ds_cc.cnt += 1
        ev = (ds_cc.sem, ds_cc.cnt, "dma")
        import os as _os
        if _os.environ.get("KDBG_NOCC"):
            P.ops["pool"].append((waits, lambda e: e.dma_start(out=cc_out[0:128, :], in_=cc_in), (ds_cc.sem, 1)))
            P.ops["pool"].append(([(ds_cc.sem, 1)], None, None))
            ds_cc.cnt = 16
            ev = (ds_cc.sem, 16, "dma")
            P.ops["pool"][-2] = (waits, lambda e: e.dma_start(out=cc_out[0:128, :], in_=cc_in), (ds_cc.sem, 16))
            P.ops["pool"].pop()
        else:
            P.ops["pool"].append((waits, lambda e: e.collective_compute("AllGather", ALU.bypass, replica_groups=[list(range(8))],
                                                                        ins=[cc_in], outs=[cc_out]), (ds_cc.sem, 1)))
        ccout_r.w = ev
        gth_t = nc.alloc_sbuf_tensor("gth", [128, 8 * 64], F32)
        gth = gth_t[:, :].rearrange("p (r c) -> p r c", r=8)
        gth_r = Res("gth")
        P.dma("sp", lambda e: e.dma_start(out=gth, in_=cc_out.rearrange("(r p) c -> p r c", p=128)), P.dsem(), reads=[ccout_r], writes=[gth_r])
        P.barrier()
        hs = sml[:, HS:HS + RC]
        P.op("dve", lambda e: e.memset(hs, 0.0), writes=[sml_r])
        for k in range(8):
            selk = cc("sel", k)
            P.op("dve", lambda e, k=k, selk=selk: e.tensor_scalar(out=sml[:, CHA:CHA + RC], in0=gth[:, k, 0:RC], scalar1=-1.0, scalar2=selk,
                                                                  op0=ALU.add, op1=ALU.mult),
                 reads=[gth_r, cst_r], writes=[sml_r])
            P.op("dve", lambda e: e.tensor_scalar(out=sml[:, CHA:CHA + RC], in0=sml[:, CHA:CHA + RC], scalar1=1.0, scalar2=None, op0=ALU.add),
                 reads=[sml_r], writes=[sml_r])
            P.op("dve", lambda e, k=k, selk=selk: e.tensor_scalar(out=sml[:, CHB:CHB + RC], in0=gth[:, k, RC:2 * RC], scalar1=selk, scalar2=None,
                                                                  op0=ALU.mult),
                 reads=[gth_r, cst_r], writes=[sml_r])
            P.op("dve", lambda e: e.tensor_tensor(out=hs, in0=hs, in1=sml[:, CHA:CHA + RC], op=ALU.mult), reads=[sml_r], writes=[sml_r])
            P.op("dve", lambda e: e.tensor_tensor(out=hs, in0=hs, in1=sml[:, CHB:CHB + RC], op=ALU.add), reads=[sml_r], writes=[sml_r])
        r1b = [view(45056, [128, T]), view(45056 + 2048, [128, T])]
        r1b_r = [Res("r1b0"), Res("r1b1")]
        r1b_s = [P.dsem(), P.dsem()]
        for c in range(RC):
            b = c % 2
            P.dma("sp", lambda e, c=c, b=b: e.dma_start(out=r1b[b], in_=r1_d[c]), r1b_s[b], writes=[r1b_r[b]])
            P.op("dve", lambda e, c=c, b=b: e.scalar_tensor_tensor(out=R0[:, c, :], in0=r1b[b], scalar=sml[:, HS + c:HS + c + 1], in1=R0[:, c, :],
                                                                  op0=ALU.mult, op1=ALU.add),
                 reads=[r1b_r[b], sml_r, R0_r[c]], writes=[R0_r[c]])
        dbg_dump("rec", R0.rearrange("p a b -> p (a b)"), R0_r, RC * T)
        if stop_after == "L":
            return finish(nc, P, W, specs, out_d, None)

        MG = view(98304, [128, KC, T])
        MG_r = [Res("MG%d" % m) for m in range(KC)]
        sg = [view(49152, [128, 512], F32), view(49152 + 2048, [128, 512], F32)]
        sg_r = [Res("sg0"), Res("sg1")]
        tA = [view(53248, [128, 512], F32), view(53248 + 2048, [128, 512], F32)]
        tA_r = [Res("tA0"), Res("tA1")]

        def lp_spec(m):
            return lambda inp: inp["w_lru_proj"][0][:, m * 128:(m + 1) * 128].reshape(RC, 128, 128).transpose(1, 0, 2).reshape(128, RC * 128)

        def proj_gate_stage(K, act_ap_fn, act_res_fn, pspec, gate_col0, bias_name, first, sg, sg_r, tA, tA_r):
            it_ = 0
            for mp in range(KC // 2):
                gslab, gslab_r = wload(4096, spec_k16([("w_in", gate_col0 + (2 * mp) * 128), ("w_in", gate_col0 + (2 * mp + 1) * 128)]))
                gv = gslab[:, :].rearrange("p (g k m) -> p g k m", g=2, k=KC)
                for mi in range(2):
                    m = 2 * mp + mi
                    if K == RC:
                        pslab, pslab_r = wload(RC * 128, pspec(m))
                        pv = pslab[:, 0:RC * 128].rearrange("p (k m) -> p k m", k=RC)
                    else:
                        if mi == 0:
                            pslab, pslab_r = wload(4096, pspec(mp))
                            pv2 = pslab[:, :].rearrange("p (g k m) -> p g k m", g=2, k=KC)
                        pv = pv2[:, mi]
                    for hf in range(2):
                        ba = (it_ % 2) * 2
                        bg = ba + 1
                        ba += 4 * 0
                        for k in range(K):
                            mm(psum[ba][:, :], pv[:, k, :], act_ap_fn(k, hf), k == 0, k == K - 1, [pslab_r, act_res_fn(k)], ps_r[ba])
                        rd = xn_reads(128 + hf * 512, 512)
                        for k in range(KC):
                            mm(psum[bg][:, :], gv[:, mi, k, :], xn[:, k, 128 + hf * 512:128 + (hf + 1) * 512],
                               k == 0, k == KC - 1, [gslab_r] + rd(k), ps_r[bg])
                        b = it_ % 2
                        P.op("act", lambda e, b=b, bg=bg, m=m: e.activation(out=sg[b], in_=psum[bg][:, :], func=AF.Sigmoid, bias=cc(bias_name, m)),
                             reads=[ps_r[bg], cst_r], writes=[sg_r[b]])
                        dst = MG[:, m, hf * 512:(hf + 1) * 512]
                        if first:
                            P.op("dve", lambda e, b=b, ba=ba, dst=dst: e.tensor_tensor(out=dst, in0=psum[ba][:, :], in1=sg[b], op=ALU.mult),
                                 reads=[ps_r[ba], sg_r[b]], writes=[MG_r[m]])
                        else:
                            P.op("dve", lambda e, b=b, ba=ba: e.tensor_tensor(out=tA[b], in0=psum[ba][:, :], in1=sg[b], op=ALU.mult),
                                 reads=[ps_r[ba], sg_r[b]], writes=[tA_r[b]])
                            P.op("dve", lambda e, b=b, dst=dst: e.tensor_tensor(out=dst, in0=tA[b], in1=dst, op=ALU.add),
                                 reads=[tA_r[b], MG_r[m]], writes=[MG_r[m]])
                        it_ += 1

        proj_gate_stage(RC, lambda k, hf: R0[:, k, hf * 512:(hf + 1) * 512], lambda k: R0_r[k], lp_spec, SPL[5], "bgl", True, sg, sg_r, tA, tA_r)
        if stop_after == "ML":
            dbg_dump("mrg", MG.rearrange("p a b -> p (a b)"), MG_r, KC * T)
            return finish(nc, P, W, specs, out_d, None)

        P.barrier()
        AT = view(65536, [128, NQ, T])
        AT_r = [Res("AT%d" % h) for h in range(NQ)]
        QT = view(0, [128, 8, 4, 128])
        QT_r = [Res("QT%d" % qb) for qb in range(8)]
        KT = view(8192, [128, TE])
        KT_r = [Res("KT%d" % t) for t in range(9)]
        VT = view(10496, [128, 9, 512])
        VT_r = [Res("VT%d" % t) for t in range(9)]
        Ctab = view(19712, [128, TE], F32)
        Stab = view(24320, [128, TE], F32)
        tab_r = Res("tab")
        sqb = [view(28928, [128, 512]), view(28928 + 1024, [128, 512])]
        sqb_r = [Res("sqb0"), Res("sqb1")]
        rsd = [view(30976, [128, 512], F32), view(30976 + 2048, [128, 512], F32)]
        rsd_r = [Res("rsd0"), Res("rsd1")]
        qnf = [view(35072, [128, 512], F32), view(35072 + 2048, [128, 512], F32)]
        qnf_r = [Res("qnf0"), Res("qnf1")]
        qnb = [view(39168, [128, 512]), view(39168 + 1024, [128, 512])]
        qnb_r = [Res("qnb0"), Res("qnb1")]
        rt1 = [view(41216, [128, 512], F32), view(41216 + 2048, [128, 512], F32)]
        rt1_r = [Res("rt1_0"), Res("rt1_1")]
        rt2 = [view(45312, [128, 512], F32), view(45312 + 2048, [128, 512], F32)]
        rt2_r = [Res("rt2_0"), Res("rt2_1")]
        EB = [view(49408 + 1024 * i, [128, 512]) for i in range(4)]
        EB_r = [Res("EB%d" % i) for i in range(4)]
        dns = [view(53504, [128, 512], F32), view(53504 + 2048, [128, 512], F32)]
        dns_r = [Res("dns0"), Res("dns1")]
        tpi = view(65536 + 16384, [128, TE], I32)
        tA_ = view(65536 + 16384 + 4608, [128, TE], F32)
        tB_ = view(65536 + 16384 + 9216, [128, TE], F32)
        tmp_r = Res("ropetmp")
        pi_ = tpi[0:32, :]
        A_ = tA_[0:32, :]
        B_ = tB_[0:32, :]
        Bi_ = tB_[0:32, :].bitcast(I32)
        pf_ = tpi[0:32, :].bitcast(F32)
        P.dma("sp", lambda e: e.dma_start(out=pi_, in_=pos_d), P.dsem(), writes=[tmp_r])
        P.op("dve", lambda e: e.tensor_copy(out=A_, in_=pi_), reads=[tmp_r], writes=[tmp_r])
        P.op("dve", lambda e: e.tensor_scalar(out=A_, in0=A_, scalar1=cst[0:32, CC["invf"]:CC["invf"] + 1], scalar2=1.0 / (2 * math.pi),
                                              op0=ALU.mult, op1=ALU.mult), reads=[tmp_r, cst_r], writes=[tmp_r])
        for which, tab in ((0, Stab), (1, Ctab)):
            if which == 1:
                P.op("dve", lambda e: e.tensor_scalar(out=A_, in0=A_, scalar1=0.25, scalar2=None, op0=ALU.add), reads=[tmp_r], writes=[tmp_r])
            P.op("dve", lambda e: e.tensor_copy(out=Bi_, in_=A_), reads=[tmp_r], writes=[tmp_r])
            P.op("dve", lambda e: e.tensor_copy(out=pf_, in_=Bi_), reads=[tmp_r], writes=[tmp_r])
            P.op("dve", lambda e: e.tensor_tensor(out=pf_, in0=A_, in1=pf_, op=ALU.subtract), reads=[tmp_r], writes=[tmp_r])
            P.op("dve", lambda e: e.tensor_scalar(out=B_, in0=pf_, scalar1=0.5, scalar2=None, op0=ALU.is_gt), reads=[tmp_r], writes=[tmp_r])
            P.op("dve", lambda e: e.tensor_tensor(out=pf_, in0=pf_, in1=B_, op=ALU.subtract), reads=[tmp_r], writes=[tmp_r])
            P.op("dve", lambda e: e.tensor_scalar(out=B_, in0=pf_, scalar1=-0.5, scalar2=None, op0=ALU.is_lt), reads=[tmp_r], writes=[tmp_r])
            P.op("dve", lambda e: e.tensor_tensor(out=pf_, in0=pf_, in1=B_, op=ALU.add), reads=[tmp_r], writes=[tmp_r])
            P.op("act", lambda e, tab=tab: e.activation(out=tab[0:32, :], in_=pf_, func=AF.Sin, scale=2 * math.pi), reads=[tmp_r], writes=[tab_r])

        vs = []
        for half in range(2):
            def vspec(inp, half=half):
                wv = inp["w_in"][0][half * 1024:(half + 1) * 1024, SPL[1]:SPL[2]]
                return wv.reshape(8, 128, 512).transpose(1, 0, 2).reshape(128, 4096)
            vs.append(wload(4096, vspec))
        for t in range(9):
            bank = t % 2
            for k in range(KC):
                slab, slab_r = vs[k // 8]
                mm(psum[bank][:, :], xn[:, k, t * 128:(t + 1) * 128], slab[:, (k % 8) * 512:(k % 8 + 1) * 512],
                   k == 0, k == KC - 1, [slab_r, xn_r[k][t]], ps_r[bank])
            P.op("act", lambda e, t=t, bank=bank: e.activation(out=VT[:, t, :], in_=psum[bank][:, :], func=AF.Copy),
                 reads=[ps_r[bank]], writes=[VT_r[t]])

        cnt = {"i": 0}

        def qk_norm_rope(pbank, gname, col0, ncols, dst_ap, dst_res):
            i = cnt["i"] % 2
            cnt["i"] += 1
            pq = psum[pbank][:, 0:ncols]
            P.op("act", lambda e: e.activation(out=sqb[i][:, 0:ncols], in_=pq, func=AF.Square), reads=[ps_r[pbank]], writes=[sqb_r[i]])
            mm(psum[6][:, 0:ncols], ones_b, sqb[i][:, 0:ncols], True, True, [cmat_r, sqb_r[i]], ps_r[6])
            P.op("act", lambda e: e.activation(out=rsd[i][:, 0:ncols], in_=psum[6][:, 0:ncols], func=AF.Sqrt, scale=1.0 / HD, bias=EPS),
                 reads=[ps_r[6]], writes=[rsd_r[i]])
            P.op("dve", lambda e: e.reciprocal(out=rsd[i][:, 0:ncols], in_=rsd[i][:, 0:ncols]), reads=[rsd_r[i]], writes=[rsd_r[i]])
            P.op("dve", lambda e: e.scalar_tensor_tensor(out=qnf[i][:, 0:ncols], in0=pq, scalar=cc(gname), in1=rsd[i][:, 0:ncols],
                                                         op0=ALU.mult, op1=ALU.mult),
                 reads=[ps_r[pbank], cst_r, rsd_r[i]], writes=[qnf_r[i]])
            P.op("act", lambda e: e.activation(out=qnb[i][:, 0:ncols], in_=qnf[i][:, 0:ncols], func=AF.Copy), reads=[qnf_r[i]], writes=[qnb_r[i]])
            mm(psum[7][0:32, 0:ncols], rotm, qnb[i][0:32, 0:ncols], True, True, [cmat_r, qnb_r[i]], ps_r[7])
            P.op("dve", lambda e: e.tensor_tensor(out=rt1[i][0:32, 0:ncols], in0=psum[7][0:32, 0:ncols], in1=Stab[0:32, col0:col0 + ncols], op=ALU.mult),
                 reads=[ps_r[7], tab_r], writes=[rt1_r[i]])
            P.op("dve", lambda e: e.tensor_tensor(out=rt2[i][0:32, 0:ncols], in0=qnf[i][0:32, 0:ncols], in1=Ctab[0:32, col0:col0 + ncols], op=ALU.mult),
                 reads=[qnf_r[i], tab_r], writes=[rt2_r[i]])
            P.op("dve", lambda e: e.tensor_tensor(out=qnb[i][0:32, 0:ncols], in0=rt1[i][0:32, 0:ncols], in1=rt2[i][0:32, 0:ncols], op=ALU.add),
                 reads=[rt1_r[i], rt2_r[i], qnb_r[i]], writes=[qnb_r[i]])
            srcv = qnb[i][:, 0:ncols]
            if len(dst_ap.shape) == 3:
                srcv = srcv.rearrange("p (a b) -> p a b", a=dst_ap.shape[1])
            P.op("act", lambda e: e.activation(out=dst_ap, in_=srcv, func=AF.Copy), reads=[qnb_r[i]], writes=dst_res)

        scale = 1.0 / math.sqrt(HD)
        for g in range(NKV):
            slab, slab_r = wload(4096, spec_k16([("w_in", SPL[0] + g * 128), ("w_in", g * 4 * 128)]))
            sv = slab[:, :].rearrange("p (g k m) -> p g k m", g=2, k=KC)
            for (c0, n) in ((0, 128), (128, 512), (640, 512)):
                rd = xn_reads(c0, n)
                for k in range(KC):
                    mm(psum[4][:, 0:n], sv[:, 0, k, :], xn[:, k, c0:c0 + n], k == 0, k == KC - 1, [slab_r] + rd(k), ps_r[4])
                qk_norm_rope(4, "kg", c0, n, KT[:, c0:c0 + n], [KT_r[t] for t in range(c0 // 128, (c0 + n) // 128)])
            for hh in range(4):
                h = g * 4 + hh
                if hh == 0:
                    qv = sv[:, 1]
                    q_r = slab_r
                elif hh in (1, 3):
                    pass
                if hh == 1:
                    slab2, slab2_r = wload(4096, spec_k16([("w_in", (h) * 128), ("w_in", (h + 1) * 128)]))
                    sv2 = slab2[:, :].rearrange("p (g k m) -> p g k m", g=2, k=KC)
                    qv, q_r = sv2[:, 0], slab2_r
                elif hh == 2:
                    qv, q_r = sv2[:, 1], slab2_r
                elif hh == 3:
                    slab3, slab3_r = wload(2048, spec_k16([("w_in", h * 128)]))
                    qv = slab3[:, 0:2048].rearrange("p (k m) -> p k m", k=KC)
                    q_r = slab3_r
                for hf in range(2):
                    rd = xn_reads(128 + hf * 512, 512)
                    for k in range(KC):
                        mm(psum[5][:, :], qv[:, k, :], xn[:, k, 128 + hf * 512:128 + (hf + 1) * 512], k == 0, k == KC - 1, [q_r] + rd(k), ps_r[5])
                    qk_norm_rope(5, "qg", 128 + hf * 512, 512,
                                 QT[:, hf * 4:(hf + 1) * 4, hh, :], [QT_r[qb] for qb in range(hf * 4, hf * 4 + 4)])
            for qb in range(8):
                eb = (qb % 2) * 2
                for which in range(2):
                    kt = qb + which
                    bank = which
                    mm(psum[bank][:, :], KT[:, kt * 128:(kt + 1) * 128], QT[:, qb].rearrange("p a b -> p (a b)"), True, True,
                       [KT_r[kt], QT_r[qb]], ps_r[bank])
                    P.op("act", lambda e, bank=bank, which=which, eb=eb: e.activation(out=EB[eb + which], in_=psum[bank][:, :], func=AF.Exp, scale=scale),
                         reads=[ps_r[bank]], writes=[EB_r[eb + which]])
                    msk = mask_cur if which == 1 else (mask_prev0 if qb == 0 else mask_prev)
                    P.op("dve", lambda e, which=which, eb=eb, msk=msk: e.tensor_tensor(out=EB[eb + which], in0=EB[eb + which], in1=msk, op=ALU.mult),
                         reads=[EB_r[eb + which], cmat_r], writes=[EB_r[eb + which]])
                for which in range(2):
                    kt = qb + which
                    mm(psum[2][:, :], VT[:, kt, g * 128:(g + 1) * 128], EB[eb + which], which == 0, which == 1, [VT_r[kt], EB_r[eb + which]], ps_r[2])
                for which in range(2):
                    mm(psum[3][:, :], ones_b, EB[eb + which], which == 0, which == 1, [cmat_r, EB_r[eb + which]], ps_r[3])
                d = qb % 2
                for hh in range(4):
                    h = g * 4 + hh
                    P.op("dve", lambda e, d=d, hh=hh, h=h: e.tensor_scalar(out=dns[d][:, hh * 128:(hh + 1) * 128], in0=psum[3][:, hh * 128:(hh + 1) * 128],
                                                                          scalar1=sml[:, ESINK + h:ESINK + h + 1], scalar2=None, op0=ALU.add),
                         reads=[ps_r[3], sml_r], writes=[dns_r[d]])
                P.op("dve", lambda e, d=d: e.reciprocal(out=dns[d], in_=dns[d]), reads=[dns_r[d]], writes=[dns_r[d]])
                extra = [tmp_r] if g >= 2 else []
                P.op("dve", lambda e, d=d, qb=qb, g=g: e.tensor_tensor(out=AT[:, g * 4:(g + 1) * 4, qb * 128:(qb + 1) * 128],
                                                                       in0=psum[2][:, :].rearrange("p (a b) -> p a b", a=4),
                                                                       in1=dns[d].rearrange("p (a b) -> p a b", a=4), op=ALU.mult),
                     reads=[ps_r[2], dns_r[d]], writes=[AT_r[g * 4 + hh] for hh in range(4)] + extra)
        dbg_dump("att", AT.rearrange("p a b -> p (a b)"), AT_r, KC * T)
        if stop_after == "AT":
            return finish(nc, P, W, specs, out_d, None)

        P.barrier()
        sg = [view(0, [128, 512], F32), view(2048, [128, 512], F32)]
        sg_r = [Res("sg0b"), Res("sg1b")]
        tA = [view(4096, [128, 512], F32), view(6144, [128, 512], F32)]
        tA_r = [Res("tA0b"), Res("tA1b")]

        def ap_spec(mp):
            return spec_k16([("w_attn_proj", (2 * mp) * 128), ("w_attn_proj", (2 * mp + 1) * 128)])
        proj_gate_stage(KC, lambda k, hf: AT[:, k, hf * 512:(hf + 1) * 512], lambda k: AT_r[k], ap_spec, SPL[4], "bga", False, sg, sg_r, tA, tA_r)
        dbg_dump("mrg", MG.rearrange("p a b -> p (a b)"), MG_r, KC * T)
        if stop_after == "MA":
            return finish(nc, P, W, specs, out_d, None)

        P.barrier()
        H = view(0, [128, 8, D], F32)
        H_r = [[Res("H%d_%d" % (t, n)) for n in range(4)] for t in range(8)]
        h_s = P.dsem()
        for t in range(8):
            P.dma("sp", lambda e, t=t: e.dma_start(out=H[:, t, :], in_=x_own[t * 128:(t + 1) * 128, :]), P.dsem(), writes=H_r[t])
        it_ = 0
        for n in range(4):
            ws = []
            for half in range(2):
                def wospec(inp, half=half, n=n):
                    w = inp["w_out"][0][half * 1024:(half + 1) * 1024, n * 512:(n + 1) * 512]
                    return w.reshape(8, 128, 512).transpose(1, 0, 2).reshape(128, 4096)
                ws.append(wload(4096, wospec))
            for t in range(8):
                bank = it_ % 4
                it_ += 1
                for k in range(KC):
                    slab, slab_r = ws[k // 8]
                    mm(psum[bank][:, :], MG[:, k, t * 128:(t + 1) * 128], slab[:, (k % 8) * 512:(k % 8 + 1) * 512],
                       k == 0, k == KC - 1, [slab_r, MG_r[k]], ps_r[bank])
                P.op("dve", lambda e, t=t, n=n, bank=bank: e.tensor_tensor(out=H[:, t, n * 512:(n + 1) * 512], in0=psum[bank][:, :],
                                                                            in1=H[:, t, n * 512:(n + 1) * 512], op=ALU.add),
                     reads=[ps_r[bank], H_r[t][n]], writes=[H_r[t][n]])
        dbg_dump("hh", H.rearrange("p a b -> p (a b)"), [r for t in range(8) for r in H_r[t]], 8 * D)
        if stop_after == "C":
            return finish(nc, P, W, specs, out_d, (H, H_r))
        P.barrier()
        hn = xn_t[:, 0:KC * T].rearrange("p (k t) -> p k t", k=KC)
        hn_r = [[Res("hn%d_%d" % (k, t)) for t in range(8)] for k in range(KC)]

        class _AllH:
            pass

        def c_src(i):
            r = Res("Hall%d" % i)
            r.w = H_r[i][3].w
            return H[:, i, :], r
        norm_transpose(c_src, 8, 1, hn, hn_r, 0, 65536)

        P.barrier()
        ACTB = [view(65536, [128, 4, T]), view(65536 + 8192, [128, 4, T])]
        ACTB_r = [[Res("act%d_%d" % (b, cc_)) for cc_ in range(4)] for b in range(2)]
        slb = [view(81920, [128, 512], F32), view(81920 + 2048, [128, 512], F32)]
        slb_r = [Res("sl0"), Res("sl1")]

        def hn_reads(hf):
            return lambda kc: [hn_r[kc][t] for t in range(hf * 4, hf * 4 + 4)]
        it2 = 0
        for gi in range(FC // 4):
            ab = gi % 2
            for pr in range(2):
                c0 = 4 * gi + 2 * pr
                gs, gs_r = wload(4096, spec_k16([("w_ffn_gate", c0 * 128), ("w_ffn_gate", (c0 + 1) * 128)]))
                us, us_r = wload(4096, spec_k16([("w_ffn_up", c0 * 128), ("w_ffn_up", (c0 + 1) * 128)]))
                gv = gs[:, :].rearrange("p (g k m) -> p g k m", g=2, k=KC)
                uv = us[:, :].rearrange("p (g k m) -> p g k m", g=2, k=KC)
                for ci in range(2):
                    cj = 2 * pr + ci
                    for hf in range(2):
                        bg_, bu_ = hf, 2 + hf
                        rd = hn_reads(hf)
                        for k in range(KC):
                            mm(psum[bg_][:, :], gv[:, ci, k, :], hn[:, k, hf * 512:(hf + 1) * 512], k == 0, k == KC - 1, [gs_r] + rd(k), ps_r[bg_])
                        for k in range(KC):
                            mm(psum[bu_][:, :], uv[:, ci, k, :], hn[:, k, hf * 512:(hf + 1) * 512], k == 0, k == KC - 1, [us_r] + rd(k), ps_r[bu_])
                        P.op("act", lambda e, hf=hf, bg_=bg_: e.activation(out=slb[hf], in_=psum[bg_][:, :], func=AF.Silu), reads=[ps_r[bg_]], writes=[slb_r[hf]])
                        P.op("dve", lambda e, hf=hf, bu_=bu_, ab=ab, cj=cj: e.tensor_tensor(out=ACTB[ab][:, cj, hf * 512:(hf + 1) * 512], in0=psum[bu_][:, :],
                                                                                           in1=slb[hf], op=ALU.mult),
                             reads=[ps_r[bu_], slb_r[hf]], writes=[ACTB_r[ab][cj]])
            dsl = []
            for pr in range(2):
                def dspec(inp, r0=(4 * gi + 2 * pr) * 128):
                    w = inp["w_ffn_down"][0][r0:r0 + 256, :]
                    return w.reshape(2, 128, D).transpose(1, 0, 2).reshape(128, 4096)
                ds_, ds_r = wload(4096, dspec)
                dsl.append((ds_[:, :].rearrange("p (c n) -> p c n", c=2), ds_r))
            for t in range(8):
                for n in range(4):
                    bank = 4 + it2 % 4
                    it2 += 1
                    for cj in range(4):
                        dv, ds_r = dsl[cj // 2]
                        mm(psum[bank][:, :], ACTB[ab][:, cj, t * 128:(t + 1) * 128], dv[:, cj % 2, n * 512:(n + 1) * 512], cj == 0, cj == 3,
                           [ACTB_r[ab][cj], ds_r], ps_r[bank])
                    P.op("dve", lambda e, t=t, n=n, bank=bank: e.tensor_tensor(out=H[:, t, n * 512:(n + 1) * 512], in0=psum[bank][:, :],
                                                                                in1=H[:, t, n * 512:(n + 1) * 512], op=ALU.add),
                         reads=[ps_r[bank], H_r[t][n]], writes=[H_r[t][n]])
        return finish(nc, P, W, specs, out_d, (H, H_r))


def finish(nc, P, W, specs, out_d, Hinfo):
    evs = []
    if Hinfo is not None:
        H, H_r = Hinfo
        for t in range(8):
            evs.append(P.dma("sp", lambda e, t=t: e.dma_start(out=out_d[t * 128:(t + 1) * 128, :], in_=H[:, t, :]), P.dsem(), reads=H_r[t]))
    P.barrier()
    P.wait("sp", evs)
    W["ap"] = nc.dram_tensor("wstream", [max(W["off"], 128)], F32, kind="ExternalInput").ap()
    P.emit()
    return nc, specs


def pack_weights(specs, inp):
    parts = []
    for n_el, fn in specs:
        a = np.ascontiguousarray(fn(inp), dtype=np.float32)
        assert a.shape == (128, n_el), (a.shape, n_el)
        parts.append(a.reshape(-1))
    if not parts:
        return np.zeros(128, np.float32)
    return np.concatenate(parts)


def make_in_maps(inp, specs):
    inp = {k: np.asarray(v) for k, v in inp.items()}
    wst = pack_weights(specs, inp)
    g12 = np.stack([np.broadcast_to(inp["norm1_g"][0][None, :], (128, D)),
                    np.broadcast_to(inp["norm2_g"][0][None, :], (128, D))]).astype(np.float32)
    maps = []
    for core in range(8):
        b, j = core // 4, core % 4
        t0 = j * T
        x_own = np.ascontiguousarray(inp["x"][b, t0:t0 + T])
        if j > 0:
            x_halo = np.ascontiguousarray(inp["x"][b, t0 - 128:t0])
            pos = inp["positions"][b, t0 - 128:t0 + T]
        else:
            x_halo = np.zeros((128, D), np.float32)
            pos = np.concatenate([np.zeros(128, np.int32), inp["positions"][b, 0:T]])
        maps.append({
            "x_own": x_own, "x_halo": x_halo,
            "pos": np.ascontiguousarray(np.broadcast_to(pos[None, :].astype(np.int32), (32, TE))),
            "consts": host_consts(inp, core), "cmat": host_cmat(core), "g12": g12, "wstream": wst,
        })
    return maps


_CACHE = {}


def kernel(**inputs):
    if "nc" not in _CACHE:
        _CACHE["nc"] = build_nc()
    nc, specs = _CACHE["nc"]
    maps = make_in_maps(inputs, specs)
    res = run_bass_kernel_spmd(nc, maps, core_ids=list(range(8)))
    out = np.zeros((2, S, D), np.float32)
    for core in range(8):
        b, j = core // 4, core % 4
        out[b, j * T:(j + 1) * T] = res.results[core]["out"]
    return out
```

```python
import contextlib
import math
import numpy as np
import concourse.bass as bass
import concourse.mybir as mybir
from concourse.bass_utils import run_bass_kernel_spmd

F32 = mybir.dt.float32
BF16 = mybir.dt.bfloat16
I32 = mybir.dt.int32
AF = mybir.ActivationFunctionType
ALU = mybir.AluOpType

D = 2048
S = 4096
T = 1024
TE = 1152
HD = 128
NQ = 16
NKV = 4
DR = 2816
NB_RNN = 16
BW = 176
DFF = 5632
KC = 16
RC = 22
FC = 44
EPS = 1e-6
THETA = 500000.0
Q_W = 2048
KV_W = 512
SPL = np.cumsum([Q_W, KV_W, KV_W, DR, DR, D]).tolist()

ENGS = ("pe", "act", "dve", "pool", "sp")
SAME_ENGINE_SYNC = True
NRING = 4
SLAB = 4096


class Res:
    __slots__ = ("name", "w", "r")

    def __init__(self, name=""):
        self.name = name
        self.w = None
        self.r = {}


class DSem:
    __slots__ = ("sem", "cnt")

    def __init__(self, sem):
        self.sem = sem
        self.cnt = 0


class Prog:
    def __init__(self, nc, stack):
        self.nc = nc
        self.stack = stack
        self.sem = {e: stack.enter_context(nc.semaphore("s_" + e)) for e in ENGS}
        self.cnt = {e: 0 for e in ENGS}
        self.ops = {e: [] for e in ENGS}
        self.waited = {e: {} for e in ENGS}
        self.pending = {e: [] for e in ENGS}
        self.ndsem = 0
        self.dma_evs = []

    def dsem(self):
        self.ndsem += 1
        return DSem(self.stack.enter_context(self.nc.semaphore("d%d" % self.ndsem)))

    def _collect(self, e, reads, writes):
        evs = []
        for r in reads:
            if r.w is not None:
                evs.append(r.w)
        for w in writes:
            for pe_ in ENGS:
                if pe_ != e and any(w is p for p in self.pending[pe_]):
                    raise RuntimeError("write to %s with pending unsignaled reads on %s" % (w.name, pe_))
            if w.w is not None:
                evs.append(w.w)
            for k, ev in w.r.items():
                if k == e and e != "pe":
                    continue
                evs.append(ev)
        waits = []
        wd = self.waited[e]
        for (sem, val, src) in evs:
            if src == e and (e == "pe" or not SAME_ENGINE_SYNC):
                continue
            if wd.get(sem, 0) >= val:
                continue
            wd[sem] = val
            waits.append((sem, val))
        return waits

    def op(self, e, fn, reads=(), writes=(), signal=True):
        waits = self._collect(e, reads, writes)
        ev = None
        if signal:
            self.cnt[e] += 1
            ev = (self.sem[e], self.cnt[e], e)
            for r in self.pending[e]:
                r.r[e] = ev
            self.pending[e] = []
            for r in reads:
                r.r[e] = ev
            for w in writes:
                w.w = ev
                w.r = {}
            self.ops[e].append((waits, fn, (self.sem[e], 1)))
        else:
            for r in reads:
                self.pending[e].append(r)
            self.ops[e].append((waits, fn, None))
        return ev

    def dma(self, q, fn, ds, reads=(), writes=(), inc=16):
        waits = self._collect(q, reads, writes)
        ds.cnt += inc
        ev = (ds.sem, ds.cnt, "dma")
        self.ops[q].append((waits, fn, (ds.sem, inc)))
        for r in reads:
            r.r[("dma", id(ds))] = ev
        for w in writes:
            w.w = ev
            w.r = {}
        self.dma_evs.append(ev)
        return ev

    def wait(self, e, evs):
        waits = []
        wd = self.waited[e]
        for (sem, val, src) in evs:
            if wd.get(sem, 0) >= val:
                continue
            wd[sem] = val
            waits.append((sem, val))
        if waits:
            self.ops[e].append((waits, None, None))

    def barrier(self, group=("pe", "act", "dve", "sp")):
        for e in group:
            assert not self.pending[e], "barrier with pending unsignaled ops on " + e
        evs = [(self.sem[o], self.cnt[o], o) for o in group if self.cnt[o] > 0]
        last = {}
        for ev in self.dma_evs:
            last[ev[0]] = ev
        self.dma_evs = list(last.values())
        for e in group:
            self.wait(e, [ev for ev in evs if ev[2] != e] + self.dma_evs)


    def check(self):
        val = {}
        ptr = {e: 0 for e in ENGS}
        total = sum(len(v) for v in self.ops.values())
        done = 0
        while done < total:
            prog = False
            for e in ENGS:
                lst = self.ops[e]
                while ptr[e] < len(lst):
                    waits, fn, inc = lst[ptr[e]]
                    if any(val.get(s, 0) < v for s, v in waits):
                        break
                    if inc is not None:
                        val[inc[0]] = val.get(inc[0], 0) + inc[1]
                    ptr[e] += 1
                    done += 1
                    prog = True
            if not prog:
                msg = []
                for e in ENGS:
                    if ptr[e] < len(self.ops[e]):
                        waits, fn, inc = self.ops[e][ptr[e]]
                        msg.append("%s@%d waits %s" % (e, ptr[e], [(s.name, v, val.get(s, 0)) for s, v in waits if val.get(s, 0) < v]))
                raise RuntimeError("DEADLOCK: " + "; ".join(msg))
        return True

    def emit(self):
        self.check()
        nc = self.nc
        ops = self.ops

        def run(eng, lst):
            for waits, fn, inc in lst:
                for sem, val in waits:
                    eng.wait_ge(sem, val)
                if fn is None:
                    continue
                ins = fn(eng)
                if inc is not None:
                    ins.then_inc(inc[0], inc[1])

        with nc.Block() as block:
            @block.sync
            def _(eng):
                run(eng, ops["sp"])

            @block.scalar
            def _(eng):
                run(eng, ops["act"])

            @block.vector
            def _(eng):
                run(eng, ops["dve"])

            @block.gpsimd
            def _(eng):
                run(eng, ops["pool"])

            @block.tensor
            def _(eng):
                run(eng, ops["pe"])


def gate_klist(c):
    n_lo = (128 * c) // BW
    n_hi = (128 * c + 127) // BW
    k_lo = (BW * n_lo) // 128
    k_hi = (BW * (n_hi + 1) - 1) // 128
    return list(range(k_lo, k_hi + 1))


CC = {}
_o = 0
for _name, _n in [("qg", 1), ("kg", 1), ("bga", 16), ("bgl", 16), ("cw", 88), ("cb", 22), ("br", 22),
                  ("bi", 22), ("lam", 22), ("invf", 1), ("sinks", 16), ("sel", 8)]:
    CC[_name] = _o
    _o += _n
NCONST = _o


def host_consts(inp, core):
    c = np.zeros((128, NCONST), np.float32)
    j = core % 4
    b = core // 4
    c[:, CC["qg"]] = inp["q_norm_g"][0]
    c[:, CC["kg"]] = inp["k_norm_g"][0]
    c[:, CC["bga"]:CC["bga"] + 16] = inp["b_gates"][0][:D].reshape(16, 128).T
    c[:, CC["bgl"]:CC["bgl"] + 16] = inp["b_gates"][0][D:].reshape(16, 128).T
    for tap in range(4):
        c[:, CC["cw"] + tap * 22:CC["cw"] + (tap + 1) * 22] = inp["conv_w"][0][tap].reshape(22, 128).T
    c[:, CC["cb"]:CC["cb"] + 22] = inp["conv_b"][0].reshape(22, 128).T
    c[:, CC["br"]:CC["br"] + 22] = inp["b_rgate"][0].reshape(22, 128).T
    c[:, CC["bi"]:CC["bi"] + 22] = inp["b_igate"][0].reshape(22, 128).T
    c[:, CC["lam"]:CC["lam"] + 22] = inp["lru_lambda"][0].reshape(22, 128).T
    invf = (np.float32(THETA) ** (-np.arange(0, 32, 2, dtype=np.float32) / np.float32(32))).astype(np.float32)
    c[0:16, CC["invf"]] = invf
    c[16:32, CC["invf"]] = invf
    c[:, CC["sinks"]:CC["sinks"] + 16] = inp["sinks"][0][None, :]
    for k in range(8):
        c[:, CC["sel"] + k] = 1.0 if (k // 4 == b and k < core) else 0.0
    return c


def host_cmat(core):
    j = core % 4
    m = np.zeros((128, 2 * 128 + 3 * 512 + 32), np.float32)
    m[:, 0:128] = np.eye(128, dtype=np.float32)
    m[:, 128:256] = 1.0
    p = np.arange(128)[:, None]
    f = np.arange(128)[None, :]
    cur = (p <= f).astype(np.float32)
    prev = (p > f).astype(np.float32)
    prev0 = prev if j > 0 else np.zeros_like(prev)
    m[:, 256:768] = np.tile(cur, (1, 4))
    m[:, 768:1280] = np.tile(prev, (1, 4))
    m[:, 1280:1792] = np.tile(prev0, (1, 4))
    rot = np.zeros((128, 32), np.float32)
    for mm_ in range(16):
        rot[mm_ + 16, mm_] = -1.0
        rot[mm_, mm_ + 16] = 1.0
    m[:, 1792:1824] = rot
    return m


NCMAT = 1824


STAGES = ("A", "L", "ML", "AT", "MA", "C", "D")


def build_nc(debug=(), stop_after=None):
    nc = bass.Bass("TRN2", target_bir_lowering=False)
    specs = []
    W = {"off": 0, "i": 0, "ap": None}

    x_own = nc.dram_tensor("x_own", [T, D], F32, kind="ExternalInput").ap()
    x_halo = nc.dram_tensor("x_halo", [128, D], F32, kind="ExternalInput").ap()
    pos_d = nc.dram_tensor("pos", [32, TE], I32, kind="ExternalInput").ap()
    consts_d = nc.dram_tensor("consts", [128, NCONST], F32, kind="ExternalInput").ap()
    cmat_d = nc.dram_tensor("cmat", [128, NCMAT], F32, kind="ExternalInput").ap()
    g12_d = nc.dram_tensor("g12", [2, 128, D], F32, kind="ExternalInput").ap()
    out_d = nc.dram_tensor("out", [T, D], F32, kind="ExternalOutput").ap()
    r1_d = nc.dram_tensor("r1_spill", [RC, 128, T], BF16).ap()
    cc_in = nc.dram_tensor("cc_in", [128, 64], F32).ap()
    cc_out = nc.dram_tensor("cc_out", [8 * 128, 64], F32).ap()
    dbg_d = {}
    for name, shape in (("xn", [128, KC * TE]), ("rec", [128, RC * T]), ("mrg", [128, KC * T]),
                        ("att", [128, KC * T]), ("hh", [128, 8 * D]), ("misc", [128, 4096])):
        if name in debug:
            dbg_d[name] = nc.dram_tensor("dbg_" + name, shape, F32, kind="ExternalOutput").ap()

    with contextlib.ExitStack() as st:
        P = Prog(nc, st)
        big = nc.alloc_sbuf_tensor("big", [128, 65536], BF16)
        xn_t = nc.alloc_sbuf_tensor("xn", [128, KC * TE], BF16)
        ring = [nc.alloc_sbuf_tensor("ring%d" % i, [128, SLAB], BF16) for i in range(NRING)]
        ring_r = [Res("ring%d" % i) for i in range(NRING)]
        ring_s = [P.dsem() for _ in range(NRING)]
        cst = nc.alloc_sbuf_tensor("cst", [128, NCONST], F32)
        cmat = nc.alloc_sbuf_tensor("cmatb", [128, NCMAT], BF16)
        sml = nc.alloc_sbuf_tensor("sml", [128, 320], F32)
        psum = [nc.alloc_psum_tensor("ps%d" % i, [128, 512], F32) for i in range(8)]
        ps_r = [Res("ps%d" % i) for i in range(8)]
        cst_r = Res("cst")
        cmat_r = Res("cmat")
        ds_c = P.dsem()

        def view(off, shape, dt=BF16):
            esz = 2 if dt == BF16 else 4
            n = int(np.prod(shape[1:]))
            a = big[:, off // 2: off // 2 + n * esz // 2]
            if dt != BF16:
                a = a.bitcast(dt)
            if len(shape) == 3:
                a = a.rearrange("p (a b) -> p a b", a=shape[1])
            elif len(shape) == 4:
                a = a.rearrange("p (a b c) -> p a b c", a=shape[1], b=shape[2])
            return a

        xn = xn_t[:, :].rearrange("p (k t) -> p k t", k=KC)
        xn_r = [[Res("xn%d_%d" % (k, t)) for t in range(9)] for k in range(KC)]
        ident = cmat[:, 0:128]
        ones_b = cmat[:, 128:256]
        mask_cur = cmat[:, 256:768]
        mask_prev = cmat[:, 768:1280]
        mask_prev0 = cmat[:, 1280:1792]
        rotm = cmat[0:32, 1792:1824]

        def cc(name, i=0):
            return cst[:, CC[name] + i: CC[name] + i + 1]

        SS, RSTD, NSP8, NSP16, ESINK, CCS, HS, TMPA, TMPB, CHA, CHB = 0, 16, 32, 54, 76, 256, 136, 158, 180, 202, 224
        sml_r = Res("sml")

        def wload(n_el, spec):
            b = W["i"] % NRING
            W["i"] += 1
            off = W["off"]
            W["off"] += 128 * n_el
            specs.append((n_el, spec))

            def fn(e, b=b, off=off, n_el=n_el):
                src = W["ap"][off: off + 128 * n_el].rearrange("(p n) -> p n", p=128)
                return e.dma_start(out=ring[b][:, 0:n_el], in_=src, max_dma_last_dim=8192)
            P.dma("pool", fn, ring_s[b], writes=[ring_r[b]])
            return ring[b], ring_r[b]

        def mm(out, lhsT, rhs, start, stop, reads, wres, signal=None):
            if signal is None:
                signal = stop
            P.op("pe", lambda e: e.matmul(out, lhsT, rhs, start=start, stop=stop),
                 reads=reads, writes=[wres], signal=signal)

        def dbg_dump(name, src_ap, res_list, width):
            if name not in dbg_d:
                return
            P.barrier()
            soff = {"xn": 98304, "rec": 114688, "mrg": 8192, "att": 8192, "hh": 73728}[name]
            ds = [P.dsem(), P.dsem()]
            nchunk = (width + 2047) // 2048
            tv = [view(soff, [128, 2048], F32), view(soff + 8192, [128, 2048], F32)]
            tr = [Res("dbgt0"), Res("dbgt1")]
            for i in range(nchunk):
                lo = i * 2048
                hi = min(width, lo + 2048)
                P.op("dve", lambda e, i=i, lo=lo, hi=hi: e.tensor_copy(out=tv[i % 2][:, 0:hi - lo], in_=src_ap[:, lo:hi]),
                     reads=res_list, writes=[tr[i % 2]])
                P.dma("sp", lambda e, i=i, lo=lo, hi=hi: e.dma_start(out=dbg_d[name][:, lo:hi], in_=tv[i % 2][:, 0:hi - lo]),
                      ds[i % 2], reads=[tr[i % 2]])
            P.barrier()

        P.dma("sp", lambda e: e.dma_start(out=cst[:, :], in_=consts_d), ds_c, writes=[cst_r])
        P.dma("pool", lambda e: e.dma_start(out=cmat[:, :], in_=cmat_d, max_dma_last_dim=4096), P.dsem(), writes=[cmat_r])
        P.op("act", lambda e: e.activation(out=sml[:, TMPA:TMPA + RC], in_=cst[:, CC["lam"]:CC["lam"] + RC], func=AF.Exp, scale=-1.0),
             reads=[cst_r], writes=[sml_r])
        P.op("act", lambda e: e.activation(out=sml[:, TMPB:TMPB + RC], in_=sml[:, TMPA:TMPA + RC], func=AF.Ln, bias=1.0),
             reads=[sml_r], writes=[sml_r])
        P.op("dve", lambda e: e.tensor_scalar(out=sml[:, NSP8:NSP8 + RC], in0=sml[:, TMPB:TMPB + RC], scalar1=-8.0, scalar2=None, op0=ALU.mult),
             reads=[sml_r], writes=[sml_r])
        P.op("dve", lambda e: e.tensor_scalar(out=sml[:, NSP16:NSP16 + RC], in0=sml[:, TMPB:TMPB + RC], scalar1=-16.0, scalar2=None, op0=ALU.mult),
             reads=[sml_r], writes=[sml_r])
        P.op("act", lambda e: e.activation(out=sml[:, ESINK:ESINK + 16], in_=cst[:, CC["sinks"]:CC["sinks"] + 16], func=AF.Exp),
             reads=[cst_r], writes=[sml_r])

        def norm_transpose(src_tile_fn, ntiles, gsel, dst, dst_r, dst_col0, a_off):
            xsb = [view(a_off, [128, D]), view(a_off + 4096, [128, D])]
            xsb_r = [Res("xsb0"), Res("xsb1")]
            sq = view(a_off + 8192, [128, D])
            sq_r = Res("sq")
            gb = view(a_off + 12288, [128, D], F32)
            gb_r = Res("gb")
            P.dma("sp", lambda e: e.dma_start(out=gb, in_=g12_d[gsel]), P.dsem(), writes=[gb_r])
            for i in range(ntiles):
                src, src_r = src_tile_fn(i)
                P.op("act", lambda e, src=src, i=i: e.activation(out=sq, in_=src, func=AF.Square, accum_out=sml[:, SS + i:SS + i + 1]),
                     reads=[src_r], writes=[sq_r, sml_r])
                P.op("dve", lambda e, i=i: e.tensor_scalar(out=sml[:, TMPA + i:TMPA + i + 1], in0=sml[:, SS + i:SS + i + 1],
                                                         scalar1=1.0 / D, scalar2=EPS, op0=ALU.mult, op1=ALU.add),
                     reads=[sml_r], writes=[sml_r])
                P.op("act", lambda e, i=i: e.activation(out=sml[:, TMPB + i:TMPB + i + 1], in_=sml[:, TMPA + i:TMPA + i + 1], func=AF.Sqrt),
                     reads=[sml_r], writes=[sml_r])
                P.op("dve", lambda e, i=i: e.reciprocal(out=sml[:, RSTD + i:RSTD + i + 1], in_=sml[:, TMPB + i:TMPB + i + 1]),
                     reads=[sml_r], writes=[sml_r])
                b = i % 2
                P.op("dve", lambda e, src=src, i=i, b=b: e.scalar_tensor_tensor(out=xsb[b], in0=src, scalar=sml[:, RSTD + i:RSTD + i + 1],
                                                                              in1=gb, op0=ALU.mult, op1=ALU.mult),
                     reads=[src_r, sml_r, gb_r], writes=[xsb_r[b]])
                for hh in range(2):
                    bank = (2 * i + hh) % 4
                    pb = psum[bank][:, :].bitcast(BF16).rearrange("p (a b) -> p a b", a=8)
                    for k8 in range(8):
                        kc = hh * 8 + k8
                        P.op("pe", lambda e, pb=pb, k8=k8, kc=kc, b=b: e.transpose(pb[:, k8, :], xsb[b][:, kc * 128:(kc + 1) * 128], ident),
                             reads=[xsb_r[b], cmat_r], writes=[ps_r[bank]], signal=(k8 == 7))
                    c0 = dst_col0 + i * 128
                    eng = "act" if hh == 0 else "dve"
                    if eng == "act":
                        P.op("act", lambda e, pb=pb, hh=hh, c0=c0: e.activation(out=dst[:, hh * 8:(hh + 1) * 8, c0:c0 + 128], in_=pb, func=AF.Copy),
                             reads=[ps_r[bank]], writes=[dst_r[k][i] for k in range(hh * 8, hh * 8 + 8)])
                    else:
                        P.op("dve", lambda e, pb=pb, hh=hh, c0=c0: e.tensor_copy(out=dst[:, hh * 8:(hh + 1) * 8, c0:c0 + 128], in_=pb),
                             reads=[ps_r[bank]], writes=[dst_r[k][i] for k in range(hh * 8, hh * 8 + 8)])

        xt = [view(0, [128, D], F32), view(8192, [128, D], F32)]
        xt_r = [Res("xt0"), Res("xt1")]
        xt_s = [P.dsem(), P.dsem()]

        def a_src(i):
            b = i % 2
            if i == 0:
                P.dma("sp", lambda e: e.dma_start(out=xt[0], in_=x_halo), xt_s[0], writes=[xt_r[0]])
            else:
                P.dma("sp", lambda e, i=i, b=b: e.dma_start(out=xt[b], in_=x_own[(i - 1) * 128:i * 128, :]), xt_s[b], writes=[xt_r[b]])
            return xt[b], xt_r[b]
        norm_transpose(a_src, 9, 0, xn, xn_r, 0, 16384)
        dbg_dump("xn", xn_t[:, :], [r for k in range(KC) for r in xn_r[k]], KC * TE)
        if stop_after == "A":
            return finish(nc, P, W, specs, out_d, None)

        def xn_reads(tok0, n):
            t0, t1 = tok0 // 128, (tok0 + n - 1) // 128
            return lambda kc: [xn_r[kc][t] for t in range(t0, t1 + 1)]

        P.barrier()
        R0 = view(0, [128, RC, T])
        R0_r = [Res("R0_%d" % c) for c in range(RC)]
        UC = view(45056, [128, 8, T])
        UC_r = [Res("UC%d" % i) for i in range(8)]
        GT = view(61440, [128, 4, T])
        GT_r = [Res("GT%d" % i) for i in range(4)]
        UP = [view(69632, [128, 1040], F32), view(69632 + 4160, [128, 1040], F32)]
        UP_r = [Res("UP0"), Res("UP1")]
        f32t = {}
        for i, nm in enumerate(("acc", "rt", "it", "at", "st", "bt", "ht", "cum", "zeros")):
            f32t[nm] = (view(77952 + 4096 * i, [128, T], F32), Res(nm))
        R1T = [view(114816, [128, T]), view(114816 + 2048, [128, T])]
        R1T_r = [Res("R1T0"), Res("R1T1")]
        r1_s = [P.dsem(), P.dsem()]
        ccs = sml[:, CCS:CCS + 64]
        P.op("dve", lambda e: e.memset(ccs, 0.0), writes=[sml_r])
        P.op("dve", lambda e: e.memset(f32t["zeros"][0], 0.0), writes=[f32t["zeros"][1]])

        def w_in_cols(lo, n):
            return lambda inp: inp["w_in"][0][:, lo:lo + n]

        def spec_k16(colfn_list):
            def fn(inp):
                parts = []
                for wname, lo in colfn_list:
                    wmat = inp[wname][0]
                    parts.append(wmat[:, lo:lo + 128].reshape(KC, 128, 128).transpose(1, 0, 2).reshape(128, KC * 128))
                return np.concatenate(parts, axis=1)
            return fn

        def gate_spec(c):
            kl = gate_klist(c)

            def fn(inp):
                parts = []
                for wname in ("w_rgate", "w_igate"):
                    full = np.zeros((len(kl) * 128, 128), np.float32)
                    wg = inp[wname][0]
                    for n in range(NB_RNN):
                        r0, r1 = n * BW, (n + 1) * BW
                        ro0, ro1 = max(r0, kl[0] * 128), min(r1, (kl[-1] + 1) * 128)
                        co0, co1 = max(r0, c * 128), min(r1, (c + 1) * 128)
                        if ro0 < ro1 and co0 < co1:
                            full[ro0 - kl[0] * 128:ro1 - kl[0] * 128, co0 - c * 128:co1 - c * 128] = wg[n][ro0 - r0:ro1 - r0, co0 - r0:co1 - r0]
                    parts.append(full.reshape(len(kl), 128, 128).transpose(1, 0, 2).reshape(128, len(kl) * 128))
                return np.concatenate(parts, axis=1)
            return fn

        def lru_gates(c):
            kl = gate_klist(c)
            nk = len(kl)
            slab, slab_r = wload(2 * nk * 128, gate_spec(c))
            sv = slab[:, 0:2 * nk * 128].rearrange("p (g k m) -> p g k m", g=2, k=nk)
            for gi in range(2):
                for hf in range(2):
                    bank = 4 + gi * 2 + hf
                    for ki, k in enumerate(kl):
                        mm(psum[bank][:, :], sv[:, gi, ki, :], UC[:, k % 8, hf * 512:(hf + 1) * 512],
                           ki == 0, ki == nk - 1, [slab_r, UC_r[k % 8]], ps_r[bank])
            rt, rt_r = f32t["rt"]
            it, it_r = f32t["it"]
            at, at_r = f32t["at"]
            s_t, st_r = f32t["st"]
            bt, bt_r = f32t["bt"]
            ht, ht_r = f32t["ht"]
            cum, cum_r = f32t["cum"]
            zt, zt_r = f32t["zeros"]
            for hf in range(2):
                sl = slice(hf * 512, (hf + 1) * 512)
                P.op("act", lambda e, hf=hf, sl=sl: e.activation(out=rt[:, sl], in_=psum[4 + hf][:, :], func=AF.Sigmoid, bias=cc("br", c)),
                     reads=[ps_r[4 + hf], cst_r], writes=[rt_r])
                P.op("act", lambda e, hf=hf, sl=sl: e.activation(out=it[:, sl], in_=psum[6 + hf][:, :], func=AF.Sigmoid, bias=cc("bi", c)),
                     reads=[ps_r[6 + hf], cst_r], writes=[it_r])
            P.op("act", lambda e: e.activation(out=at, in_=rt, func=AF.Exp, scale=sml[:, NSP8 + c:NSP8 + c + 1]),
                 reads=[rt_r, sml_r], writes=[at_r])
            P.op("act", lambda e: e.activation(out=s_t, in_=rt, func=AF.Exp, scale=sml[:, NSP16 + c:NSP16 + c + 1]),
                 reads=[rt_r, sml_r], writes=[st_r])
            P.op("act", lambda e: e.activation(out=s_t, in_=s_t, func=AF.Sqrt, scale=-1.0, bias=1.0),
                 reads=[st_r], writes=[st_r])
            P.op("pool", lambda e: e.tensor_tensor(out=bt, in0=it, in1=UC[:, c % 8, :], op=ALU.mult),
                 reads=[it_r, UC_r[c % 8]], writes=[bt_r])
            P.op("pool", lambda e: e.tensor_tensor(out=bt, in0=bt, in1=s_t, op=ALU.mult),
                 reads=[bt_r, st_r], writes=[bt_r])
            P.op("dve", lambda e: e.tensor_tensor_scan(out=ht, data0=at, data1=bt, initial=0.0, op0=ALU.mult, op1=ALU.add),
                 reads=[at_r, bt_r], writes=[ht_r])
            P.op("dve", lambda e: e.tensor_tensor_scan(out=cum, data0=at, data1=zt, initial=1.0, op0=ALU.mult, op1=ALU.add),
                 reads=[at_r, zt_r], writes=[cum_r])
            P.op("dve", lambda e: e.tensor_tensor(out=R0[:, c, :], in0=ht, in1=GT[:, c % 4, :], op=ALU.mult),
                 reads=[ht_r, GT_r[c % 4]], writes=[R0_r[c]])
            b = c % 2
            P.op("dve", lambda e, b=b: e.tensor_tensor(out=R1T[b], in0=cum, in1=GT[:, c % 4, :], op=ALU.mult),
                 reads=[cum_r, GT_r[c % 4]], writes=[R1T_r[b]])
            P.dma("sp", lambda e, b=b: e.dma_start(out=r1_d[c], in_=R1T[b]), r1_s[b], reads=[R1T_r[b]])
            P.op("act", lambda e: e.activation(out=sml[:, CCS + c:CCS + c + 1], in_=cum[:, T - 1:T], func=AF.Copy),
                 reads=[cum_r], writes=[sml_r])
            P.op("act", lambda e: e.activation(out=sml[:, CCS + RC + c:CCS + RC + c + 1], in_=ht[:, T - 1:T], func=AF.Copy),
                 reads=[ht_r], writes=[sml_r])

        ug = {}

        def ug_load(c):
            ug[c] = wload(4096, spec_k16([("w_in", SPL[2] + c * 128), ("w_in", SPL[3] + c * 128)]))
        ug_load(0)

        def lru_chunk(c):
            if c + 1 < RC:
                ug_load(c + 1)
            slab, slab_r = ug[c]
            sv = slab[:, :].rearrange("p (g k m) -> p g k m", g=2, k=KC)
            for hf in range(2):
                rd = xn_reads(128 + hf * 512, 512)
                for k in range(KC):
                    mm(psum[hf][:, :], sv[:, 0, k, :], xn[:, k, 128 + hf * 512:128 + (hf + 1) * 512],
                       k == 0, k == KC - 1, [slab_r] + rd(k), ps_r[hf])
            for k in range(KC):
                mm(psum[2][:, 0:8], sv[:, 0, k, :], xn[:, k, 120:128], k == 0, k == KC - 1, [slab_r, xn_r[k][0]], ps_r[2])
            up, up_r = UP[c % 2], UP_r[c % 2]
            P.op("act", lambda e, up=up: e.activation(out=up[:, 0:8], in_=psum[2][:, 0:8], func=AF.Copy), reads=[ps_r[2]], writes=[up_r])
            P.op("act", lambda e, up=up: e.activation(out=up[:, 8:520], in_=psum[0][:, :], func=AF.Copy), reads=[ps_r[0]], writes=[up_r])
            P.op("act", lambda e, up=up: e.activation(out=up[:, 520:1032], in_=psum[1][:, :], func=AF.Copy), reads=[ps_r[1]], writes=[up_r])
            for hf in range(2):
                rd = xn_reads(128 + hf * 512, 512)
                for k in range(KC):
                    mm(psum[3][:, :], sv[:, 1, k, :], xn[:, k, 128 + hf * 512:128 + (hf + 1) * 512],
                       k == 0, k == KC - 1, [slab_r] + rd(k), ps_r[3])
                P.op("act", lambda e, hf=hf: e.activation(out=GT[:, c % 4, hf * 512:(hf + 1) * 512], in_=psum[3][:, :], func=AF.Gelu_apprx_tanh),
                     reads=[ps_r[3]], writes=[GT_r[c % 4]])
            acc, acc_r = f32t["acc"]
            P.op("act", lambda e, up=up: e.activation(out=acc, in_=up[:, 5:5 + T], func=AF.Identity, scale=cc("cw", 0 * 22 + c), bias=cc("cb", c)),
                 reads=[up_r, cst_r], writes=[acc_r])
            for tap in (1, 2):
                P.op("dve", lambda e, up=up, tap=tap: e.scalar_tensor_tensor(out=acc, in0=up[:, 5 + tap:5 + tap + T], scalar=cc("cw", tap * 22 + c),
                                                                             in1=acc, op0=ALU.mult, op1=ALU.add),
                     reads=[up_r, cst_r, acc_r], writes=[acc_r])
            P.op("dve", lambda e, up=up: e.scalar_tensor_tensor(out=UC[:, c % 8, :], in0=up[:, 8:8 + T], scalar=cc("cw", 3 * 22 + c),
                                                               in1=acc, op0=ALU.mult, op1=ALU.add),
                 reads=[up_r, cst_r, acc_r], writes=[UC_r[c % 8]])
        gates_done = 0
        for c in range(RC):
            lru_chunk(c)
            while gates_done < RC and gate_klist(gates_done)[-1] <= c:
                lru_gates(gates_done)
                gates_done += 1

        assert gates_done == RC

        if stop_after == "L0":
            return finish(nc, P, W, specs, out_d, None)
        ccin_r, ccout_r = Res("ccin"), Res("ccout")
        P.dma("pool", lambda e: e.dma_start(out=cc_in, in_=ccs), P.dsem(), reads=[sml_r], writes=[ccin_r])
        ds_cc = P.dsem()
        waits = P._collect("pool", [ccin_r], [ccout_r])
        ds_cc.cnt += 1
        ev = (ds_cc.sem, ds_cc.cnt, "dma")
        import os as _os
        if _os.environ.get("KDBG_NOCC"):
            P.ops["pool"].append((waits, lambda e: e.dma_start(out=cc_out[0:128, :], in_=cc_in), (ds_cc.sem, 1)))
            P.ops["pool"].append(([(ds_cc.sem, 1)], None, None))
            ds_cc.cnt = 16
            ev = (ds_cc.sem, 16, "dma")
            P.ops["pool"][-2] = (waits, lambda e: e.dma_start(out=cc_out[0:128, :], in_=cc_in), (ds_cc.sem, 16))
            P.ops["pool"].pop()
        else:
            P.ops["pool"].append((waits, lambda e: e.collective_compute("AllGather", ALU.bypass, replica_groups=[list(range(8))],
                                                                        ins=[cc_in], outs=[cc_out]), (ds_cc.sem, 1)))
        ccout_r.w = ev
        gth_t = nc.alloc_sbuf_tensor("gth", [128, 8 * 64], F32)
        gth = gth_t[:, :].rearrange("p (r c) -> p r c", r=8)
        gth_r = Res("gth")
        P.barrier()
        MG = view(98304, [128, KC, T])
        MG_r = [Res("MG%d" % m) for m in range(KC)]
        itg = 0
        for mp in range(KC // 2):
            gslab, gslab_r = wload(4096, spec_k16([("w_in", SPL[5] + (2 * mp) * 128), ("w_in", SPL[5] + (2 * mp + 1) * 128)]))
            gv = gslab[:, :].rearrange("p (g k m) -> p g k m", g=2, k=KC)
            for mi in range(2):
                m = 2 * mp + mi
                for hf in range(2):
                    bg = itg % 4
                    itg += 1
                    rd = xn_reads(128 + hf * 512, 512)
                    for k in range(KC):
                        mm(psum[bg][:, :], gv[:, mi, k, :], xn[:, k, 128 + hf * 512:128 + (hf + 1) * 512],
                           k == 0, k == KC - 1, [gslab_r] + rd(k), ps_r[bg])
                    P.op("act", lambda e, bg=bg, m=m, hf=hf: e.activation(out=MG[:, m, hf * 512:(hf + 1) * 512], in_=psum[bg][:, :], func=AF.Sigmoid,
                                                                          bias=cc("bgl", m)),
                         reads=[ps_r[bg], cst_r], writes=[MG_r[m]])
        P.dma("sp", lambda e: e.dma_start(out=gth, in_=cc_out.rearrange("(r p) c -> p r c", p=128)), P.dsem(), reads=[ccout_r], writes=[gth_r])
        hs = sml[:, HS:HS + RC]
        P.op("dve", lambda e: e.memset(hs, 0.0), writes=[sml_r])
        for k in range(8):
            selk = cc("sel", k)
            P.op("dve", lambda e, k=k, selk=selk: e.tensor_scalar(out=sml[:, CHA:CHA + RC], in0=gth[:, k, 0:RC], scalar1=-1.0, scalar2=selk,
                                                                  op0=ALU.add, op1=ALU.mult),
                 reads=[gth_r, cst_r], writes=[sml_r])
            P.op("dve", lambda e: e.tensor_scalar(out=sml[:, CHA:CHA + RC], in0=sml[:, CHA:CHA + RC], scalar1=1.0, scalar2=None, op0=ALU.add),
                 reads=[sml_r], writes=[sml_r])
            P.op("dve", lambda e, k=k, selk=selk: e.tensor_scalar(out=sml[:, CHB:CHB + RC], in0=gth[:, k, RC:2 * RC], scalar1=selk, scalar2=None,
                                                                  op0=ALU.mult),
                 reads=[gth_r, cst_r], writes=[sml_r])
            P.op("dve", lambda e: e.tensor_tensor(out=hs, in0=hs, in1=sml[:, CHA:CHA + RC], op=ALU.mult), reads=[sml_r], writes=[sml_r])
            P.op("dve", lambda e: e.tensor_tensor(out=hs, in0=hs, in1=sml[:, CHB:CHB + RC], op=ALU.add), reads=[sml_r], writes=[sml_r])
        r1b = [view(45056, [128, T]), view(45056 + 2048, [128, T])]
        r1b_r = [Res("r1b0"), Res("r1b1")]
        r1b_s = [P.dsem(), P.dsem()]
        for c in range(RC):
            b = c % 2
            P.dma("sp", lambda e, c=c, b=b: e.dma_start(out=r1b[b], in_=r1_d[c]), r1b_s[b], writes=[r1b_r[b]])
            P.op("dve", lambda e, c=c, b=b: e.scalar_tensor_tensor(out=R0[:, c, :], in0=r1b[b], scalar=sml[:, HS + c:HS + c + 1], in1=R0[:, c, :],
                                                                  op0=ALU.mult, op1=ALU.add),
                 reads=[r1b_r[b], sml_r, R0_r[c]], writes=[R0_r[c]])
        dbg_dump("rec", R0.rearrange("p a b -> p (a b)"), R0_r, RC * T)
        if stop_after == "L":
            return finish(nc, P, W, specs, out_d, None)

        def lp_spec(m):
            return lambda inp: inp["w_lru_proj"][0][:, m * 128:(m + 1) * 128].reshape(RC, 128, 128).transpose(1, 0, 2).reshape(128, RC * 128)
        itp = 0
        for m in range(KC):
            pslab, pslab_r = wload(RC * 128, lp_spec(m))
            pv = pslab[:, 0:RC * 128].rearrange("p (k m) -> p k m", k=RC)
            for hf in range(2):
                ba = itp % 4
                itp += 1
                for k in range(RC):
                    mm(psum[ba][:, :], pv[:, k, :], R0[:, k, hf * 512:(hf + 1) * 512], k == 0, k == RC - 1, [pslab_r, R0_r[k]], ps_r[ba])
                dst = MG[:, m, hf * 512:(hf + 1) * 512]
                P.op("dve", lambda e, ba=ba, dst=dst: e.tensor_tensor(out=dst, in0=psum[ba][:, :], in1=dst, op=ALU.mult),
                     reads=[ps_r[ba], MG_r[m]], writes=[MG_r[m]])
        if stop_after == "ML":
            dbg_dump("mrg", MG.rearrange("p a b -> p (a b)"), MG_r, KC * T)
            return finish(nc, P, W, specs, out_d, None)

        def proj_gate_stage(K, act_ap_fn, act_res_fn, pspec, gate_col0, bias_name, first, sg, sg_r, tA, tA_r):
            it_ = 0
            for mp in range(KC // 2):
                gslab, gslab_r = wload(4096, spec_k16([("w_in", gate_col0 + (2 * mp) * 128), ("w_in", gate_col0 + (2 * mp + 1) * 128)]))
                gv = gslab[:, :].rearrange("p (g k m) -> p g k m", g=2, k=KC)
                for mi in range(2):
                    m = 2 * mp + mi
                    if mi == 0:
                        pslab, pslab_r = wload(4096, pspec(mp))
                        pv2 = pslab[:, :].rearrange("p (g k m) -> p g k m", g=2, k=KC)
                    pv = pv2[:, mi]
                    for hf in range(2):
                        ba = (it_ % 2) * 2
                        bg = ba + 1
                        for k in range(K):
                            mm(psum[ba][:, :], pv[:, k, :], act_ap_fn(k, hf), k == 0, k == K - 1, [pslab_r, act_res_fn(k)], ps_r[ba])
                        rd = xn_reads(128 + hf * 512, 512)
                        for k in range(KC):
                            mm(psum[bg][:, :], gv[:, mi, k, :], xn[:, k, 128 + hf * 512:128 + (hf + 1) * 512],
                               k == 0, k == KC - 1, [gslab_r] + rd(k), ps_r[bg])
                        b = it_ % 2
                        P.op("act", lambda e, b=b, bg=bg, m=m: e.activation(out=sg[b], in_=psum[bg][:, :], func=AF.Sigmoid, bias=cc(bias_name, m)),
                             reads=[ps_r[bg], cst_r], writes=[sg_r[b]])
                        dst = MG[:, m, hf * 512:(hf + 1) * 512]
                        P.op("dve", lambda e, b=b, ba=ba: e.tensor_tensor(out=tA[b], in0=psum[ba][:, :], in1=sg[b], op=ALU.mult),
                             reads=[ps_r[ba], sg_r[b]], writes=[tA_r[b]])
                        P.op("dve", lambda e, b=b, dst=dst: e.tensor_tensor(out=dst, in0=tA[b], in1=dst, op=ALU.add),
                             reads=[tA_r[b], MG_r[m]], writes=[MG_r[m]])
                        it_ += 1

        P.barrier()
        AT = view(65536, [128, NQ, T])
        AT_r = [Res("AT%d" % h) for h in range(NQ)]
        QT = view(0, [128, 8, 4, 128])
        QT_r = [Res("QT%d" % qb) for qb in range(8)]
        KT = view(8192, [128, TE])
        KT_r = [Res("KT%d" % t) for t in range(9)]
        VT = view(10496, [128, 9, 512])
        VT_r = [Res("VT%d" % t) for t in range(9)]
        Ctab = view(19712, [128, TE], F32)
        Stab = view(24320, [128, TE], F32)
        tab_r = Res("tab")
        sqb = [view(28928, [128, 512]), view(28928 + 1024, [128, 512])]
        sqb_r = [Res("sqb0"), Res("sqb1")]
        rsd = [view(30976, [128, 512], F32), view(30976 + 2048, [128, 512], F32)]
        rsd_r = [Res("rsd0"), Res("rsd1")]
        qnf = [view(35072, [128, 512], F32), view(35072 + 2048, [128, 512], F32)]
        qnf_r = [Res("qnf0"), Res("qnf1")]
        qnb = [view(39168, [128, 512]), view(39168 + 1024, [128, 512])]
        qnb_r = [Res("qnb0"), Res("qnb1")]
        rt1 = [view(41216, [128, 512], F32), view(41216 + 2048, [128, 512], F32)]
        rt1_r = [Res("rt1_0"), Res("rt1_1")]
        rt2 = [view(45312, [128, 512], F32), view(45312 + 2048, [128, 512], F32)]
        rt2_r = [Res("rt2_0"), Res("rt2_1")]
        EB = [view(49408 + 1024 * i, [128, 512]) for i in range(4)]
        EB_r = [Res("EB%d" % i) for i in range(4)]
        dns = [view(53504, [128, 512], F32), view(53504 + 2048, [128, 512], F32)]
        dns_r = [Res("dns0"), Res("dns1")]
        tpi = view(65536 + 16384, [128, TE], I32)
        tA_ = view(65536 + 16384 + 4608, [128, TE], F32)
        tB_ = view(65536 + 16384 + 9216, [128, TE], F32)
        tmp_r = Res("ropetmp")
        pi_ = tpi[0:32, :]
        A_ = tA_[0:32, :]
        B_ = tB_[0:32, :]
        Bi_ = tB_[0:32, :].bitcast(I32)
        pf_ = tpi[0:32, :].bitcast(F32)
        P.dma("sp", lambda e: e.dma_start(out=pi_, in_=pos_d), P.dsem(), writes=[tmp_r])
        P.op("dve", lambda e: e.tensor_copy(out=A_, in_=pi_), reads=[tmp_r], writes=[tmp_r])
        P.op("dve", lambda e: e.tensor_scalar(out=A_, in0=A_, scalar1=cst[0:32, CC["invf"]:CC["invf"] + 1], scalar2=1.0 / (2 * math.pi),
                                              op0=ALU.mult, op1=ALU.mult), reads=[tmp_r, cst_r], writes=[tmp_r])
        for which, tab in ((0, Stab), (1, Ctab)):
            if which == 1:
                P.op("dve", lambda e: e.tensor_scalar(out=A_, in0=A_, scalar1=0.25, scalar2=None, op0=ALU.add), reads=[tmp_r], writes=[tmp_r])
            P.op("dve", lambda e: e.tensor_copy(out=Bi_, in_=A_), reads=[tmp_r], writes=[tmp_r])
            P.op("dve", lambda e: e.tensor_copy(out=pf_, in_=Bi_), reads=[tmp_r], writes=[tmp_r])
            P.op("dve", lambda e: e.tensor_tensor(out=pf_, in0=A_, in1=pf_, op=ALU.subtract), reads=[tmp_r], writes=[tmp_r])
            P.op("dve", lambda e: e.tensor_scalar(out=B_, in0=pf_, scalar1=0.5, scalar2=None, op0=ALU.is_gt), reads=[tmp_r], writes=[tmp_r])
            P.op("dve", lambda e: e.tensor_tensor(out=pf_, in0=pf_, in1=B_, op=ALU.subtract), reads=[tmp_r], writes=[tmp_r])
            P.op("dve", lambda e: e.tensor_scalar(out=B_, in0=pf_, scalar1=-0.5, scalar2=None, op0=ALU.is_lt), reads=[tmp_r], writes=[tmp_r])
            P.op("dve", lambda e: e.tensor_tensor(out=pf_, in0=pf_, in1=B_, op=ALU.add), reads=[tmp_r], writes=[tmp_r])
            P.op("act", lambda e, tab=tab: e.activation(out=tab[0:32, :], in_=pf_, func=AF.Sin, scale=2 * math.pi), reads=[tmp_r], writes=[tab_r])

        vs = []
        for half in range(2):
            def vspec(inp, half=half):
                wv = inp["w_in"][0][half * 1024:(half + 1) * 1024, SPL[1]:SPL[2]]
                return wv.reshape(8, 128, 512).transpose(1, 0, 2).reshape(128, 4096)
            vs.append(wload(4096, vspec))
        for t in range(9):
            bank = t % 2
            for k in range(KC):
                slab, slab_r = vs[k // 8]
                mm(psum[bank][:, :], xn[:, k, t * 128:(t + 1) * 128], slab[:, (k % 8) * 512:(k % 8 + 1) * 512],
                   k == 0, k == KC - 1, [slab_r, xn_r[k][t]], ps_r[bank])
            P.op("act", lambda e, t=t, bank=bank: e.activation(out=VT[:, t, :], in_=psum[bank][:, :], func=AF.Copy),
                 reads=[ps_r[bank]], writes=[VT_r[t]])

        cnt = {"i": 0}
        pbc = {"i": 0}

        def qk_norm_rope(pbank, gname, col0, ncols, dst_ap, dst_res):
            i = cnt["i"] % 2
            cnt["i"] += 1
            pq = psum[pbank][:, 0:ncols]
            P.op("act", lambda e: e.activation(out=sqb[i][:, 0:ncols], in_=pq, func=AF.Square), reads=[ps_r[pbank]], writes=[sqb_r[i]])
            mm(psum[6][:, 0:ncols], ones_b, sqb[i][:, 0:ncols], True, True, [cmat_r, sqb_r[i]], ps_r[6])
            P.op("act", lambda e: e.activation(out=rsd[i][:, 0:ncols], in_=psum[6][:, 0:ncols], func=AF.Sqrt, scale=1.0 / HD, bias=EPS),
                 reads=[ps_r[6]], writes=[rsd_r[i]])
            P.op("dve", lambda e: e.reciprocal(out=rsd[i][:, 0:ncols], in_=rsd[i][:, 0:ncols]), reads=[rsd_r[i]], writes=[rsd_r[i]])
            P.op("dve", lambda e: e.scalar_tensor_tensor(out=qnf[i][:, 0:ncols], in0=pq, scalar=cc(gname), in1=rsd[i][:, 0:ncols],
                                                         op0=ALU.mult, op1=ALU.mult),
                 reads=[ps_r[pbank], cst_r, rsd_r[i]], writes=[qnf_r[i]])
            P.op("act", lambda e: e.activation(out=qnb[i][:, 0:ncols], in_=qnf[i][:, 0:ncols], func=AF.Copy), reads=[qnf_r[i]], writes=[qnb_r[i]])
            mm(psum[7][0:32, 0:ncols], rotm, qnb[i][0:32, 0:ncols], True, True, [cmat_r, qnb_r[i]], ps_r[7])
            P.op("dve", lambda e: e.tensor_tensor(out=rt1[i][0:32, 0:ncols], in0=psum[7][0:32, 0:ncols], in1=Stab[0:32, col0:col0 + ncols], op=ALU.mult),
                 reads=[ps_r[7], tab_r], writes=[rt1_r[i]])
            P.op("dve", lambda e: e.tensor_tensor(out=rt2[i][0:32, 0:ncols], in0=qnf[i][0:32, 0:ncols], in1=Ctab[0:32, col0:col0 + ncols], op=ALU.mult),
                 reads=[qnf_r[i], tab_r], writes=[rt2_r[i]])
            P.op("dve", lambda e: e.tensor_tensor(out=qnb[i][0:32, 0:ncols], in0=rt1[i][0:32, 0:ncols], in1=rt2[i][0:32, 0:ncols], op=ALU.add),
                 reads=[rt1_r[i], rt2_r[i], qnb_r[i]], writes=[qnb_r[i]])
            srcv = qnb[i][:, 0:ncols]
            if len(dst_ap.shape) == 3:
                srcv = srcv.rearrange("p (a b) -> p a b", a=dst_ap.shape[1])
            P.op("act", lambda e: e.activation(out=dst_ap, in_=srcv, func=AF.Copy), reads=[qnb_r[i]], writes=dst_res)

        scale = 1.0 / math.sqrt(HD)
        for g in range(NKV):
            slab, slab_r = wload(4096, spec_k16([("w_in", SPL[0] + g * 128), ("w_in", g * 4 * 128)]))
            sv = slab[:, :].rearrange("p (g k m) -> p g k m", g=2, k=KC)
            for (c0, n) in ((0, 128), (128, 512), (640, 512)):
                rd = xn_reads(c0, n)
                pbk = 4 + pbc["i"] % 2
                pbc["i"] += 1
                for k in range(KC):
                    mm(psum[pbk][:, 0:n], sv[:, 0, k, :], xn[:, k, c0:c0 + n], k == 0, k == KC - 1, [slab_r] + rd(k), ps_r[pbk])
                qk_norm_rope(pbk, "kg", c0, n, KT[:, c0:c0 + n], [KT_r[t] for t in range(c0 // 128, (c0 + n) // 128)])
            for hh in range(4):
                h = g * 4 + hh
                if hh == 0:
                    qv = sv[:, 1]
                    q_r = slab_r
                elif hh in (1, 3):
                    pass
                if hh == 1:
                    slab2, slab2_r = wload(4096, spec_k16([("w_in", (h) * 128), ("w_in", (h + 1) * 128)]))
                    sv2 = slab2[:, :].rearrange("p (g k m) -> p g k m", g=2, k=KC)
                    qv, q_r = sv2[:, 0], slab2_r
                elif hh == 2:
                    qv, q_r = sv2[:, 1], slab2_r
                elif hh == 3:
                    slab3, slab3_r = wload(2048, spec_k16([("w_in", h * 128)]))
                    qv = slab3[:, 0:2048].rearrange("p (k m) -> p k m", k=KC)
                    q_r = slab3_r
                for hf in range(2):
                    rd = xn_reads(128 + hf * 512, 512)
                    pbk = 4 + pbc["i"] % 2
                    pbc["i"] += 1
                    for k in range(KC):
                        mm(psum[pbk][:, :], qv[:, k, :], xn[:, k, 128 + hf * 512:128 + (hf + 1) * 512], k == 0, k == KC - 1, [q_r] + rd(k), ps_r[pbk])
                    qk_norm_rope(pbk, "qg", 128 + hf * 512, 512,
                                 QT[:, hf * 4:(hf + 1) * 4, hh, :], [QT_r[qb] for qb in range(hf * 4, hf * 4 + 4)])
            for qb in range(8):
                eb = (qb % 2) * 2
                for which in range(2):
                    kt = qb + which
                    bank = which
                    mm(psum[bank][:, :], KT[:, kt * 128:(kt + 1) * 128], QT[:, qb].rearrange("p a b -> p (a b)"), True, True,
                       [KT_r[kt], QT_r[qb]], ps_r[bank])
                    P.op("act", lambda e, bank=bank, which=which, eb=eb: e.activation(out=EB[eb + which], in_=psum[bank][:, :], func=AF.Exp, scale=scale),
                         reads=[ps_r[bank]], writes=[EB_r[eb + which]])
                    msk = mask_cur if which == 1 else (mask_prev0 if qb == 0 else mask_prev)
                    P.op("dve", lambda e, which=which, eb=eb, msk=msk: e.tensor_tensor(out=EB[eb + which], in0=EB[eb + which], in1=msk, op=ALU.mult),
                         reads=[EB_r[eb + which], cmat_r], writes=[EB_r[eb + which]])
                for which in range(2):
                    kt = qb + which
                    mm(psum[2][:, :], VT[:, kt, g * 128:(g + 1) * 128], EB[eb + which], which == 0, which == 1, [VT_r[kt], EB_r[eb + which]], ps_r[2])
                for which in range(2):
                    mm(psum[3][:, :], ones_b, EB[eb + which], which == 0, which == 1, [cmat_r, EB_r[eb + which]], ps_r[3])
                d = qb % 2
                for hh in range(4):
                    h = g * 4 + hh
                    P.op("dve", lambda e, d=d, hh=hh, h=h: e.tensor_scalar(out=dns[d][:, hh * 128:(hh + 1) * 128], in0=psum[3][:, hh * 128:(hh + 1) * 128],
                                                                          scalar1=sml[:, ESINK + h:ESINK + h + 1], scalar2=None, op0=ALU.add),
                         reads=[ps_r[3], sml_r], writes=[dns_r[d]])
                P.op("dve", lambda e, d=d: e.reciprocal(out=dns[d], in_=dns[d]), reads=[dns_r[d]], writes=[dns_r[d]])
                extra = [tmp_r] if g >= 2 else []
                P.op("dve", lambda e, d=d, qb=qb, g=g: e.tensor_tensor(out=AT[:, g * 4:(g + 1) * 4, qb * 128:(qb + 1) * 128],
                                                                       in0=psum[2][:, :].rearrange("p (a b) -> p a b", a=4),
                                                                       in1=dns[d].rearrange("p (a b) -> p a b", a=4), op=ALU.mult),
                     reads=[ps_r[2], dns_r[d]], writes=[AT_r[g * 4 + hh] for hh in range(4)] + extra)
        dbg_dump("att", AT.rearrange("p a b -> p (a b)"), AT_r, KC * T)
        if stop_after == "AT":
            return finish(nc, P, W, specs, out_d, None)

        P.barrier()
        sg = [view(0, [128, 512], F32), view(2048, [128, 512], F32)]
        sg_r = [Res("sg0b"), Res("sg1b")]
        tA = [view(4096, [128, 512], F32), view(6144, [128, 512], F32)]
        tA_r = [Res("tA0b"), Res("tA1b")]

        def ap_spec(mp):
            return spec_k16([("w_attn_proj", (2 * mp) * 128), ("w_attn_proj", (2 * mp + 1) * 128)])
        proj_gate_stage(KC, lambda k, hf: AT[:, k, hf * 512:(hf + 1) * 512], lambda k: AT_r[k], ap_spec, SPL[4], "bga", False, sg, sg_r, tA, tA_r)
        dbg_dump("mrg", MG.rearrange("p a b -> p (a b)"), MG_r, KC * T)
        if stop_after == "MA":
            return finish(nc, P, W, specs, out_d, None)

        P.barrier()
        H = view(0, [128, 8, D], F32)
        H_r = [[Res("H%d_%d" % (t, n)) for n in range(4)] for t in range(8)]
        h_s = P.dsem()
        for t in range(8):
            P.dma("sp", lambda e, t=t: e.dma_start(out=H[:, t, :], in_=x_own[t * 128:(t + 1) * 128, :]), P.dsem(), writes=H_r[t])
        it_ = 0
        for n in range(4):
            ws = []
            for half in range(2):
                def wospec(inp, half=half, n=n):
                    w = inp["w_out"][0][half * 1024:(half + 1) * 1024, n * 512:(n + 1) * 512]
                    return w.reshape(8, 128, 512).transpose(1, 0, 2).reshape(128, 4096)
                ws.append(wload(4096, wospec))
            for t in range(8):
                bank = it_ % 4
                it_ += 1
                for k in range(KC):
                    slab, slab_r = ws[k // 8]
                    mm(psum[bank][:, :], MG[:, k, t * 128:(t + 1) * 128], slab[:, (k % 8) * 512:(k % 8 + 1) * 512],
                       k == 0, k == KC - 1, [slab_r, MG_r[k]], ps_r[bank])
                P.op("dve", lambda e, t=t, n=n, bank=bank: e.tensor_tensor(out=H[:, t, n * 512:(n + 1) * 512], in0=psum[bank][:, :],
                                                                            in1=H[:, t, n * 512:(n + 1) * 512], op=ALU.add),
                     reads=[ps_r[bank], H_r[t][n]], writes=[H_r[t][n]])
        dbg_dump("hh", H.rearrange("p a b -> p (a b)"), [r for t in range(8) for r in H_r[t]], 8 * D)
        if stop_after == "C":
            return finish(nc, P, W, specs, out_d, (H, H_r))
        P.barrier()
        hn = xn_t[:, 0:KC * T].rearrange("p (k t) -> p k t", k=KC)
        hn_r = [[Res("hn%d_%d" % (k, t)) for t in range(8)] for k in range(KC)]

        class _AllH:
            pass

        def c_src(i):
            r = Res("Hall%d" % i)
            r.w = H_r[i][3].w
            return H[:, i, :], r
        norm_transpose(c_src, 8, 1, hn, hn_r, 0, 65536)

        P.barrier()
        ACTB = [view(65536, [128, 4, T]), view(65536 + 8192, [128, 4, T])]
        ACTB_r = [[Res("act%d_%d" % (b, cc_)) for cc_ in range(4)] for b in range(2)]
        slb = [view(81920, [128, 512], F32), view(81920 + 2048, [128, 512], F32)]
        slb_r = [Res("sl0"), Res("sl1")]

        def hn_reads(hf):
            return lambda kc: [hn_r[kc][t] for t in range(hf * 4, hf * 4 + 4)]
        it2 = 0
        for gi in range(FC // 4):
            ab = gi % 2
            for pr in range(2):
                c0 = 4 * gi + 2 * pr
                gs, gs_r = wload(4096, spec_k16([("w_ffn_gate", c0 * 128), ("w_ffn_gate", (c0 + 1) * 128)]))
                us, us_r = wload(4096, spec_k16([("w_ffn_up", c0 * 128), ("w_ffn_up", (c0 + 1) * 128)]))
                gv = gs[:, :].rearrange("p (g k m) -> p g k m", g=2, k=KC)
                uv = us[:, :].rearrange("p (g k m) -> p g k m", g=2, k=KC)
                for ci in range(2):
                    cj = 2 * pr + ci
                    for hf in range(2):
                        bg_, bu_ = hf, 2 + hf
                        rd = hn_reads(hf)
                        for k in range(KC):
                            mm(psum[bg_][:, :], gv[:, ci, k, :], hn[:, k, hf * 512:(hf + 1) * 512], k == 0, k == KC - 1, [gs_r] + rd(k), ps_r[bg_])
                        for k in range(KC):
                            mm(psum[bu_][:, :], uv[:, ci, k, :], hn[:, k, hf * 512:(hf + 1) * 512], k == 0, k == KC - 1, [us_r] + rd(k), ps_r[bu_])
                        P.op("act", lambda e, hf=hf, bg_=bg_: e.activation(out=slb[hf], in_=psum[bg_][:, :], func=AF.Silu), reads=[ps_r[bg_]], writes=[slb_r[hf]])
                        P.op("dve", lambda e, hf=hf, bu_=bu_, ab=ab, cj=cj: e.tensor_tensor(out=ACTB[ab][:, cj, hf * 512:(hf + 1) * 512], in0=psum[bu_][:, :],
                                                                                           in1=slb[hf], op=ALU.mult),
                             reads=[ps_r[bu_], slb_r[hf]], writes=[ACTB_r[ab][cj]])
            dsl = []
            for pr in range(2):
                def dspec(inp, r0=(4 * gi + 2 * pr) * 128):
                    w = inp["w_ffn_down"][0][r0:r0 + 256, :]
                    return w.reshape(2, 128, D).transpose(1, 0, 2).reshape(128, 4096)
                ds_, ds_r = wload(4096, dspec)
                dsl.append((ds_[:, :].rearrange("p (c n) -> p c n", c=2), ds_r))
            for t in range(8):
                for n in range(4):
                    bank = 4 + it2 % 4
                    it2 += 1
                    for cj in range(4):
                        dv, ds_r = dsl[cj // 2]
                        mm(psum[bank][:, :], ACTB[ab][:, cj, t * 128:(t + 1) * 128], dv[:, cj % 2, n * 512:(n + 1) * 512], cj == 0, cj == 3,
                           [ACTB_r[ab][cj], ds_r], ps_r[bank])
                    P.op("dve", lambda e, t=t, n=n, bank=bank: e.tensor_tensor(out=H[:, t, n * 512:(n + 1) * 512], in0=psum[bank][:, :],
                                                                                in1=H[:, t, n * 512:(n + 1) * 512], op=ALU.add),
                         reads=[ps_r[bank], H_r[t][n]], writes=[H_r[t][n]])
        return finish(nc, P, W, specs, out_d, (H, H_r))


def finish(nc, P, W, specs, out_d, Hinfo):
    evs = []
    if Hinfo is not None:
        H, H_r = Hinfo
        for t in range(8):
            evs.append(P.dma("sp", lambda e, t=t: e.dma_start(out=out_d[t * 128:(t + 1) * 128, :], in_=H[:, t, :]), P.dsem(), reads=H_r[t]))
    P.barrier()
    P.wait("sp", evs)
    W["ap"] = nc.dram_tensor("wstream", [max(W["off"], 128)], F32, kind="ExternalInput").ap()
    P.emit()
    return nc, specs


def pack_weights(specs, inp):
    parts = []
    for n_el, fn in specs:
        a = np.ascontiguousarray(fn(inp), dtype=np.float32)
        assert a.shape == (128, n_el), (a.shape, n_el)
        parts.append(a.reshape(-1))
    if not parts:
        return np.zeros(128, np.float32)
    return np.concatenate(parts)


def make_in_maps(inp, specs):
    inp = {k: np.asarray(v) for k, v in inp.items()}
    wst = pack_weights(specs, inp)
    g12 = np.stack([np.broadcast_to(inp["norm1_g"][0][None, :], (128, D)),
                    np.broadcast_to(inp["norm2_g"][0][None, :], (128, D))]).astype(np.float32)
    maps = []
    for core in range(8):
        b, j = core // 4, core % 4
        t0 = j * T
        x_own = np.ascontiguousarray(inp["x"][b, t0:t0 + T])
        if j > 0:
            x_halo = np.ascontiguousarray(inp["x"][b, t0 - 128:t0])
            pos = inp["positions"][b, t0 - 128:t0 + T]
        else:
            x_halo = np.zeros((128, D), np.float32)
            pos = np.concatenate([np.zeros(128, np.int32), inp["positions"][b, 0:T]])
        maps.append({
            "x_own": x_own, "x_halo": x_halo,
            "pos": np.ascontiguousarray(np.broadcast_to(pos[None, :].astype(np.int32), (32, TE))),
            "consts": host_consts(inp, core), "cmat": host_cmat(core), "g12": g12, "wstream": wst,
        })
    return maps


_CACHE = {}


def kernel(**inputs):
    if "nc" not in _CACHE:
        _CACHE["nc"] = build_nc()
    nc, specs = _CACHE["nc"]
    maps = make_in_maps(inputs, specs)
    res = run_bass_kernel_spmd(nc, maps, core_ids=list(range(8)))
    out = np.zeros((2, S, D), np.float32)
    for core in range(8):
        b, j = core // 4, core % 4
        out[b, j * T:(j + 1) * T] = res.results[core]["out"]
    return out
```

```python
import contextlib
import math
import numpy as np
import concourse.bass as bass
import concourse.mybir as mybir
from concourse.bass_utils import run_bass_kernel_spmd

F32 = mybir.dt.float32
BF16 = mybir.dt.bfloat16
I32 = mybir.dt.int32
AF = mybir.ActivationFunctionType
ALU = mybir.AluOpType

D = 2048
S = 4096
T = 1024
TE = 1152
HD = 128
NQ = 16
NKV = 4
DR = 2816
NB_RNN = 16
BW = 176
DFF = 5632
KC = 16
RC = 22
FC = 44
EPS = 1e-6
THETA = 500000.0
Q_W = 2048
KV_W = 512
SPL = np.cumsum([Q_W, KV_W, KV_W, DR, DR, D]).tolist()

ENGS = ("pe", "act", "dve", "pool", "sp")
SAME_ENGINE_SYNC = True
NRING = 4
SLAB = 4096


class Res:
    __slots__ = ("name", "w", "r")

    def __init__(self, name=""):
        self.name = name
        self.w = None
        self.r = {}


class DSem:
    __slots__ = ("sem", "cnt")

    def __init__(self, sem):
        self.sem = sem
        self.cnt = 0


class Prog:
    def __init__(self, nc, stack):
        self.nc = nc
        self.stack = stack
        self.sem = {e: stack.enter_context(nc.semaphore("s_" + e)) for e in ENGS}
        self.cnt = {e: 0 for e in ENGS}
        self.ops = {e: [] for e in ENGS}
        self.waited = {e: {} for e in ENGS}
        self.pending = {e: [] for e in ENGS}
        self.ndsem = 0
        self.dma_evs = []

    def dsem(self):
        self.ndsem += 1
        return DSem(self.stack.enter_context(self.nc.semaphore("d%d" % self.ndsem)))

    def _collect(self, e, reads, writes):
        evs = []
        for r in reads:
            if r.w is not None:
                evs.append(r.w)
        for w in writes:
            for pe_ in ENGS:
                if pe_ != e and any(w is p for p in self.pending[pe_]):
                    raise RuntimeError("write to %s with pending unsignaled reads on %s" % (w.name, pe_))
            if w.w is not None:
                evs.append(w.w)
            for k, ev in w.r.items():
                if k == e and e != "pe":
                    continue
                evs.append(ev)
        waits = []
        wd = self.waited[e]
        for (sem, val, src) in evs:
            if src == e and (e == "pe" or not SAME_ENGINE_SYNC):
                continue
            if wd.get(sem, 0) >= val:
                continue
            wd[sem] = val
            waits.append((sem, val))
        return waits

    def op(self, e, fn, reads=(), writes=(), signal=True):
        waits = self._collect(e, reads, writes)
        ev = None
        if signal:
            self.cnt[e] += 1
            ev = (self.sem[e], self.cnt[e], e)
            for r in self.pending[e]:
                r.r[e] = ev
            self.pending[e] = []
            for r in reads:
                r.r[e] = ev
            for w in writes:
                w.w = ev
                w.r = {}
            self.ops[e].append((waits, fn, (self.sem[e], 1)))
        else:
            for r in reads:
                self.pending[e].append(r)
            self.ops[e].append((waits, fn, None))
        return ev

    def dma(self, q, fn, ds, reads=(), writes=(), inc=16):
        waits = self._collect(q, reads, writes)
        ds.cnt += inc
        ev = (ds.sem, ds.cnt, "dma")
        self.ops[q].append((waits, fn, (ds.sem, inc)))
        for r in reads:
            r.r[("dma", id(ds))] = ev
        for w in writes:
            w.w = ev
            w.r = {}
        self.dma_evs.append(ev)
        return ev

    def wait(self, e, evs):
        waits = []
        wd = self.waited[e]
        for (sem, val, src) in evs:
            if wd.get(sem, 0) >= val:
                continue
            wd[sem] = val
            waits.append((sem, val))
        if waits:
            self.ops[e].append((waits, None, None))

    def barrier(self, group=("pe", "act", "dve", "sp")):
        for e in group:
            assert not self.pending[e], "barrier with pending unsignaled ops on " + e
        evs = [(self.sem[o], self.cnt[o], o) for o in group if self.cnt[o] > 0]
        last = {}
        for ev in self.dma_evs:
            last[ev[0]] = ev
        self.dma_evs = list(last.values())
        for e in group:
            self.wait(e, [ev for ev in evs if ev[2] != e] + self.dma_evs)


    def check(self):
        val = {}
        ptr = {e: 0 for e in ENGS}
        total = sum(len(v) for v in self.ops.values())
        done = 0
        while done < total:
            prog = False
            for e in ENGS:
                lst = self.ops[e]
                while ptr[e] < len(lst):
                    waits, fn, inc = lst[ptr[e]]
                    if any(val.get(s, 0) < v for s, v in waits):
                        break
                    if inc is not None:
                        val[inc[0]] = val.get(inc[0], 0) + inc[1]
                    ptr[e] += 1
                    done += 1
                    prog = True
            if not prog:
                msg = []
                for e in ENGS:
                    if ptr[e] < len(self.ops[e]):
                        waits, fn, inc = self.ops[e][ptr[e]]
                        msg.append("%s@%d waits %s" % (e, ptr[e], [(s.name, v, val.get(s, 0)) for s, v in waits if val.get(s, 0) < v]))
                raise RuntimeError("DEADLOCK: " + "; ".join(msg))
        return True

    def emit(self):
        self.check()
        nc = self.nc
        ops = self.ops

        def run(eng, lst):
            for waits, fn, inc in lst:
                for sem, val in waits:
                    eng.wait_ge(sem, val)
                if fn is None:
                    continue
                ins = fn(eng)
                if inc is not None:
                    ins.then_inc(inc[0], inc[1])

        with nc.Block() as block:
            @block.sync
            def _(eng):
                run(eng, ops["sp"])

            @block.scalar
            def _(eng):
                run(eng, ops["act"])

            @block.vector
            def _(eng):
                run(eng, ops["dve"])

            @block.gpsimd
            def _(eng):
                run(eng, ops["pool"])

            @block.tensor
            def _(eng):
                run(eng, ops["pe"])


def gate_klist(c):
    n_lo = (128 * c) // BW
    n_hi = (128 * c + 127) // BW
    k_lo = (BW * n_lo) // 128
    k_hi = (BW * (n_hi + 1) - 1) // 128
    return list(range(k_lo, k_hi + 1))


CC = {}
_o = 0
for _name, _n in [("qg", 1), ("kg", 1), ("bga", 16), ("bgl", 16), ("cw", 88), ("cb", 22), ("br", 22),
                  ("bi", 22), ("lam", 22), ("invf", 1), ("sinks", 16), ("sel", 8), ("eps", 1), ("one", 1)]:
    CC[_name] = _o
    _o += _n
NCONST = _o


def host_consts(inp, core):
    c = np.zeros((128, NCONST), np.float32)
    j = core % 4
    b = core // 4
    c[:, CC["qg"]] = inp["q_norm_g"][0]
    c[:, CC["kg"]] = inp["k_norm_g"][0]
    c[:, CC["bga"]:CC["bga"] + 16] = inp["b_gates"][0][:D].reshape(16, 128).T
    c[:, CC["bgl"]:CC["bgl"] + 16] = inp["b_gates"][0][D:].reshape(16, 128).T
    for tap in range(4):
        c[:, CC["cw"] + tap * 22:CC["cw"] + (tap + 1) * 22] = inp["conv_w"][0][tap].reshape(22, 128).T
    c[:, CC["cb"]:CC["cb"] + 22] = inp["conv_b"][0].reshape(22, 128).T
    c[:, CC["br"]:CC["br"] + 22] = inp["b_rgate"][0].reshape(22, 128).T
    c[:, CC["bi"]:CC["bi"] + 22] = inp["b_igate"][0].reshape(22, 128).T
    c[:, CC["lam"]:CC["lam"] + 22] = inp["lru_lambda"][0].reshape(22, 128).T
    invf = (np.float32(THETA) ** (-np.arange(0, 32, 2, dtype=np.float32) / np.float32(32))).astype(np.float32)
    c[0:16, CC["invf"]] = invf
    c[16:32, CC["invf"]] = invf
    c[:, CC["sinks"]:CC["sinks"] + 16] = inp["sinks"][0][None, :]
    c[:, CC["eps"]] = EPS
    c[:, CC["one"]] = 1.0
    for k in range(8):
        c[:, CC["sel"] + k] = 1.0 if (k // 4 == b and k < core) else 0.0
    return c


def host_cmat(core):
    j = core % 4
    m = np.zeros((128, 2 * 128 + 3 * 512 + 32), np.float32)
    m[:, 0:128] = np.eye(128, dtype=np.float32)
    m[:, 128:256] = 1.0
    p = np.arange(128)[:, None]
    f = np.arange(128)[None, :]
    cur = (p <= f).astype(np.float32)
    prev = (p > f).astype(np.float32)
    prev0 = prev if j > 0 else np.zeros_like(prev)
    m[:, 256:768] = np.tile(cur, (1, 4))
    m[:, 768:1280] = np.tile(prev, (1, 4))
    m[:, 1280:1792] = np.tile(prev0, (1, 4))
    rot = np.zeros((128, 32), np.float32)
    for mm_ in range(16):
        rot[mm_ + 16, mm_] = -1.0
        rot[mm_, mm_ + 16] = 1.0
    m[:, 1792:1824] = rot
    return m


NCMAT = 1824


STAGES = ("A", "L", "ML", "AT", "MA", "C", "D")


def build_nc(debug=(), stop_after=None):
    nc = bass.Bass("TRN2", target_bir_lowering=False)
    specs = []
    W = {"off": 0, "i": 0, "ap": None}

    x_own = nc.dram_tensor("x_own", [T, D], F32, kind="ExternalInput").ap()
    x_halo = nc.dram_tensor("x_halo", [128, D], F32, kind="ExternalInput").ap()
    pos_d = nc.dram_tensor("pos", [32, TE], I32, kind="ExternalInput").ap()
    consts_d = nc.dram_tensor("consts", [128, NCONST], F32, kind="ExternalInput").ap()
    cmat_d = nc.dram_tensor("cmat", [128, NCMAT], F32, kind="ExternalInput").ap()
    g12_d = nc.dram_tensor("g12", [2, 128, D], F32, kind="ExternalInput").ap()
    out_d = nc.dram_tensor("out", [T, D], F32, kind="ExternalOutput").ap()
    r1_d = nc.dram_tensor("r1_spill", [RC, 128, T], BF16).ap()
    cc_in = nc.dram_tensor("cc_in", [128, 64], F32).ap()
    cc_out = nc.dram_tensor("cc_out", [8 * 128, 64], F32).ap()
    dbg_d = {}
    for name, shape in (("xn", [128, KC * TE]), ("rec", [128, RC * T]), ("mrg", [128, KC * T]),
                        ("att", [128, KC * T]), ("hh", [128, 8 * D]), ("misc", [128, 4096])):
        if name in debug:
            dbg_d[name] = nc.dram_tensor("dbg_" + name, shape, F32, kind="ExternalOutput").ap()

    with contextlib.ExitStack() as st:
        P = Prog(nc, st)
        big = nc.alloc_sbuf_tensor("big", [128, 65536], BF16)
        xn_t = nc.alloc_sbuf_tensor("xn", [128, KC * TE], BF16)
        ring = [nc.alloc_sbuf_tensor("ring%d" % i, [128, SLAB], BF16) for i in range(NRING)]
        ring_r = [Res("ring%d" % i) for i in range(NRING)]
        ring_s = [P.dsem() for _ in range(NRING)]
        cst = nc.alloc_sbuf_tensor("cst", [128, NCONST], F32)
        cmat = nc.alloc_sbuf_tensor("cmatb", [128, NCMAT], BF16)
        sml = nc.alloc_sbuf_tensor("sml", [128, 320], F32)
        psum = [nc.alloc_psum_tensor("ps%d" % i, [128, 512], F32) for i in range(8)]
        ps_r = [Res("ps%d" % i) for i in range(8)]
        cst_r = Res("cst")
        cmat_r = Res("cmat")
        ds_c = P.dsem()

        def view(off, shape, dt=BF16):
            esz = 2 if dt == BF16 else 4
            n = int(np.prod(shape[1:]))
            a = big[:, off // 2: off // 2 + n * esz // 2]
            if dt != BF16:
                a = a.bitcast(dt)
            if len(shape) == 3:
                a = a.rearrange("p (a b) -> p a b", a=shape[1])
            elif len(shape) == 4:
                a = a.rearrange("p (a b c) -> p a b c", a=shape[1], b=shape[2])
            return a

        xn = xn_t[:, :].rearrange("p (k t) -> p k t", k=KC)
        xn_r = [[Res("xn%d_%d" % (k, t)) for t in range(9)] for k in range(KC)]
        ident = cmat[:, 0:128]
        ones_b = cmat[:, 128:256]
        mask_cur = cmat[:, 256:768]
        mask_prev = cmat[:, 768:1280]
        mask_prev0 = cmat[:, 1280:1792]
        rotm = cmat[0:32, 1792:1824]

        def cc(name, i=0):
            return cst[:, CC[name] + i: CC[name] + i + 1]

        SS, RSTD, NSP8, NSP16, ESINK, CCS, HS, TMPA, TMPB, CHA, CHB = 0, 16, 32, 54, 76, 256, 136, 158, 180, 202, 224
        sml_r = Res("sml")

        def wload(n_el, spec):
            b = W["i"] % NRING
            W["i"] += 1
            off = W["off"]
            W["off"] += 128 * n_el
            specs.append((n_el, spec))

            def fn(e, b=b, off=off, n_el=n_el):
                src = W["ap"][off: off + 128 * n_el].rearrange("(p n) -> p n", p=128)
                return e.dma_start(out=ring[b][:, 0:n_el], in_=src, max_dma_last_dim=8192)
            P.dma("pool", fn, ring_s[b], writes=[ring_r[b]])
            return ring[b], ring_r[b]

        def mm(out, lhsT, rhs, start, stop, reads, wres, signal=None):
            if signal is None:
                signal = stop
            P.op("pe", lambda e: e.matmul(out, lhsT, rhs, start=start, stop=stop),
                 reads=reads, writes=[wres], signal=signal)

        def dbg_dump(name, src_ap, res_list, width):
            if name not in dbg_d:
                return
            P.barrier()
            soff = {"xn": 98304, "rec": 114688, "mrg": 8192, "att": 8192, "hh": 73728}[name]
            ds = [P.dsem(), P.dsem()]
            nchunk = (width + 2047) // 2048
            tv = [view(soff, [128, 2048], F32), view(soff + 8192, [128, 2048], F32)]
            tr = [Res("dbgt0"), Res("dbgt1")]
            for i in range(nchunk):
                lo = i * 2048
                hi = min(width, lo + 2048)
                P.op("dve", lambda e, i=i, lo=lo, hi=hi: e.tensor_copy(out=tv[i % 2][:, 0:hi - lo], in_=src_ap[:, lo:hi]),
                     reads=res_list, writes=[tr[i % 2]])
                P.dma("sp", lambda e, i=i, lo=lo, hi=hi: e.dma_start(out=dbg_d[name][:, lo:hi], in_=tv[i % 2][:, 0:hi - lo]),
                      ds[i % 2], reads=[tr[i % 2]])
            P.barrier()

        P.dma("sp", lambda e: e.dma_start(out=cst[:, :], in_=consts_d), ds_c, writes=[cst_r])
        P.dma("pool", lambda e: e.dma_start(out=cmat[:, :], in_=cmat_d, max_dma_last_dim=4096), P.dsem(), writes=[cmat_r])
        P.op("act", lambda e: e.activation(out=sml[:, TMPA:TMPA + RC], in_=cst[:, CC["lam"]:CC["lam"] + RC], func=AF.Exp, scale=-1.0),
             reads=[cst_r], writes=[sml_r])
        P.op("act", lambda e: e.activation(out=sml[:, TMPB:TMPB + RC], in_=sml[:, TMPA:TMPA + RC], func=AF.Ln, bias=1.0),
             reads=[sml_r], writes=[sml_r])
        P.op("dve", lambda e: e.tensor_scalar(out=sml[:, NSP8:NSP8 + RC], in0=sml[:, TMPB:TMPB + RC], scalar1=-8.0, scalar2=None, op0=ALU.mult),
             reads=[sml_r], writes=[sml_r])
        P.op("dve", lambda e: e.tensor_scalar(out=sml[:, NSP16:NSP16 + RC], in0=sml[:, TMPB:TMPB + RC], scalar1=-16.0, scalar2=None, op0=ALU.mult),
             reads=[sml_r], writes=[sml_r])
        P.op("act", lambda e: e.activation(out=sml[:, ESINK:ESINK + 16], in_=cst[:, CC["sinks"]:CC["sinks"] + 16], func=AF.Exp),
             reads=[cst_r], writes=[sml_r])

        def norm_transpose(src_tile_fn, ntiles, gsel, dst, dst_r, dst_col0, a_off):
            xsb = [view(a_off, [128, D]), view(a_off + 4096, [128, D])]
            xsb_r = [Res("xsb0"), Res("xsb1")]
            sq = view(a_off + 8192, [128, D])
            sq_r = Res("sq")
            gb = view(a_off + 12288, [128, D], F32)
            gb_r = Res("gb")
            P.dma("sp", lambda e: e.dma_start(out=gb, in_=g12_d[gsel]), P.dsem(), writes=[gb_r])
            for i in range(ntiles):
                src, src_r = src_tile_fn(i)
                P.op("act", lambda e, src=src, i=i: e.activation(out=sq, in_=src, func=AF.Square, accum_out=sml[:, SS + i:SS + i + 1]),
                     reads=[src_r], writes=[sq_r, sml_r])
                P.op("dve", lambda e, i=i: e.tensor_scalar(out=sml[:, TMPA + i:TMPA + i + 1], in0=sml[:, SS + i:SS + i + 1],
                                                         scalar1=1.0 / D, scalar2=EPS, op0=ALU.mult, op1=ALU.add),
                     reads=[sml_r], writes=[sml_r])
                P.op("act", lambda e, i=i: e.activation(out=sml[:, TMPB + i:TMPB + i + 1], in_=sml[:, TMPA + i:TMPA + i + 1], func=AF.Sqrt),
                     reads=[sml_r], writes=[sml_r])
                P.op("dve", lambda e, i=i: e.reciprocal(out=sml[:, RSTD + i:RSTD + i + 1], in_=sml[:, TMPB + i:TMPB + i + 1]),
                     reads=[sml_r], writes=[sml_r])
                b = i % 2
                P.op("dve", lambda e, src=src, i=i, b=b: e.scalar_tensor_tensor(out=xsb[b], in0=src, scalar=sml[:, RSTD + i:RSTD + i + 1],
                                                                              in1=gb, op0=ALU.mult, op1=ALU.mult),
                     reads=[src_r, sml_r, gb_r], writes=[xsb_r[b]])
                for hh in range(2):
                    bank = (2 * i + hh) % 4
                    pb = psum[bank][:, :].bitcast(BF16).rearrange("p (a b) -> p a b", a=8)
                    for k8 in range(8):
                        kc = hh * 8 + k8
                        P.op("pe", lambda e, pb=pb, k8=k8, kc=kc, b=b: e.transpose(pb[:, k8, :], xsb[b][:, kc * 128:(kc + 1) * 128], ident),
                             reads=[xsb_r[b], cmat_r], writes=[ps_r[bank]], signal=(k8 == 7))
                    c0 = dst_col0 + i * 128
                    eng = "act" if hh == 0 else "dve"
                    if eng == "act":
                        P.op("act", lambda e, pb=pb, hh=hh, c0=c0: e.activation(out=dst[:, hh * 8:(hh + 1) * 8, c0:c0 + 128], in_=pb, func=AF.Copy),
                             reads=[ps_r[bank]], writes=[dst_r[k][i] for k in range(hh * 8, hh * 8 + 8)])
                    else:
                        P.op("dve", lambda e, pb=pb, hh=hh, c0=c0: e.tensor_copy(out=dst[:, hh * 8:(hh + 1) * 8, c0:c0 + 128], in_=pb),
                             reads=[ps_r[bank]], writes=[dst_r[k][i] for k in range(hh * 8, hh * 8 + 8)])

        xt = [view(0, [128, D], F32), view(8192, [128, D], F32)]
        xt_r = [Res("xt0"), Res("xt1")]
        xt_s = [P.dsem(), P.dsem()]

        def a_src(i):
            b = i % 2
            if i == 0:
                P.dma("sp", lambda e: e.dma_start(out=xt[0], in_=x_halo), xt_s[0], writes=[xt_r[0]])
            else:
                P.dma("sp", lambda e, i=i, b=b: e.dma_start(out=xt[b], in_=x_own[(i - 1) * 128:i * 128, :]), xt_s[b], writes=[xt_r[b]])
            return xt[b], xt_r[b]
        norm_transpose(a_src, 9, 0, xn, xn_r, 0, 16384)
        dbg_dump("xn", xn_t[:, :], [r for k in range(KC) for r in xn_r[k]], KC * TE)
        if stop_after == "A":
            return finish(nc, P, W, specs, out_d, None)

        def xn_reads(tok0, n):
            t0, t1 = tok0 // 128, (tok0 + n - 1) // 128
            return lambda kc: [xn_r[kc][t] for t in range(t0, t1 + 1)]

        P.barrier()
        R0 = view(0, [128, RC, T])
        R0_r = [Res("R0_%d" % c) for c in range(RC)]
        UC = view(45056, [128, 8, T])
        UC_r = [Res("UC%d" % i) for i in range(8)]
        GT = view(61440, [128, 4, T])
        GT_r = [Res("GT%d" % i) for i in range(4)]
        UP = [view(69632, [128, 1040], F32), view(69632 + 4160, [128, 1040], F32)]
        UP_r = [Res("UP0"), Res("UP1")]
        f32t = {}
        for i, nm in enumerate(("acc", "rt", "it", "at", "st", "bt", "ht", "cum", "zeros")):
            f32t[nm] = (view(77952 + 4096 * i, [128, T], F32), Res(nm))
        R1T = [view(114816, [128, T]), view(114816 + 2048, [128, T])]
        R1T_r = [Res("R1T0"), Res("R1T1")]
        r1_s = [P.dsem(), P.dsem()]
        ccs = sml[:, CCS:CCS + 64]
        ccs_r = Res("ccs")
        P.op("dve", lambda e: e.memset(ccs, 0.0), writes=[ccs_r])
        P.op("dve", lambda e: e.memset(f32t["zeros"][0], 0.0), writes=[f32t["zeros"][1]])

        def w_in_cols(lo, n):
            return lambda inp: inp["w_in"][0][:, lo:lo + n]

        def spec_k16(colfn_list):
            def fn(inp):
                parts = []
                for wname, lo in colfn_list:
                    wmat = inp[wname][0]
                    parts.append(wmat[:, lo:lo + 128].reshape(KC, 128, 128).transpose(1, 0, 2).reshape(128, KC * 128))
                return np.concatenate(parts, axis=1)
            return fn

        def gate_spec(c):
            kl = gate_klist(c)

            def fn(inp):
                parts = []
                for wname in ("w_rgate", "w_igate"):
                    full = np.zeros((len(kl) * 128, 128), np.float32)
                    wg = inp[wname][0]
                    for n in range(NB_RNN):
                        r0, r1 = n * BW, (n + 1) * BW
                        ro0, ro1 = max(r0, kl[0] * 128), min(r1, (kl[-1] + 1) * 128)
                        co0, co1 = max(r0, c * 128), min(r1, (c + 1) * 128)
                        if ro0 < ro1 and co0 < co1:
                            full[ro0 - kl[0] * 128:ro1 - kl[0] * 128, co0 - c * 128:co1 - c * 128] = wg[n][ro0 - r0:ro1 - r0, co0 - r0:co1 - r0]
                    parts.append(full.reshape(len(kl), 128, 128).transpose(1, 0, 2).reshape(128, len(kl) * 128))
                return np.concatenate(parts, axis=1)
            return fn

        def lru_gates(c):
            kl = gate_klist(c)
            nk = len(kl)
            slab, slab_r = wload(2 * nk * 128, gate_spec(c))
            sv = slab[:, 0:2 * nk * 128].rearrange("p (g k m) -> p g k m", g=2, k=nk)
            for gi in range(2):
                for hf in range(2):
                    bank = 4 + gi * 2 + hf
                    for ki, k in enumerate(kl):
                        mm(psum[bank][:, :], sv[:, gi, ki, :], UC[:, k % 8, hf * 512:(hf + 1) * 512],
                           ki == 0, ki == nk - 1, [slab_r, UC_r[k % 8]], ps_r[bank])
            rt, rt_r = f32t["rt"]
            it, it_r = f32t["it"]
            at, at_r = f32t["at"]
            s_t, st_r = f32t["st"]
            bt, bt_r = f32t["bt"]
            ht, ht_r = f32t["ht"]
            cum, cum_r = f32t["cum"]
            zt, zt_r = f32t["zeros"]
            for hf in range(2):
                sl = slice(hf * 512, (hf + 1) * 512)
                P.op("act", lambda e, hf=hf, sl=sl: e.activation(out=rt[:, sl], in_=psum[4 + hf][:, :], func=AF.Sigmoid, bias=cc("br", c)),
                     reads=[ps_r[4 + hf], cst_r], writes=[rt_r])
                P.op("act", lambda e, hf=hf, sl=sl: e.activation(out=it[:, sl], in_=psum[6 + hf][:, :], func=AF.Sigmoid, bias=cc("bi", c)),
                     reads=[ps_r[6 + hf], cst_r], writes=[it_r])
            P.op("act", lambda e: e.activation(out=at, in_=rt, func=AF.Exp, scale=sml[:, NSP8 + c:NSP8 + c + 1]),
                 reads=[rt_r, sml_r], writes=[at_r])
            P.op("act", lambda e: e.activation(out=s_t, in_=rt, func=AF.Exp, scale=sml[:, NSP16 + c:NSP16 + c + 1]),
                 reads=[rt_r, sml_r], writes=[st_r])
            P.op("act", lambda e: e.activation(out=s_t, in_=s_t, func=AF.Sqrt, scale=-1.0, bias=1.0),
                 reads=[st_r], writes=[st_r])
            P.op("pool", lambda e: e.tensor_tensor(out=bt, in0=it, in1=UC[:, c % 8, :], op=ALU.mult),
                 reads=[it_r, UC_r[c % 8]], writes=[bt_r])
            P.op("pool", lambda e: e.tensor_tensor(out=bt, in0=bt, in1=s_t, op=ALU.mult),
                 reads=[bt_r, st_r], writes=[bt_r])

        def lru_gates_p2(c):
            at, at_r = f32t["at"]
            bt, bt_r = f32t["bt"]
            ht, ht_r = f32t["ht"]
            cum, cum_r = f32t["cum"]
            zt, zt_r = f32t["zeros"]
            P.op("dve", lambda e: e.tensor_tensor_scan(out=ht, data0=at, data1=bt, initial=0.0, op0=ALU.mult, op1=ALU.add),
                 reads=[at_r, bt_r], writes=[ht_r])
            P.op("dve", lambda e: e.tensor_tensor_scan(out=cum, data0=at, data1=zt, initial=1.0, op0=ALU.mult, op1=ALU.add),
                 reads=[at_r, zt_r], writes=[cum_r])
            P.op("dve", lambda e: e.tensor_tensor(out=R0[:, c, :], in0=ht, in1=GT[:, c % 4, :], op=ALU.mult),
                 reads=[ht_r, GT_r[c % 4]], writes=[R0_r[c]])
            b = c % 2
            P.op("dve", lambda e, b=b: e.tensor_tensor(out=R1T[b], in0=cum, in1=GT[:, c % 4, :], op=ALU.mult),
                 reads=[cum_r, GT_r[c % 4]], writes=[R1T_r[b]])
            P.dma("sp", lambda e, b=b: e.dma_start(out=r1_d[c], in_=R1T[b]), r1_s[b], reads=[R1T_r[b]])
            P.op("dve", lambda e: e.tensor_copy(out=sml[:, CCS + c:CCS + c + 1], in_=cum[:, T - 1:T]),
                 reads=[cum_r], writes=[ccs_r])
            P.op("dve", lambda e: e.tensor_copy(out=sml[:, CCS + RC + c:CCS + RC + c + 1], in_=ht[:, T - 1:T]),
                 reads=[ht_r], writes=[ccs_r])

        ug = {}

        def ug_load(c):
            ug[c] = wload(4096, spec_k16([("w_in", SPL[2] + c * 128), ("w_in", SPL[3] + c * 128)]))
        ug_load(0)

        def lru_chunk(c):
            if c + 1 < RC:
                ug_load(c + 1)
            slab, slab_r = ug[c]
            sv = slab[:, :].rearrange("p (g k m) -> p g k m", g=2, k=KC)
            for hf in range(2):
                rd = xn_reads(128 + hf * 512, 512)
                for k in range(KC):
                    mm(psum[hf][:, :], sv[:, 0, k, :], xn[:, k, 128 + hf * 512:128 + (hf + 1) * 512],
                       k == 0, k == KC - 1, [slab_r] + rd(k), ps_r[hf])
            for k in range(KC):
                mm(psum[2][:, 0:8], sv[:, 0, k, :], xn[:, k, 120:128], k == 0, k == KC - 1, [slab_r, xn_r[k][0]], ps_r[2])
            up, up_r = UP[c % 2], UP_r[c % 2]
            P.op("act", lambda e, up=up: e.activation(out=up[:, 0:8], in_=psum[2][:, 0:8], func=AF.Copy), reads=[ps_r[2]], writes=[up_r])
            P.op("dve", lambda e, up=up: e.tensor_copy(out=up[:, 8:520], in_=psum[0][:, :]), reads=[ps_r[0]], writes=[up_r])
            P.op("dve", lambda e, up=up: e.tensor_copy(out=up[:, 520:1032], in_=psum[1][:, :]), reads=[ps_r[1]], writes=[up_r])
            for hf in range(2):
                rd = xn_reads(128 + hf * 512, 512)
                gb_ = 3
                for k in range(KC):
                    mm(psum[gb_][:, :], sv[:, 1, k, :], xn[:, k, 128 + hf * 512:128 + (hf + 1) * 512],
                       k == 0, k == KC - 1, [slab_r] + rd(k), ps_r[gb_])
                P.op("act", lambda e, hf=hf, gb_=gb_: e.activation(out=GT[:, c % 4, hf * 512:(hf + 1) * 512], in_=psum[gb_][:, :], func=AF.Gelu_apprx_tanh),
                     reads=[ps_r[gb_]], writes=[GT_r[c % 4]])
            acc, acc_r = f32t["acc"]
            P.op("dve", lambda e, up=up: e.tensor_scalar(out=acc, in0=up[:, 5:5 + T], scalar1=cc("cw", 0 * 22 + c), scalar2=cc("cb", c),
                                                        op0=ALU.mult, op1=ALU.add),
                 reads=[up_r, cst_r], writes=[acc_r])
            for tap in (1, 2):
                P.op("dve", lambda e, up=up, tap=tap: e.scalar_tensor_tensor(out=acc, in0=up[:, 5 + tap:5 + tap + T], scalar=cc("cw", tap * 22 + c),
                                                                             in1=acc, op0=ALU.mult, op1=ALU.add),
                     reads=[up_r, cst_r, acc_r], writes=[acc_r])
            P.op("dve", lambda e, up=up: e.scalar_tensor_tensor(out=UC[:, c % 8, :], in0=up[:, 8:8 + T], scalar=cc("cw", 3 * 22 + c),
                                                               in1=acc, op0=ALU.mult, op1=ALU.add),
                 reads=[up_r, cst_r, acc_r], writes=[UC_r[c % 8]])
        gates_done = 0
        for c in range(RC):
            first = None
            if gates_done < RC and gate_klist(gates_done)[-1] <= c - 1:
                first = gates_done
                gates_done += 1
                lru_gates(first)
            lru_chunk(c)
            if first is not None:
                lru_gates_p2(first)
            while gates_done < RC and gate_klist(gates_done)[-1] <= c - 1:
                lru_gates(gates_done)
                lru_gates_p2(gates_done)
                gates_done += 1
        while gates_done < RC:
            lru_gates(gates_done)
            lru_gates_p2(gates_done)
            gates_done += 1

        assert gates_done == RC

        if stop_after == "L0":
            return finish(nc, P, W, specs, out_d, None)
        ccin_r, ccout_r = Res("ccin"), Res("ccout")
        P.dma("pool", lambda e: e.dma_start(out=cc_in, in_=ccs), P.dsem(), reads=[ccs_r], writes=[ccin_r])
        ds_cc = P.dsem()
        waits = P._collect("pool", [ccin_r], [ccout_r])
        ds_cc.cnt += 1
        ev = (ds_cc.sem, ds_cc.cnt, "dma")
        import os as _os
        if _os.environ.get("KDBG_NOCC"):
            P.ops["pool"].append((waits, lambda e: e.dma_start(out=cc_out[0:128, :], in_=cc_in), (ds_cc.sem, 1)))
            P.ops["pool"].append(([(ds_cc.sem, 1)], None, None))
            ds_cc.cnt = 16
            ev = (ds_cc.sem, 16, "dma")
            P.ops["pool"][-2] = (waits, lambda e: e.dma_start(out=cc_out[0:128, :], in_=cc_in), (ds_cc.sem, 16))
            P.ops["pool"].pop()
        else:
            P.ops["pool"].append((waits, lambda e: e.collective_compute("AllGather", ALU.bypass, replica_groups=[list(range(8))],
                                                                        ins=[cc_in], outs=[cc_out]), (ds_cc.sem, 1)))
        ccout_r.w = ev
        gth_t = nc.alloc_sbuf_tensor("gth", [128, 8 * 64], F32)
        gth = gth_t[:, :].rearrange("p (r c) -> p r c", r=8)
        gth_r = Res("gth")
        P.barrier()
        MG = view(98304, [128, KC, T])
        MG_r = [Res("MG%d" % m) for m in range(KC)]
        itg = 0
        for mp in range(KC // 2):
            gslab, gslab_r = wload(4096, spec_k16([("w_in", SPL[5] + (2 * mp) * 128), ("w_in", SPL[5] + (2 * mp + 1) * 128)]))
            gv = gslab[:, :].rearrange("p (g k m) -> p g k m", g=2, k=KC)
            for mi in range(2):
                m = 2 * mp + mi
                for hf in range(2):
                    bg = itg % 4
                    itg += 1
                    rd = xn_reads(128 + hf * 512, 512)
                    for k in range(KC):
                        mm(psum[bg][:, :], gv[:, mi, k, :], xn[:, k, 128 + hf * 512:128 + (hf + 1) * 512],
                           k == 0, k == KC - 1, [gslab_r] + rd(k), ps_r[bg])
                    P.op("act", lambda e, bg=bg, m=m, hf=hf: e.activation(out=MG[:, m, hf * 512:(hf + 1) * 512], in_=psum[bg][:, :], func=AF.Sigmoid,
                                                                          bias=cc("bgl", m)),
                         reads=[ps_r[bg], cst_r], writes=[MG_r[m]])
        P.dma("sp", lambda e: e.dma_start(out=gth, in_=cc_out.rearrange("(r p) c -> p r c", p=128)), P.dsem(), reads=[ccout_r], writes=[gth_r])
        hs = sml[:, HS:HS + RC]
        P.op("dve", lambda e: e.memset(hs, 0.0), writes=[sml_r])
        for k in range(8):
            selk = cc("sel", k)
            P.op("dve", lambda e, k=k, selk=selk: e.tensor_scalar(out=sml[:, CHA:CHA + RC], in0=gth[:, k, 0:RC], scalar1=-1.0, scalar2=selk,
                                                                  op0=ALU.add, op1=ALU.mult),
                 reads=[gth_r, cst_r], writes=[sml_r])
            P.op("dve", lambda e: e.tensor_scalar(out=sml[:, CHA:CHA + RC], in0=sml[:, CHA:CHA + RC], scalar1=1.0, scalar2=None, op0=ALU.add),
                 reads=[sml_r], writes=[sml_r])
            P.op("dve", lambda e, k=k, selk=selk: e.tensor_scalar(out=sml[:, CHB:CHB + RC], in0=gth[:, k, RC:2 * RC], scalar1=selk, scalar2=None,
                                                                  op0=ALU.mult),
                 reads=[gth_r, cst_r], writes=[sml_r])
            P.op("dve", lambda e: e.tensor_tensor(out=hs, in0=hs, in1=sml[:, CHA:CHA + RC], op=ALU.mult), reads=[sml_r], writes=[sml_r])
            P.op("dve", lambda e: e.tensor_tensor(out=hs, in0=hs, in1=sml[:, CHB:CHB + RC], op=ALU.add), reads=[sml_r], writes=[sml_r])
        r1b = [view(45056, [128, T]), view(45056 + 2048, [128, T])]
        r1b_r = [Res("r1b0"), Res("r1b1")]
        r1b_s = [P.dsem(), P.dsem()]
        for c in range(RC):
            b = c % 2
            P.dma("sp", lambda e, c=c, b=b: e.dma_start(out=r1b[b], in_=r1_d[c]), r1b_s[b], writes=[r1b_r[b]])
            P.op("dve", lambda e, c=c, b=b: e.scalar_tensor_tensor(out=R0[:, c, :], in0=r1b[b], scalar=sml[:, HS + c:HS + c + 1], in1=R0[:, c, :],
                                                                  op0=ALU.mult, op1=ALU.add),
                 reads=[r1b_r[b], sml_r, R0_r[c]], writes=[R0_r[c]])
        dbg_dump("rec", R0.rearrange("p a b -> p (a b)"), R0_r, RC * T)
        if stop_after == "L":
            return finish(nc, P, W, specs, out_d, None)

        def lp_spec(m):
            return lambda inp: inp["w_lru_proj"][0][:, m * 128:(m + 1) * 128].reshape(RC, 128, 128).transpose(1, 0, 2).reshape(128, RC * 128)
        itp = 0
        for m in range(KC):
            pslab, pslab_r = wload(RC * 128, lp_spec(m))
            pv = pslab[:, 0:RC * 128].rearrange("p (k m) -> p k m", k=RC)
            for hf in range(2):
                ba = itp % 4
                itp += 1
                for k in range(RC):
                    mm(psum[ba][:, :], pv[:, k, :], R0[:, k, hf * 512:(hf + 1) * 512], k == 0, k == RC - 1, [pslab_r, R0_r[k]], ps_r[ba])
                dst = MG[:, m, hf * 512:(hf + 1) * 512]
                P.op("dve", lambda e, ba=ba, dst=dst: e.tensor_tensor(out=dst, in0=psum[ba][:, :], in1=dst, op=ALU.mult),
                     reads=[ps_r[ba], MG_r[m]], writes=[MG_r[m]])
        if stop_after == "ML":
            dbg_dump("mrg", MG.rearrange("p a b -> p (a b)"), MG_r, KC * T)
            return finish(nc, P, W, specs, out_d, None)

        def proj_gate_stage(K, act_ap_fn, act_res_fn, pspec, gate_col0, bias_name, first, sg, sg_r, tA, tA_r):
            it_ = 0
            for mp in range(KC // 2):
                gslab, gslab_r = wload(4096, spec_k16([("w_in", gate_col0 + (2 * mp) * 128), ("w_in", gate_col0 + (2 * mp + 1) * 128)]))
                gv = gslab[:, :].rearrange("p (g k m) -> p g k m", g=2, k=KC)
                for mi in range(2):
                    m = 2 * mp + mi
                    if mi == 0:
                        pslab, pslab_r = wload(4096, pspec(mp))
                        pv2 = pslab[:, :].rearrange("p (g k m) -> p g k m", g=2, k=KC)
                    pv = pv2[:, mi]
                    for hf in range(2):
                        ba = (it_ % 2) * 2
                        bg = ba + 1
                        for k in range(K):
                            mm(psum[ba][:, :], pv[:, k, :], act_ap_fn(k, hf), k == 0, k == K - 1, [pslab_r, act_res_fn(k)], ps_r[ba])
                        rd = xn_reads(128 + hf * 512, 512)
                        for k in range(KC):
                            mm(psum[bg][:, :], gv[:, mi, k, :], xn[:, k, 128 + hf * 512:128 + (hf + 1) * 512],
                               k == 0, k == KC - 1, [gslab_r] + rd(k), ps_r[bg])
                        b = it_ % 2
                        P.op("act", lambda e, b=b, bg=bg, m=m: e.activation(out=sg[b], in_=psum[bg][:, :], func=AF.Sigmoid, bias=cc(bias_name, m)),
                             reads=[ps_r[bg], cst_r], writes=[sg_r[b]])
                        dst = MG[:, m, hf * 512:(hf + 1) * 512]
                        P.op("dve", lambda e, b=b, ba=ba: e.tensor_tensor(out=tA[b], in0=psum[ba][:, :], in1=sg[b], op=ALU.mult),
                             reads=[ps_r[ba], sg_r[b]], writes=[tA_r[b]])
                        P.op("dve", lambda e, b=b, dst=dst: e.tensor_tensor(out=dst, in0=tA[b], in1=dst, op=ALU.add),
                             reads=[tA_r[b], MG_r[m]], writes=[MG_r[m]])
                        it_ += 1

        P.barrier()
        AT = view(65536, [128, NQ, T])
        AT_r = [Res("AT%d" % h) for h in range(NQ)]
        QT = view(0, [128, 8, 4, 128])
        QT_r = [Res("QT%d" % qb) for qb in range(8)]
        KT = view(8192, [128, TE])
        KT_r = [Res("KT%d" % t) for t in range(9)]
        VT = view(10496, [128, 9, 512])
        VT_r = [Res("VT%d" % t) for t in range(9)]
        Ctab = view(19712, [128, TE], F32)
        Stab = view(24320, [128, TE], F32)
        tab_r = Res("tab")
        sqb = [view(28928, [128, 512]), view(28928 + 1024, [128, 512])]
        sqb_r = [Res("sqb0"), Res("sqb1")]
        rsd = [view(30976, [128, 512], F32), view(30976 + 2048, [128, 512], F32)]
        rsd_r = [Res("rsd0"), Res("rsd1")]
        qnf = [view(35072, [128, 512], F32), view(35072 + 2048, [128, 512], F32)]
        qnf_r = [Res("qnf0"), Res("qnf1")]
        qnb = [view(39168, [128, 512]), view(39168 + 1024, [128, 512])]
        qnb_r = [Res("qnb0"), Res("qnb1")]
        rt1 = [view(41216, [128, 512], F32), view(41216 + 2048, [128, 512], F32)]
        rt1_r = [Res("rt1_0"), Res("rt1_1")]
        rt2 = [view(45312, [128, 512], F32), view(45312 + 2048, [128, 512], F32)]
        rt2_r = [Res("rt2_0"), Res("rt2_1")]
        EB = [view(49408 + 1024 * i, [128, 512]) for i in range(4)]
        EB_r = [Res("EB%d" % i) for i in range(4)]
        dns = [view(53504, [128, 512], F32), view(53504 + 2048, [128, 512], F32)]
        dns_r = [Res("dns0"), Res("dns1")]
        tpi = view(65536 + 16384, [128, TE], I32)
        tA_ = view(65536 + 16384 + 4608, [128, TE], F32)
        tB_ = view(65536 + 16384 + 9216, [128, TE], F32)
        tmp_r = Res("ropetmp")
        pi_ = tpi[0:32, :]
        A_ = tA_[0:32, :]
        B_ = tB_[0:32, :]
        Bi_ = tB_[0:32, :].bitcast(I32)
        pf_ = tpi[0:32, :].bitcast(F32)
        P.dma("sp", lambda e: e.dma_start(out=pi_, in_=pos_d), P.dsem(), writes=[tmp_r])
        P.op("dve", lambda e: e.tensor_copy(out=A_, in_=pi_), reads=[tmp_r], writes=[tmp_r])
        P.op("dve", lambda e: e.tensor_scalar(out=A_, in0=A_, scalar1=cst[0:32, CC["invf"]:CC["invf"] + 1], scalar2=1.0 / (2 * math.pi),
                                              op0=ALU.mult, op1=ALU.mult), reads=[tmp_r, cst_r], writes=[tmp_r])
        for which, tab in ((0, Stab), (1, Ctab)):
            if which == 1:
                P.op("dve", lambda e: e.tensor_scalar(out=A_, in0=A_, scalar1=0.25, scalar2=None, op0=ALU.add), reads=[tmp_r], writes=[tmp_r])
            P.op("dve", lambda e: e.tensor_copy(out=Bi_, in_=A_), reads=[tmp_r], writes=[tmp_r])
            P.op("dve", lambda e: e.tensor_copy(out=pf_, in_=Bi_), reads=[tmp_r], writes=[tmp_r])
            P.op("dve", lambda e: e.tensor_tensor(out=pf_, in0=A_, in1=pf_, op=ALU.subtract), reads=[tmp_r], writes=[tmp_r])
            P.op("dve", lambda e: e.tensor_scalar(out=B_, in0=pf_, scalar1=0.5, scalar2=None, op0=ALU.is_gt), reads=[tmp_r], writes=[tmp_r])
            P.op("dve", lambda e: e.tensor_tensor(out=pf_, in0=pf_, in1=B_, op=ALU.subtract), reads=[tmp_r], writes=[tmp_r])
            P.op("dve", lambda e: e.tensor_scalar(out=B_, in0=pf_, scalar1=-0.5, scalar2=None, op0=ALU.is_lt), reads=[tmp_r], writes=[tmp_r])
            P.op("dve", lambda e: e.tensor_tensor(out=pf_, in0=pf_, in1=B_, op=ALU.add), reads=[tmp_r], writes=[tmp_r])
            P.op("act", lambda e, tab=tab: e.activation(out=tab[0:32, :], in_=pf_, func=AF.Sin, scale=2 * math.pi), reads=[tmp_r], writes=[tab_r])

        vs = []
        for half in range(2):
            def vspec(inp, half=half):
                wv = inp["w_in"][0][half * 1024:(half + 1) * 1024, SPL[1]:SPL[2]]
                return wv.reshape(8, 128, 512).transpose(1, 0, 2).reshape(128, 4096)
            vs.append(wload(4096, vspec))
        for t in range(9):
            bank = t % 2
            for k in range(KC):
                slab, slab_r = vs[k // 8]
                mm(psum[bank][:, :], xn[:, k, t * 128:(t + 1) * 128], slab[:, (k % 8) * 512:(k % 8 + 1) * 512],
                   k == 0, k == KC - 1, [slab_r, xn_r[k][t]], ps_r[bank])
            P.op("act", lambda e, t=t, bank=bank: e.activation(out=VT[:, t, :], in_=psum[bank][:, :], func=AF.Copy),
                 reads=[ps_r[bank]], writes=[VT_r[t]])

        cnt = {"i": 0}
        pbc = {"i": 0}

        def qk_norm_rope(pbank, gname, col0, ncols, dst_ap, dst_res):
            i = cnt["i"] % 2
            cnt["i"] += 1
            pq = psum[pbank][:, 0:ncols]
            P.op("act", lambda e: e.activation(out=sqb[i][:, 0:ncols], in_=pq, func=AF.Square), reads=[ps_r[pbank]], writes=[sqb_r[i]])
            mm(psum[6][:, 0:ncols], ones_b, sqb[i][:, 0:ncols], True, True, [cmat_r, sqb_r[i]], ps_r[6])
            P.op("act", lambda e: e.activation(out=rsd[i][:, 0:ncols], in_=psum[6][:, 0:ncols], func=AF.Ln, scale=1.0 / HD, bias=cst[:, CC["eps"]:CC["eps"] + 1]),
                 reads=[ps_r[6], cst_r], writes=[rsd_r[i]])
            P.op("act", lambda e: e.activation(out=rsd[i][:, 0:ncols], in_=rsd[i][:, 0:ncols], func=AF.Exp, scale=-0.5), reads=[rsd_r[i]], writes=[rsd_r[i]])
            P.op("dve", lambda e: e.scalar_tensor_tensor(out=qnf[i][:, 0:ncols], in0=pq, scalar=cc(gname), in1=rsd[i][:, 0:ncols],
                                                         op0=ALU.mult, op1=ALU.mult),
                 reads=[ps_r[pbank], cst_r, rsd_r[i]], writes=[qnf_r[i]])
            P.op("act", lambda e: e.activation(out=qnb[i][:, 0:ncols], in_=qnf[i][:, 0:ncols], func=AF.Copy), reads=[qnf_r[i]], writes=[qnb_r[i]])
            mm(psum[7][0:32, 0:ncols], rotm, qnb[i][0:32, 0:ncols], True, True, [cmat_r, qnb_r[i]], ps_r[7])
            P.op("dve", lambda e: e.tensor_tensor(out=rt1[i][0:32, 0:ncols], in0=psum[7][0:32, 0:ncols], in1=Stab[0:32, col0:col0 + ncols], op=ALU.mult),
                 reads=[ps_r[7], tab_r], writes=[rt1_r[i]])
            P.op("dve", lambda e: e.tensor_tensor(out=rt2[i][0:32, 0:ncols], in0=qnf[i][0:32, 0:ncols], in1=Ctab[0:32, col0:col0 + ncols], op=ALU.mult),
                 reads=[qnf_r[i], tab_r], writes=[rt2_r[i]])
            P.op("dve", lambda e: e.tensor_tensor(out=qnb[i][0:32, 0:ncols], in0=rt1[i][0:32, 0:ncols], in1=rt2[i][0:32, 0:ncols], op=ALU.add),
                 reads=[rt1_r[i], rt2_r[i], qnb_r[i]], writes=[qnb_r[i]])
            srcv = qnb[i][:, 0:ncols]
            if len(dst_ap.shape) == 3:
                srcv = srcv.rearrange("p (a b) -> p a b", a=dst_ap.shape[1])
            P.op("act", lambda e: e.activation(out=dst_ap, in_=srcv, func=AF.Copy), reads=[qnb_r[i]], writes=dst_res)

        scale = 1.0 / math.sqrt(HD)
        for g in range(NKV):
            slab, slab_r = wload(4096, spec_k16([("w_in", SPL[0] + g * 128), ("w_in", g * 4 * 128)]))
            sv = slab[:, :].rearrange("p (g k m) -> p g k m", g=2, k=KC)
            for (c0, n) in ((0, 128), (128, 512), (640, 512)):
                rd = xn_reads(c0, n)
                pbk = 4 + pbc["i"] % 2
                pbc["i"] += 1
                for k in range(KC):
                    mm(psum[pbk][:, 0:n], sv[:, 0, k, :], xn[:, k, c0:c0 + n], k == 0, k == KC - 1, [slab_r] + rd(k), ps_r[pbk])
                qk_norm_rope(pbk, "kg", c0, n, KT[:, c0:c0 + n], [KT_r[t] for t in range(c0 // 128, (c0 + n) // 128)])
            for hh in range(4):
                h = g * 4 + hh
                if hh == 0:
                    qv = sv[:, 1]
                    q_r = slab_r
                elif hh in (1, 3):
                    pass
                if hh == 1:
                    slab2, slab2_r = wload(4096, spec_k16([("w_in", (h) * 128), ("w_in", (h + 1) * 128)]))
                    sv2 = slab2[:, :].rearrange("p (g k m) -> p g k m", g=2, k=KC)
                    qv, q_r = sv2[:, 0], slab2_r
                elif hh == 2:
                    qv, q_r = sv2[:, 1], slab2_r
                elif hh == 3:
                    slab3, slab3_r = wload(2048, spec_k16([("w_in", h * 128)]))
                    qv = slab3[:, 0:2048].rearrange("p (k m) -> p k m", k=KC)
                    q_r = slab3_r
                for hf in range(2):
                    rd = xn_reads(128 + hf * 512, 512)
                    pbk = 4 + pbc["i"] % 2
                    pbc["i"] += 1
                    for k in range(KC):
                        mm(psum[pbk][:, :], qv[:, k, :], xn[:, k, 128 + hf * 512:128 + (hf + 1) * 512], k == 0, k == KC - 1, [q_r] + rd(k), ps_r[pbk])
                    qk_norm_rope(pbk, "qg", 128 + hf * 512, 512,
                                 QT[:, hf * 4:(hf + 1) * 4, hh, :], [QT_r[qb] for qb in range(hf * 4, hf * 4 + 4)])
            for qb in range(8):
                eb = (qb % 2) * 2
                for which in range(2):
                    kt = qb + which
                    bank = which
                    mm(psum[bank][:, :], KT[:, kt * 128:(kt + 1) * 128], QT[:, qb].rearrange("p a b -> p (a b)"), True, True,
                       [KT_r[kt], QT_r[qb]], ps_r[bank])
                    P.op("act", lambda e, bank=bank, which=which, eb=eb: e.activation(out=EB[eb + which], in_=psum[bank][:, :], func=AF.Exp, scale=scale),
                         reads=[ps_r[bank]], writes=[EB_r[eb + which]])
                    msk = mask_cur if which == 1 else (mask_prev0 if qb == 0 else mask_prev)
                    P.op("dve", lambda e, which=which, eb=eb, msk=msk: e.tensor_tensor(out=EB[eb + which], in0=EB[eb + which], in1=msk, op=ALU.mult),
                         reads=[EB_r[eb + which], cmat_r], writes=[EB_r[eb + which]])
                for which in range(2):
                    kt = qb + which
                    mm(psum[2][:, :], VT[:, kt, g * 128:(g + 1) * 128], EB[eb + which], which == 0, which == 1, [VT_r[kt], EB_r[eb + which]], ps_r[2])
                for which in range(2):
                    mm(psum[3][:, :], ones_b, EB[eb + which], which == 0, which == 1, [cmat_r, EB_r[eb + which]], ps_r[3])
                d = qb % 2
                for hh in range(4):
                    h = g * 4 + hh
                    P.op("act", lambda e, d=d, hh=hh, h=h: e.activation(out=dns[d][:, hh * 128:(hh + 1) * 128], in_=psum[3][:, hh * 128:(hh + 1) * 128],
                                                                       func=AF.Ln, bias=sml[:, ESINK + h:ESINK + h + 1]),
                         reads=[ps_r[3], sml_r], writes=[dns_r[d]])
                P.op("act", lambda e, d=d: e.activation(out=dns[d], in_=dns[d], func=AF.Exp, scale=-1.0), reads=[dns_r[d]], writes=[dns_r[d]])
                extra = [tmp_r] if g >= 2 else []
                P.op("dve", lambda e, d=d, qb=qb, g=g: e.tensor_tensor(out=AT[:, g * 4:(g + 1) * 4, qb * 128:(qb + 1) * 128],
                                                                       in0=psum[2][:, :].rearrange("p (a b) -> p a b", a=4),
                                                                       in1=dns[d].rearrange("p (a b) -> p a b", a=4), op=ALU.mult),
                     reads=[ps_r[2], dns_r[d]], writes=[AT_r[g * 4 + hh] for hh in range(4)] + extra)
        dbg_dump("att", AT.rearrange("p a b -> p (a b)"), AT_r, KC * T)
        if stop_after == "AT":
            return finish(nc, P, W, specs, out_d, None)

        P.barrier()
        sg = [view(0, [128, 512], F32), view(2048, [128, 512], F32)]
        sg_r = [Res("sg0b"), Res("sg1b")]
        tA = [view(4096, [128, 512], F32), view(6144, [128, 512], F32)]
        tA_r = [Res("tA0b"), Res("tA1b")]

        def ap_spec(mp):
            return spec_k16([("w_attn_proj", (2 * mp) * 128), ("w_attn_proj", (2 * mp + 1) * 128)])
        proj_gate_stage(KC, lambda k, hf: AT[:, k, hf * 512:(hf + 1) * 512], lambda k: AT_r[k], ap_spec, SPL[4], "bga", False, sg, sg_r, tA, tA_r)
        dbg_dump("mrg", MG.rearrange("p a b -> p (a b)"), MG_r, KC * T)
        if stop_after == "MA":
            return finish(nc, P, W, specs, out_d, None)

        P.barrier()
        H = view(0, [128, 8, D], F32)
        H_r = [[Res("H%d_%d" % (t, n)) for n in range(4)] for t in range(8)]
        h_s = P.dsem()
        for t in range(8):
            P.dma("sp", lambda e, t=t: e.dma_start(out=H[:, t, :], in_=x_own[t * 128:(t + 1) * 128, :]), P.dsem(), writes=H_r[t])
        it_ = 0
        for n in range(4):
            ws = []
            for half in range(2):
                def wospec(inp, half=half, n=n):
                    w = inp["w_out"][0][half * 1024:(half + 1) * 1024, n * 512:(n + 1) * 512]
                    return w.reshape(8, 128, 512).transpose(1, 0, 2).reshape(128, 4096)
                ws.append(wload(4096, wospec))
            for t in range(8):
                bank = it_ % 4
                it_ += 1
                for k in range(KC):
                    slab, slab_r = ws[k // 8]
                    mm(psum[bank][:, :], MG[:, k, t * 128:(t + 1) * 128], slab[:, (k % 8) * 512:(k % 8 + 1) * 512],
                       k == 0, k == KC - 1, [slab_r, MG_r[k]], ps_r[bank])
                P.op("dve", lambda e, t=t, n=n, bank=bank: e.tensor_tensor(out=H[:, t, n * 512:(n + 1) * 512], in0=psum[bank][:, :],
                                                                            in1=H[:, t, n * 512:(n + 1) * 512], op=ALU.add),
                     reads=[ps_r[bank], H_r[t][n]], writes=[H_r[t][n]])
        dbg_dump("hh", H.rearrange("p a b -> p (a b)"), [r for t in range(8) for r in H_r[t]], 8 * D)
        if stop_after == "C":
            return finish(nc, P, W, specs, out_d, (H, H_r))
        P.barrier()
        hn = xn_t[:, 0:KC * T].rearrange("p (k t) -> p k t", k=KC)
        hn_r = [[Res("hn%d_%d" % (k, t)) for t in range(8)] for k in range(KC)]

        class _AllH:
            pass

        def c_src(i):
            r = Res("Hall%d" % i)
            r.w = H_r[i][3].w
            return H[:, i, :], r
        norm_transpose(c_src, 8, 1, hn, hn_r, 0, 65536)

        P.barrier()
        ACTB = [view(65536, [128, 4, T]), view(65536 + 8192, [128, 4, T])]
        ACTB_r = [[Res("act%d_%d" % (b, cc_)) for cc_ in range(4)] for b in range(2)]
        slb = [view(81920, [128, 512], F32), view(81920 + 2048, [128, 512], F32)]
        slb_r = [Res("sl0"), Res("sl1")]

        def hn_reads(hf):
            return lambda kc: [hn_r[kc][t] for t in range(hf * 4, hf * 4 + 4)]
        it2 = 0
        for gi in range(FC // 4):
            ab = gi % 2
            for pr in range(2):
                c0 = 4 * gi + 2 * pr
                gs, gs_r = wload(4096, spec_k16([("w_ffn_gate", c0 * 128), ("w_ffn_gate", (c0 + 1) * 128)]))
                us, us_r = wload(4096, spec_k16([("w_ffn_up", c0 * 128), ("w_ffn_up", (c0 + 1) * 128)]))
                gv = gs[:, :].rearrange("p (g k m) -> p g k m", g=2, k=KC)
                uv = us[:, :].rearrange("p (g k m) -> p g k m", g=2, k=KC)
                for ci in range(2):
                    cj = 2 * pr + ci
                    for hf in range(2):
                        bg_, bu_ = hf, 2 + hf
                        rd = hn_reads(hf)
                        for k in range(KC):
                            mm(psum[bg_][:, :], gv[:, ci, k, :], hn[:, k, hf * 512:(hf + 1) * 512], k == 0, k == KC - 1, [gs_r] + rd(k), ps_r[bg_])
                        for k in range(KC):
                            mm(psum[bu_][:, :], uv[:, ci, k, :], hn[:, k, hf * 512:(hf + 1) * 512], k == 0, k == KC - 1, [us_r] + rd(k), ps_r[bu_])
                        P.op("act", lambda e, hf=hf, bg_=bg_: e.activation(out=slb[hf], in_=psum[bg_][:, :], func=AF.Silu), reads=[ps_r[bg_]], writes=[slb_r[hf]])
                        P.op("dve", lambda e, hf=hf, bu_=bu_, ab=ab, cj=cj: e.tensor_tensor(out=ACTB[ab][:, cj, hf * 512:(hf + 1) * 512], in0=psum[bu_][:, :],
                                                                                           in1=slb[hf], op=ALU.mult),
                             reads=[ps_r[bu_], slb_r[hf]], writes=[ACTB_r[ab][cj]])
            dsl = []
            for pr in range(2):
                def dspec(inp, r0=(4 * gi + 2 * pr) * 128):
                    w = inp["w_ffn_down"][0][r0:r0 + 256, :]
                    return w.reshape(2, 128, D).transpose(1, 0, 2).reshape(128, 4096)
                ds_, ds_r = wload(4096, dspec)
                dsl.append((ds_[:, :].rearrange("p (c n) -> p c n", c=2), ds_r))
            for t in range(8):
                for n in range(4):
                    bank = 4 + it2 % 4
                    it2 += 1
                    for cj in range(4):
                        dv, ds_r = dsl[cj // 2]
                        mm(psum[bank][:, :], ACTB[ab][:, cj, t * 128:(t + 1) * 128], dv[:, cj % 2, n * 512:(n + 1) * 512], cj == 0, cj == 3,
                           [ACTB_r[ab][cj], ds_r], ps_r[bank])
                    P.op("dve", lambda e, t=t, n=n, bank=bank: e.tensor_tensor(out=H[:, t, n * 512:(n + 1) * 512], in0=psum[bank][:, :],
                                                                                in1=H[:, t, n * 512:(n + 1) * 512], op=ALU.add),
                         reads=[ps_r[bank], H_r[t][n]], writes=[H_r[t][n]])
        return finish(nc, P, W, specs, out_d, (H, H_r))


def finish(nc, P, W, specs, out_d, Hinfo):
    evs = []
    if Hinfo is not None:
        H, H_r = Hinfo
        for t in range(8):
            evs.append(P.dma("sp", lambda e, t=t: e.dma_start(out=out_d[t * 128:(t + 1) * 128, :], in_=H[:, t, :]), P.dsem(), reads=H_r[t]))
    P.barrier()
    P.wait("sp", evs)
    W["ap"] = nc.dram_tensor("wstream", [max(W["off"], 128)], F32, kind="ExternalInput").ap()
    P.emit()
    return nc, specs


def pack_weights(specs, inp):
    parts = []
    for n_el, fn in specs:
        a = np.ascontiguousarray(fn(inp), dtype=np.float32)
        assert a.shape == (128, n_el), (a.shape, n_el)
        parts.append(a.reshape(-1))
    if not parts:
        return np.zeros(128, np.float32)
    return np.concatenate(parts)


def make_in_maps(inp, specs):
    inp = {k: np.asarray(v) for k, v in inp.items()}
    wst = pack_weights(specs, inp)
    g12 = np.stack([np.broadcast_to(inp["norm1_g"][0][None, :], (128, D)),
                    np.broadcast_to(inp["norm2_g"][0][None, :], (128, D))]).astype(np.float32)
    maps = []
    for core in range(8):
        b, j = core // 4, core % 4
        t0 = j * T
        x_own = np.ascontiguousarray(inp["x"][b, t0:t0 + T])
        if j > 0:
            x_halo = np.ascontiguousarray(inp["x"][b, t0 - 128:t0])
            pos = inp["positions"][b, t0 - 128:t0 + T]
        else:
            x_halo = np.zeros((128, D), np.float32)
            pos = np.concatenate([np.zeros(128, np.int32), inp["positions"][b, 0:T]])
        maps.append({
            "x_own": x_own, "x_halo": x_halo,
            "pos": np.ascontiguousarray(np.broadcast_to(pos[None, :].astype(np.int32), (32, TE))),
            "consts": host_consts(inp, core), "cmat": host_cmat(core), "g12": g12, "wstream": wst,
        })
    return maps


_CACHE = {}


def kernel(**inputs):
    if "nc" not in _CACHE:
        _CACHE["nc"] = build_nc()
    nc, specs = _CACHE["nc"]
    maps = make_in_maps(inputs, specs)
    res = run_bass_kernel_spmd(nc, maps, core_ids=list(range(8)))
    out = np.zeros((2, S, D), np.float32)
    for core in range(8):
        b, j = core // 4, core % 4
        out[b, j * T:(j + 1) * T] = res.results[core]["out"]
    return out
```

```python
import contextlib
import math
import numpy as np
import concourse.bass as bass
import concourse.mybir as mybir
from concourse.bass_utils import run_bass_kernel_spmd

F32 = mybir.dt.float32
BF16 = mybir.dt.bfloat16
I32 = mybir.dt.int32
AF = mybir.ActivationFunctionType
ALU = mybir.AluOpType

D = 2048
S = 4096
T = 1024
TE = 1152
HD = 128
NQ = 16
NKV = 4
DR = 2816
NB_RNN = 16
BW = 176
DFF = 5632
KC = 16
RC = 22
FC = 44
EPS = 1e-6
THETA = 500000.0
Q_W = 2048
KV_W = 512
SPL = np.cumsum([Q_W, KV_W, KV_W, DR, DR, D]).tolist()

ENGS = ("pe", "act", "dve", "pool", "sp")
SAME_ENGINE_SYNC = True
NRING = 4
SLAB = 4096


class Res:
    __slots__ = ("name", "w", "r")

    def __init__(self, name=""):
        self.name = name
        self.w = None
        self.r = {}


class DSem:
    __slots__ = ("sem", "cnt")

    def __init__(self, sem):
        self.sem = sem
        self.cnt = 0


class Prog:
    def __init__(self, nc, stack):
        self.nc = nc
        self.stack = stack
        self.sem = {e: stack.enter_context(nc.semaphore("s_" + e)) for e in ENGS}
        self.cnt = {e: 0 for e in ENGS}
        self.ops = {e: [] for e in ENGS}
        self.waited = {e: {} for e in ENGS}
        self.pending = {e: [] for e in ENGS}
        self.ndsem = 0
        self.dma_evs = []

    def dsem(self):
        self.ndsem += 1
        return DSem(self.stack.enter_context(self.nc.semaphore("d%d" % self.ndsem)))

    def _collect(self, e, reads, writes):
        evs = []
        for r in reads:
            if r.w is not None:
                evs.append(r.w)
        for w in writes:
            for pe_ in ENGS:
                if pe_ != e and any(w is p for p in self.pending[pe_]):
                    raise RuntimeError("write to %s with pending unsignaled reads on %s" % (w.name, pe_))
            if w.w is not None:
                evs.append(w.w)
            for k, ev in w.r.items():
                if k == e and (e == "pe" or not SAME_ENGINE_SYNC):
                    continue
                evs.append(ev)
        waits = []
        wd = self.waited[e]
        for (sem, val, src) in evs:
            if src == e and (e == "pe" or not SAME_ENGINE_SYNC):
                continue
            if wd.get(sem, 0) >= val:
                continue
            wd[sem] = val
            waits.append((sem, val))
        return waits

    def op(self, e, fn, reads=(), writes=(), signal=True):
        waits = self._collect(e, reads, writes)
        ev = None
        if signal:
            self.cnt[e] += 1
            ev = (self.sem[e], self.cnt[e], e)
            for r in self.pending[e]:
                r.r[e] = ev
            self.pending[e] = []
            for r in reads:
                r.r[e] = ev
            for w in writes:
                w.w = ev
                w.r = {}
            self.ops[e].append((waits, fn, (self.sem[e], 1)))
        else:
            for r in reads:
                self.pending[e].append(r)
            self.ops[e].append((waits, fn, None))
        return ev

    def dma(self, q, fn, ds, reads=(), writes=(), inc=16):
        waits = self._collect(q, reads, writes)
        ds.cnt += inc
        ev = (ds.sem, ds.cnt, "dma")
        self.ops[q].append((waits, fn, (ds.sem, inc)))
        for r in reads:
            r.r[("dma", id(ds))] = ev
        for w in writes:
            w.w = ev
            w.r = {}
        self.dma_evs.append(ev)
        return ev

    def wait(self, e, evs):
        waits = []
        wd = self.waited[e]
        for (sem, val, src) in evs:
            if wd.get(sem, 0) >= val:
                continue
            wd[sem] = val
            waits.append((sem, val))
        if waits:
            self.ops[e].append((waits, None, None))

    def barrier(self, group=("pe", "act", "dve", "sp")):
        for e in group:
            assert not self.pending[e], "barrier with pending unsignaled ops on " + e
        evs = [(self.sem[o], self.cnt[o], o) for o in group if self.cnt[o] > 0]
        last = {}
        for ev in self.dma_evs:
            last[ev[0]] = ev
        self.dma_evs = list(last.values())
        for e in group:
            self.wait(e, [ev for ev in evs if ev[2] != e] + self.dma_evs)


    def check(self):
        val = {}
        ptr = {e: 0 for e in ENGS}
        total = sum(len(v) for v in self.ops.values())
        done = 0
        while done < total:
            prog = False
            for e in ENGS:
                lst = self.ops[e]
                while ptr[e] < len(lst):
                    waits, fn, inc = lst[ptr[e]]
                    if any(val.get(s, 0) < v for s, v in waits):
                        break
                    if inc is not None:
                        val[inc[0]] = val.get(inc[0], 0) + inc[1]
                    ptr[e] += 1
                    done += 1
                    prog = True
            if not prog:
                msg = []
                for e in ENGS:
                    if ptr[e] < len(self.ops[e]):
                        waits, fn, inc = self.ops[e][ptr[e]]
                        msg.append("%s@%d waits %s" % (e, ptr[e], [(s.name, v, val.get(s, 0)) for s, v in waits if val.get(s, 0) < v]))
                raise RuntimeError("DEADLOCK: " + "; ".join(msg))
        return True

    def emit(self):
        self.check()
        nc = self.nc
        ops = self.ops

        def run(eng, lst):
            for waits, fn, inc in lst:
                for sem, val in waits:
                    eng.wait_ge(sem, val)
                if fn is None:
                    continue
                ins = fn(eng)
                if inc is not None:
                    ins.then_inc(inc[0], inc[1])

        with nc.Block() as block:
            @block.sync
            def _(eng):
                run(eng, ops["sp"])

            @block.scalar
            def _(eng):
                run(eng, ops["act"])

            @block.vector
            def _(eng):
                run(eng, ops["dve"])

            @block.gpsimd
            def _(eng):
                run(eng, ops["pool"])

            @block.tensor
            def _(eng):
                run(eng, ops["pe"])


def gate_klist(c):
    n_lo = (128 * c) // BW
    n_hi = (128 * c + 127) // BW
    k_lo = (BW * n_lo) // 128
    k_hi = (BW * (n_hi + 1) - 1) // 128
    return list(range(k_lo, k_hi + 1))


CC = {}
_o = 0
for _name, _n in [("qg", 1), ("kg", 1), ("bga", 16), ("bgl", 16), ("cw", 88), ("cb", 22), ("br", 22),
                  ("bi", 22), ("lam", 22), ("invf", 1), ("sinks", 16), ("sel", 8), ("eps", 1), ("one", 1)]:
    CC[_name] = _o
    _o += _n
NCONST = _o


def host_consts(inp, core):
    c = np.zeros((128, NCONST), np.float32)
    j = core % 4
    b = core // 4
    c[:, CC["qg"]] = inp["q_norm_g"][0]
    c[:, CC["kg"]] = inp["k_norm_g"][0]
    c[:, CC["bga"]:CC["bga"] + 16] = inp["b_gates"][0][:D].reshape(16, 128).T
    c[:, CC["bgl"]:CC["bgl"] + 16] = inp["b_gates"][0][D:].reshape(16, 128).T
    for tap in range(4):
        c[:, CC["cw"] + tap * 22:CC["cw"] + (tap + 1) * 22] = inp["conv_w"][0][tap].reshape(22, 128).T
    c[:, CC["cb"]:CC["cb"] + 22] = inp["conv_b"][0].reshape(22, 128).T
    c[:, CC["br"]:CC["br"] + 22] = inp["b_rgate"][0].reshape(22, 128).T
    c[:, CC["bi"]:CC["bi"] + 22] = inp["b_igate"][0].reshape(22, 128).T
    c[:, CC["lam"]:CC["lam"] + 22] = inp["lru_lambda"][0].reshape(22, 128).T
    invf = (np.float32(THETA) ** (-np.arange(0, 32, 2, dtype=np.float32) / np.float32(32))).astype(np.float32)
    c[0:16, CC["invf"]] = invf
    c[16:32, CC["invf"]] = invf
    c[:, CC["sinks"]:CC["sinks"] + 16] = inp["sinks"][0][None, :]
    c[:, CC["eps"]] = EPS
    c[:, CC["one"]] = 1.0
    for k in range(8):
        c[:, CC["sel"] + k] = 1.0 if (k // 4 == b and k < core) else 0.0
    return c


def host_cmat(core):
    j = core % 4
    m = np.zeros((128, 2 * 128 + 3 * 512 + 32), np.float32)
    m[:, 0:128] = np.eye(128, dtype=np.float32)
    m[:, 128:256] = 1.0
    p = np.arange(128)[:, None]
    f = np.arange(128)[None, :]
    cur = (p <= f).astype(np.float32)
    prev = (p > f).astype(np.float32)
    prev0 = prev if j > 0 else np.zeros_like(prev)
    m[:, 256:768] = np.tile(cur, (1, 4))
    m[:, 768:1280] = np.tile(prev, (1, 4))
    m[:, 1280:1792] = np.tile(prev0, (1, 4))
    rot = np.zeros((128, 32), np.float32)
    for mm_ in range(16):
        rot[mm_ + 16, mm_] = -1.0
        rot[mm_, mm_ + 16] = 1.0
    m[:, 1792:1824] = rot
    return m


NCMAT = 1824


STAGES = ("A", "L", "ML", "AT", "MA", "C", "D")


def build_nc(debug=(), stop_after=None):
    nc = bass.Bass("TRN2", target_bir_lowering=False)
    specs = []
    W = {"off": 0, "i": 0, "ap": None}

    x_own = nc.dram_tensor("x_own", [T, D], F32, kind="ExternalInput").ap()
    x_halo = nc.dram_tensor("x_halo", [128, D], F32, kind="ExternalInput").ap()
    pos_d = nc.dram_tensor("pos", [32, TE], I32, kind="ExternalInput").ap()
    consts_d = nc.dram_tensor("consts", [128, NCONST], F32, kind="ExternalInput").ap()
    cmat_d = nc.dram_tensor("cmat", [128, NCMAT], F32, kind="ExternalInput").ap()
    g12_d = nc.dram_tensor("g12", [2, 128, D], F32, kind="ExternalInput").ap()
    out_d = nc.dram_tensor("out", [T, D], F32, kind="ExternalOutput").ap()
    r1_d = nc.dram_tensor("r1_spill", [RC, 128, T], BF16).ap()
    cc_in = nc.dram_tensor("cc_in", [128, 64], F32).ap()
    cc_out = nc.dram_tensor("cc_out", [8 * 128, 64], F32).ap()
    dbg_d = {}
    for name, shape in (("xn", [128, KC * TE]), ("rec", [128, RC * T]), ("mrg", [128, KC * T]),
                        ("att", [128, KC * T]), ("hh", [128, 8 * D]), ("misc", [128, 4096])):
        if name in debug:
            dbg_d[name] = nc.dram_tensor("dbg_" + name, shape, F32, kind="ExternalOutput").ap()

    with contextlib.ExitStack() as st:
        P = Prog(nc, st)
        big = nc.alloc_sbuf_tensor("big", [128, 65536], BF16)
        xn_t = nc.alloc_sbuf_tensor("xn", [128, KC * TE], BF16)
        ring = [nc.alloc_sbuf_tensor("ring%d" % i, [128, SLAB], BF16) for i in range(NRING)]
        ring_r = [Res("ring%d" % i) for i in range(NRING)]
        ring_s = [P.dsem() for _ in range(NRING)]
        cst = nc.alloc_sbuf_tensor("cst", [128, NCONST], F32)
        cmat = nc.alloc_sbuf_tensor("cmatb", [128, NCMAT], BF16)
        sml = nc.alloc_sbuf_tensor("sml", [128, 320], F32)
        psum = [nc.alloc_psum_tensor("ps%d" % i, [128, 512], F32) for i in range(8)]
        ps_r = [Res("ps%d" % i) for i in range(8)]
        cst_r = Res("cst")
        cmat_r = Res("cmat")
        ds_c = P.dsem()

        def view(off, shape, dt=BF16):
            esz = 2 if dt == BF16 else 4
            n = int(np.prod(shape[1:]))
            a = big[:, off // 2: off // 2 + n * esz // 2]
            if dt != BF16:
                a = a.bitcast(dt)
            if len(shape) == 3:
                a = a.rearrange("p (a b) -> p a b", a=shape[1])
            elif len(shape) == 4:
                a = a.rearrange("p (a b c) -> p a b c", a=shape[1], b=shape[2])
            return a

        xn = xn_t[:, :].rearrange("p (k t) -> p k t", k=KC)
        xn_r = [[Res("xn%d_%d" % (k, t)) for t in range(9)] for k in range(KC)]
        ident = cmat[:, 0:128]
        ones_b = cmat[:, 128:256]
        mask_cur = cmat[:, 256:768]
        mask_prev = cmat[:, 768:1280]
        mask_prev0 = cmat[:, 1280:1792]
        rotm = cmat[0:32, 1792:1824]

        def cc(name, i=0):
            return cst[:, CC[name] + i: CC[name] + i + 1]

        SS, RSTD, NSP8, NSP16, ESINK, CCS, HS, TMPA, TMPB, CHA, CHB = 0, 16, 32, 54, 76, 256, 136, 158, 180, 202, 224
        sml_r = Res("sml")

        def wload(n_el, spec):
            b = W["i"] % NRING
            W["i"] += 1
            off = W["off"]
            W["off"] += 128 * n_el
            specs.append((n_el, spec))

            def fn(e, b=b, off=off, n_el=n_el):
                src = W["ap"][off: off + 128 * n_el].rearrange("(p n) -> p n", p=128)
                return e.dma_start(out=ring[b][:, 0:n_el], in_=src, max_dma_last_dim=8192)
            P.dma("pool", fn, ring_s[b], writes=[ring_r[b]])
            return ring[b], ring_r[b]

        def mm(out, lhsT, rhs, start, stop, reads, wres, signal=None):
            if signal is None:
                signal = stop
            P.op("pe", lambda e: e.matmul(out, lhsT, rhs, start=start, stop=stop),
                 reads=reads, writes=[wres], signal=signal)

        def dbg_dump(name, src_ap, res_list, width):
            if name not in dbg_d:
                return
            P.barrier()
            soff = {"xn": 98304, "rec": 114688, "mrg": 8192, "att": 8192, "hh": 73728}[name]
            ds = [P.dsem(), P.dsem()]
            nchunk = (width + 2047) // 2048
            tv = [view(soff, [128, 2048], F32), view(soff + 8192, [128, 2048], F32)]
            tr = [Res("dbgt0"), Res("dbgt1")]
            for i in range(nchunk):
                lo = i * 2048
                hi = min(width, lo + 2048)
                P.op("dve", lambda e, i=i, lo=lo, hi=hi: e.tensor_copy(out=tv[i % 2][:, 0:hi - lo], in_=src_ap[:, lo:hi]),
                     reads=res_list, writes=[tr[i % 2]])
                P.dma("sp", lambda e, i=i, lo=lo, hi=hi: e.dma_start(out=dbg_d[name][:, lo:hi], in_=tv[i % 2][:, 0:hi - lo]),
                      ds[i % 2], reads=[tr[i % 2]])
            P.barrier()

        P.dma("sp", lambda e: e.dma_start(out=cst[:, :], in_=consts_d), ds_c, writes=[cst_r])
        P.dma("pool", lambda e: e.dma_start(out=cmat[:, :], in_=cmat_d, max_dma_last_dim=4096), P.dsem(), writes=[cmat_r])
        P.op("act", lambda e: e.activation(out=sml[:, TMPA:TMPA + RC], in_=cst[:, CC["lam"]:CC["lam"] + RC], func=AF.Exp, scale=-1.0),
             reads=[cst_r], writes=[sml_r])
        P.op("act", lambda e: e.activation(out=sml[:, TMPB:TMPB + RC], in_=sml[:, TMPA:TMPA + RC], func=AF.Ln, bias=1.0),
             reads=[sml_r], writes=[sml_r])
        P.op("dve", lambda e: e.tensor_scalar(out=sml[:, NSP8:NSP8 + RC], in0=sml[:, TMPB:TMPB + RC], scalar1=-8.0, scalar2=None, op0=ALU.mult),
             reads=[sml_r], writes=[sml_r])
        P.op("dve", lambda e: e.tensor_scalar(out=sml[:, NSP16:NSP16 + RC], in0=sml[:, TMPB:TMPB + RC], scalar1=-16.0, scalar2=None, op0=ALU.mult),
             reads=[sml_r], writes=[sml_r])
        P.op("act", lambda e: e.activation(out=sml[:, ESINK:ESINK + 16], in_=cst[:, CC["sinks"]:CC["sinks"] + 16], func=AF.Exp),
             reads=[cst_r], writes=[sml_r])

        def norm_transpose(src_tile_fn, ntiles, gsel, dst, dst_r, dst_col0, a_off):
            xsb = [view(a_off, [128, D]), view(a_off + 4096, [128, D])]
            xsb_r = [Res("xsb0"), Res("xsb1")]
            sq = view(a_off + 8192, [128, D])
            sq_r = Res("sq")
            gb = view(a_off + 12288, [128, D], F32)
            gb_r = Res("gb")
            P.dma("sp", lambda e: e.dma_start(out=gb, in_=g12_d[gsel]), P.dsem(), writes=[gb_r])
            for i in range(ntiles):
                src, src_r = src_tile_fn(i)
                P.op("act", lambda e, src=src, i=i: e.activation(out=sq, in_=src, func=AF.Square, accum_out=sml[:, SS + i:SS + i + 1]),
                     reads=[src_r], writes=[sq_r, sml_r])
                P.op("dve", lambda e, i=i: e.tensor_scalar(out=sml[:, TMPA + i:TMPA + i + 1], in0=sml[:, SS + i:SS + i + 1],
                                                         scalar1=1.0 / D, scalar2=EPS, op0=ALU.mult, op1=ALU.add),
                     reads=[sml_r], writes=[sml_r])
                P.op("act", lambda e, i=i: e.activation(out=sml[:, TMPB + i:TMPB + i + 1], in_=sml[:, TMPA + i:TMPA + i + 1], func=AF.Sqrt),
                     reads=[sml_r], writes=[sml_r])
                P.op("dve", lambda e, i=i: e.reciprocal(out=sml[:, RSTD + i:RSTD + i + 1], in_=sml[:, TMPB + i:TMPB + i + 1]),
                     reads=[sml_r], writes=[sml_r])
                b = i % 2
                P.op("dve", lambda e, src=src, i=i, b=b: e.scalar_tensor_tensor(out=xsb[b], in0=src, scalar=sml[:, RSTD + i:RSTD + i + 1],
                                                                              in1=gb, op0=ALU.mult, op1=ALU.mult),
                     reads=[src_r, sml_r, gb_r], writes=[xsb_r[b]])
                for hh in range(2):
                    bank = (2 * i + hh) % 4
                    pb = psum[bank][:, :].bitcast(BF16).rearrange("p (a b) -> p a b", a=8)
                    for k8 in range(8):
                        kc = hh * 8 + k8
                        P.op("pe", lambda e, pb=pb, k8=k8, kc=kc, b=b: e.transpose(pb[:, k8, :], xsb[b][:, kc * 128:(kc + 1) * 128], ident),
                             reads=[xsb_r[b], cmat_r], writes=[ps_r[bank]], signal=(k8 == 7))
                    c0 = dst_col0 + i * 128
                    eng = "act" if hh == 0 else "dve"
                    if eng == "act":
                        P.op("act", lambda e, pb=pb, hh=hh, c0=c0: e.activation(out=dst[:, hh * 8:(hh + 1) * 8, c0:c0 + 128], in_=pb, func=AF.Copy),
                             reads=[ps_r[bank]], writes=[dst_r[k][i] for k in range(hh * 8, hh * 8 + 8)])
                    else:
                        P.op("dve", lambda e, pb=pb, hh=hh, c0=c0: e.tensor_copy(out=dst[:, hh * 8:(hh + 1) * 8, c0:c0 + 128], in_=pb),
                             reads=[ps_r[bank]], writes=[dst_r[k][i] for k in range(hh * 8, hh * 8 + 8)])

        xt = [view(0, [128, D], F32), view(8192, [128, D], F32)]
        xt_r = [Res("xt0"), Res("xt1")]
        xt_s = [P.dsem(), P.dsem()]

        def a_src(i):
            b = i % 2
            if i == 0:
                P.dma("sp", lambda e: e.dma_start(out=xt[0], in_=x_halo), xt_s[0], writes=[xt_r[0]])
            else:
                P.dma("sp", lambda e, i=i, b=b: e.dma_start(out=xt[b], in_=x_own[(i - 1) * 128:i * 128, :]), xt_s[b], writes=[xt_r[b]])
            return xt[b], xt_r[b]
        norm_transpose(a_src, 9, 0, xn, xn_r, 0, 16384)
        dbg_dump("xn", xn_t[:, :], [r for k in range(KC) for r in xn_r[k]], KC * TE)
        if stop_after == "A":
            return finish(nc, P, W, specs, out_d, None)

        def xn_reads(tok0, n):
            t0, t1 = tok0 // 128, (tok0 + n - 1) // 128
            return lambda kc: [xn_r[kc][t] for t in range(t0, t1 + 1)]

        P.barrier()
        R0 = view(0, [128, RC, T])
        R0_r = [Res("R0_%d" % c) for c in range(RC)]
        UC = view(45056, [128, 8, T])
        UC_r = [Res("UC%d" % i) for i in range(8)]
        GT = view(61440, [128, 4, T])
        GT_r = [Res("GT%d" % i) for i in range(4)]
        UP = [view(69632, [128, 1040], F32), view(69632 + 4160, [128, 1040], F32)]
        UP_r = [Res("UP0"), Res("UP1")]
        f32t = {}
        for i, nm in enumerate(("acc", "rt", "it", "at", "st", "bt", "ht", "cum", "zeros")):
            f32t[nm] = (view(77952 + 4096 * i, [128, T], F32), Res(nm))
        R1T = [view(114816, [128, T]), view(114816 + 2048, [128, T])]
        R1T_r = [Res("R1T0"), Res("R1T1")]
        r1_s = [P.dsem(), P.dsem()]
        ccs = sml[:, CCS:CCS + 64]
        ccs_r = Res("ccs")
        P.op("dve", lambda e: e.memset(ccs, 0.0), writes=[ccs_r])
        P.op("dve", lambda e: e.memset(f32t["zeros"][0], 0.0), writes=[f32t["zeros"][1]])

        def w_in_cols(lo, n):
            return lambda inp: inp["w_in"][0][:, lo:lo + n]

        def spec_k16(colfn_list):
            def fn(inp):
                parts = []
                for wname, lo in colfn_list:
                    wmat = inp[wname][0]
                    parts.append(wmat[:, lo:lo + 128].reshape(KC, 128, 128).transpose(1, 0, 2).reshape(128, KC * 128))
                return np.concatenate(parts, axis=1)
            return fn

        def gate_spec(c):
            kl = gate_klist(c)

            def fn(inp):
                parts = []
                for wname in ("w_rgate", "w_igate"):
                    full = np.zeros((len(kl) * 128, 128), np.float32)
                    wg = inp[wname][0]
                    for n in range(NB_RNN):
                        r0, r1 = n * BW, (n + 1) * BW
                        ro0, ro1 = max(r0, kl[0] * 128), min(r1, (kl[-1] + 1) * 128)
                        co0, co1 = max(r0, c * 128), min(r1, (c + 1) * 128)
                        if ro0 < ro1 and co0 < co1:
                            full[ro0 - kl[0] * 128:ro1 - kl[0] * 128, co0 - c * 128:co1 - c * 128] = wg[n][ro0 - r0:ro1 - r0, co0 - r0:co1 - r0]
                    parts.append(full.reshape(len(kl), 128, 128).transpose(1, 0, 2).reshape(128, len(kl) * 128))
                return np.concatenate(parts, axis=1)
            return fn

        def lru_gates(c):
            kl = gate_klist(c)
            nk = len(kl)
            slab, slab_r = wload(2 * nk * 128, gate_spec(c))
            sv = slab[:, 0:2 * nk * 128].rearrange("p (g k m) -> p g k m", g=2, k=nk)
            for gi in range(2):
                for hf in range(2):
                    bank = 4 + gi * 2 + hf
                    for ki, k in enumerate(kl):
                        mm(psum[bank][:, :], sv[:, gi, ki, :], UC[:, k % 8, hf * 512:(hf + 1) * 512],
                           ki == 0, ki == nk - 1, [slab_r, UC_r[k % 8]], ps_r[bank])
            rt, rt_r = f32t["rt"]
            it, it_r = f32t["it"]
            at, at_r = f32t["at"]
            s_t, st_r = f32t["st"]
            bt, bt_r = f32t["bt"]
            ht, ht_r = f32t["ht"]
            cum, cum_r = f32t["cum"]
            zt, zt_r = f32t["zeros"]
            for hf in range(2):
                sl = slice(hf * 512, (hf + 1) * 512)
                P.op("act", lambda e, hf=hf, sl=sl: e.activation(out=rt[:, sl], in_=psum[4 + hf][:, :], func=AF.Sigmoid, bias=cc("br", c)),
                     reads=[ps_r[4 + hf], cst_r], writes=[rt_r])
                P.op("act", lambda e, hf=hf, sl=sl: e.activation(out=it[:, sl], in_=psum[6 + hf][:, :], func=AF.Sigmoid, bias=cc("bi", c)),
                     reads=[ps_r[6 + hf], cst_r], writes=[it_r])
            P.op("act", lambda e: e.activation(out=at, in_=rt, func=AF.Exp, scale=sml[:, NSP8 + c:NSP8 + c + 1]),
                 reads=[rt_r, sml_r], writes=[at_r])
            P.op("act", lambda e: e.activation(out=s_t, in_=rt, func=AF.Exp, scale=sml[:, NSP16 + c:NSP16 + c + 1]),
                 reads=[rt_r, sml_r], writes=[st_r])
            P.op("act", lambda e: e.activation(out=s_t, in_=s_t, func=AF.Sqrt, scale=-1.0, bias=1.0),
                 reads=[st_r], writes=[st_r])
            P.op("pool", lambda e: e.tensor_tensor(out=bt, in0=it, in1=UC[:, c % 8, :], op=ALU.mult),
                 reads=[it_r, UC_r[c % 8]], writes=[bt_r])
            P.op("pool", lambda e: e.tensor_tensor(out=bt, in0=bt, in1=s_t, op=ALU.mult),
                 reads=[bt_r, st_r], writes=[bt_r])

        def lru_gates_p2(c):
            at, at_r = f32t["at"]
            bt, bt_r = f32t["bt"]
            ht, ht_r = f32t["ht"]
            cum, cum_r = f32t["cum"]
            zt, zt_r = f32t["zeros"]
            P.op("dve", lambda e: e.tensor_tensor_scan(out=ht, data0=at, data1=bt, initial=0.0, op0=ALU.mult, op1=ALU.add),
                 reads=[at_r, bt_r], writes=[ht_r])
            P.op("dve", lambda e: e.tensor_tensor_scan(out=cum, data0=at, data1=zt, initial=1.0, op0=ALU.mult, op1=ALU.add),
                 reads=[at_r, zt_r], writes=[cum_r])
            P.op("dve", lambda e: e.tensor_tensor(out=R0[:, c, :], in0=ht, in1=GT[:, c % 4, :], op=ALU.mult),
                 reads=[ht_r, GT_r[c % 4]], writes=[R0_r[c]])
            b = c % 2
            P.op("dve", lambda e, b=b: e.tensor_tensor(out=R1T[b], in0=cum, in1=GT[:, c % 4, :], op=ALU.mult),
                 reads=[cum_r, GT_r[c % 4]], writes=[R1T_r[b]])
            P.dma("sp", lambda e, b=b: e.dma_start(out=r1_d[c], in_=R1T[b]), r1_s[b], reads=[R1T_r[b]])
            P.op("dve", lambda e: e.tensor_copy(out=sml[:, CCS + c:CCS + c + 1], in_=cum[:, T - 1:T]),
                 reads=[cum_r], writes=[ccs_r])
            P.op("dve", lambda e: e.tensor_copy(out=sml[:, CCS + RC + c:CCS + RC + c + 1], in_=ht[:, T - 1:T]),
                 reads=[ht_r], writes=[ccs_r])

        ug = {}

        def ug_load(c):
            ug[c] = wload(4096, spec_k16([("w_in", SPL[2] + c * 128), ("w_in", SPL[3] + c * 128)]))
        ug_load(0)

        def lru_chunk(c):
            if c + 1 < RC:
                ug_load(c + 1)
            slab, slab_r = ug[c]
            sv = slab[:, :].rearrange("p (g k m) -> p g k m", g=2, k=KC)
            for hf in range(2):
                rd = xn_reads(128 + hf * 512, 512)
                for k in range(KC):
                    mm(psum[hf][:, :], sv[:, 0, k, :], xn[:, k, 128 + hf * 512:128 + (hf + 1) * 512],
                       k == 0, k == KC - 1, [slab_r] + rd(k), ps_r[hf])
            for k in range(KC):
                mm(psum[2][:, 0:8], sv[:, 0, k, :], xn[:, k, 120:128], k == 0, k == KC - 1, [slab_r, xn_r[k][0]], ps_r[2])
            up, up_r = UP[c % 2], UP_r[c % 2]
            P.op("act", lambda e, up=up: e.activation(out=up[:, 0:8], in_=psum[2][:, 0:8], func=AF.Copy), reads=[ps_r[2]], writes=[up_r])
            P.op("dve", lambda e, up=up: e.tensor_copy(out=up[:, 8:520], in_=psum[0][:, :]), reads=[ps_r[0]], writes=[up_r])
            P.op("dve", lambda e, up=up: e.tensor_copy(out=up[:, 520:1032], in_=psum[1][:, :]), reads=[ps_r[1]], writes=[up_r])
            for hf in range(2):
                rd = xn_reads(128 + hf * 512, 512)
                gb_ = 3
                for k in range(KC):
                    mm(psum[gb_][:, :], sv[:, 1, k, :], xn[:, k, 128 + hf * 512:128 + (hf + 1) * 512],
                       k == 0, k == KC - 1, [slab_r] + rd(k), ps_r[gb_])
                P.op("act", lambda e, hf=hf, gb_=gb_: e.activation(out=GT[:, c % 4, hf * 512:(hf + 1) * 512], in_=psum[gb_][:, :], func=AF.Gelu_apprx_tanh),
                     reads=[ps_r[gb_]], writes=[GT_r[c % 4]])
            acc, acc_r = f32t["acc"]
            P.op("dve", lambda e, up=up: e.tensor_scalar(out=acc, in0=up[:, 5:5 + T], scalar1=cc("cw", 0 * 22 + c), scalar2=cc("cb", c),
                                                        op0=ALU.mult, op1=ALU.add),
                 reads=[up_r, cst_r], writes=[acc_r])
            for tap in (1, 2):
                P.op("dve", lambda e, up=up, tap=tap: e.scalar_tensor_tensor(out=acc, in0=up[:, 5 + tap:5 + tap + T], scalar=cc("cw", tap * 22 + c),
                                                                             in1=acc, op0=ALU.mult, op1=ALU.add),
                     reads=[up_r, cst_r, acc_r], writes=[acc_r])
            P.op("dve", lambda e, up=up: e.scalar_tensor_tensor(out=UC[:, c % 8, :], in0=up[:, 8:8 + T], scalar=cc("cw", 3 * 22 + c),
                                                               in1=acc, op0=ALU.mult, op1=ALU.add),
                 reads=[up_r, cst_r, acc_r], writes=[UC_r[c % 8]])
        gates_done = 0
        for c in range(RC):
            first = None
            if gates_done < RC and gate_klist(gates_done)[-1] <= c - 1:
                first = gates_done
                gates_done += 1
                lru_gates(first)
            lru_chunk(c)
            if first is not None:
                lru_gates_p2(first)
            while gates_done < RC and gate_klist(gates_done)[-1] <= c - 1:
                lru_gates(gates_done)
                lru_gates_p2(gates_done)
                gates_done += 1
        while gates_done < RC:
            lru_gates(gates_done)
            lru_gates_p2(gates_done)
            gates_done += 1

        assert gates_done == RC

        if stop_after == "L0":
            return finish(nc, P, W, specs, out_d, None)
        ccin_r, ccout_r = Res("ccin"), Res("ccout")
        P.dma("pool", lambda e: e.dma_start(out=cc_in, in_=ccs), P.dsem(), reads=[ccs_r], writes=[ccin_r])
        ds_cc = P.dsem()
        waits = P._collect("pool", [ccin_r], [ccout_r])
        ds_cc.cnt += 1
        ev = (ds_cc.sem, ds_cc.cnt, "dma")
        import os as _os
        if _os.environ.get("KDBG_NOCC"):
            P.ops["pool"].append((waits, lambda e: e.dma_start(out=cc_out[0:128, :], in_=cc_in), (ds_cc.sem, 1)))
            P.ops["pool"].append(([(ds_cc.sem, 1)], None, None))
            ds_cc.cnt = 16
            ev = (ds_cc.sem, 16, "dma")
            P.ops["pool"][-2] = (waits, lambda e: e.dma_start(out=cc_out[0:128, :], in_=cc_in), (ds_cc.sem, 16))
            P.ops["pool"].pop()
        else:
            P.ops["pool"].append((waits, lambda e: e.collective_compute("AllGather", ALU.bypass, replica_groups=[list(range(8))],
                                                                        ins=[cc_in], outs=[cc_out]), (ds_cc.sem, 1)))
        ccout_r.w = ev
        gth_t = nc.alloc_sbuf_tensor("gth", [128, 8 * 64], F32)
        gth = gth_t[:, :].rearrange("p (r c) -> p r c", r=8)
        gth_r = Res("gth")
        P.barrier()
        MG = view(98304, [128, KC, T])
        MG_r = [Res("MG%d" % m) for m in range(KC)]
        itg = 0
        for mp in range(KC // 2):
            gslab, gslab_r = wload(4096, spec_k16([("w_in", SPL[5] + (2 * mp) * 128), ("w_in", SPL[5] + (2 * mp + 1) * 128)]))
            gv = gslab[:, :].rearrange("p (g k m) -> p g k m", g=2, k=KC)
            for mi in range(2):
                m = 2 * mp + mi
                for hf in range(2):
                    bg = itg % 4
                    itg += 1
                    rd = xn_reads(128 + hf * 512, 512)
                    for k in range(KC):
                        mm(psum[bg][:, :], gv[:, mi, k, :], xn[:, k, 128 + hf * 512:128 + (hf + 1) * 512],
                           k == 0, k == KC - 1, [gslab_r] + rd(k), ps_r[bg])
                    P.op("act", lambda e, bg=bg, m=m, hf=hf: e.activation(out=MG[:, m, hf * 512:(hf + 1) * 512], in_=psum[bg][:, :], func=AF.Sigmoid,
                                                                          bias=cc("bgl", m)),
                         reads=[ps_r[bg], cst_r], writes=[MG_r[m]])
        P.dma("sp", lambda e: e.dma_start(out=gth, in_=cc_out.rearrange("(r p) c -> p r c", p=128)), P.dsem(), reads=[ccout_r], writes=[gth_r])
        hs = sml[:, HS:HS + RC]
        P.op("dve", lambda e: e.memset(hs, 0.0), writes=[sml_r])
        for k in range(8):
            selk = cc("sel", k)
            P.op("dve", lambda e, k=k, selk=selk: e.tensor_scalar(out=sml[:, CHA:CHA + RC], in0=gth[:, k, 0:RC], scalar1=-1.0, scalar2=selk,
                                                                  op0=ALU.add, op1=ALU.mult),
                 reads=[gth_r, cst_r], writes=[sml_r])
            P.op("dve", lambda e: e.tensor_scalar(out=sml[:, CHA:CHA + RC], in0=sml[:, CHA:CHA + RC], scalar1=1.0, scalar2=None, op0=ALU.add),
                 reads=[sml_r], writes=[sml_r])
            P.op("dve", lambda e, k=k, selk=selk: e.tensor_scalar(out=sml[:, CHB:CHB + RC], in0=gth[:, k, RC:2 * RC], scalar1=selk, scalar2=None,
                                                                  op0=ALU.mult),
                 reads=[gth_r, cst_r], writes=[sml_r])
            P.op("dve", lambda e: e.tensor_tensor(out=hs, in0=hs, in1=sml[:, CHA:CHA + RC], op=ALU.mult), reads=[sml_r], writes=[sml_r])
            P.op("dve", lambda e: e.tensor_tensor(out=hs, in0=hs, in1=sml[:, CHB:CHB + RC], op=ALU.add), reads=[sml_r], writes=[sml_r])
        r1b = [view(45056, [128, T]), view(45056 + 2048, [128, T])]
        r1b_r = [Res("r1b0"), Res("r1b1")]
        r1b_s = [P.dsem(), P.dsem()]
        for c in range(RC):
            b = c % 2
            P.dma("sp", lambda e, c=c, b=b: e.dma_start(out=r1b[b], in_=r1_d[c]), r1b_s[b], writes=[r1b_r[b]])
            P.op("dve", lambda e, c=c, b=b: e.scalar_tensor_tensor(out=R0[:, c, :], in0=r1b[b], scalar=sml[:, HS + c:HS + c + 1], in1=R0[:, c, :],
                                                                  op0=ALU.mult, op1=ALU.add),
                 reads=[r1b_r[b], sml_r, R0_r[c]], writes=[R0_r[c]])
        dbg_dump("rec", R0.rearrange("p a b -> p (a b)"), R0_r, RC * T)
        if stop_after == "L":
            return finish(nc, P, W, specs, out_d, None)

        def lp_spec(m):
            return lambda inp: inp["w_lru_proj"][0][:, m * 128:(m + 1) * 128].reshape(RC, 128, 128).transpose(1, 0, 2).reshape(128, RC * 128)
        itp = 0
        for m in range(KC):
            pslab, pslab_r = wload(RC * 128, lp_spec(m))
            pv = pslab[:, 0:RC * 128].rearrange("p (k m) -> p k m", k=RC)
            for hf in range(2):
                ba = itp % 4
                itp += 1
                for k in range(RC):
                    mm(psum[ba][:, :], pv[:, k, :], R0[:, k, hf * 512:(hf + 1) * 512], k == 0, k == RC - 1, [pslab_r, R0_r[k]], ps_r[ba])
                dst = MG[:, m, hf * 512:(hf + 1) * 512]
                P.op("dve", lambda e, ba=ba, dst=dst: e.tensor_tensor(out=dst, in0=psum[ba][:, :], in1=dst, op=ALU.mult),
                     reads=[ps_r[ba], MG_r[m]], writes=[MG_r[m]])
        if stop_after == "ML":
            dbg_dump("mrg", MG.rearrange("p a b -> p (a b)"), MG_r, KC * T)
            return finish(nc, P, W, specs, out_d, None)

        def proj_gate_stage(K, act_ap_fn, act_res_fn, pspec, gate_col0, bias_name, first, sg, sg_r, tA, tA_r):
            it_ = 0
            for mp in range(KC // 2):
                gslab, gslab_r = wload(4096, spec_k16([("w_in", gate_col0 + (2 * mp) * 128), ("w_in", gate_col0 + (2 * mp + 1) * 128)]))
                gv = gslab[:, :].rearrange("p (g k m) -> p g k m", g=2, k=KC)
                for mi in range(2):
                    m = 2 * mp + mi
                    if mi == 0:
                        pslab, pslab_r = wload(4096, pspec(mp))
                        pv2 = pslab[:, :].rearrange("p (g k m) -> p g k m", g=2, k=KC)
                    pv = pv2[:, mi]
                    for hf in range(2):
                        ba = (it_ % 2) * 2
                        bg = ba + 1
                        for k in range(K):
                            mm(psum[ba][:, :], pv[:, k, :], act_ap_fn(k, hf), k == 0, k == K - 1, [pslab_r, act_res_fn(k)], ps_r[ba])
                        rd = xn_reads(128 + hf * 512, 512)
                        for k in range(KC):
                            mm(psum[bg][:, :], gv[:, mi, k, :], xn[:, k, 128 + hf * 512:128 + (hf + 1) * 512],
                               k == 0, k == KC - 1, [gslab_r] + rd(k), ps_r[bg])
                        b = it_ % 2
                        P.op("act", lambda e, b=b, bg=bg, m=m: e.activation(out=sg[b], in_=psum[bg][:, :], func=AF.Sigmoid, bias=cc(bias_name, m)),
                             reads=[ps_r[bg], cst_r], writes=[sg_r[b]])
                        dst = MG[:, m, hf * 512:(hf + 1) * 512]
                        P.op("dve", lambda e, b=b, ba=ba: e.tensor_tensor(out=tA[b], in0=psum[ba][:, :], in1=sg[b], op=ALU.mult),
                             reads=[ps_r[ba], sg_r[b]], writes=[tA_r[b]])
                        P.op("dve", lambda e, b=b, dst=dst: e.tensor_tensor(out=dst, in0=tA[b], in1=dst, op=ALU.add),
                             reads=[tA_r[b], MG_r[m]], writes=[MG_r[m]])
                        it_ += 1

        P.barrier()
        AT = view(65536, [128, NQ, T])
        AT_r = [Res("AT%d" % h) for h in range(NQ)]
        QT = view(0, [128, 8, 4, 128])
        QT_r = [Res("QT%d" % qb) for qb in range(8)]
        KT = view(8192, [128, TE])
        KT_r = [Res("KT%d" % t) for t in range(9)]
        VT = view(10496, [128, 9, 512])
        VT_r = [Res("VT%d" % t) for t in range(9)]
        Ctab = view(19712, [128, TE], F32)
        Stab = view(24320, [128, TE], F32)
        tab_r = Res("tab")
        sqb = [view(28928, [128, 512]), view(28928 + 1024, [128, 512])]
        sqb_r = [Res("sqb0"), Res("sqb1")]
        rsd = [view(30976, [128, 512], F32), view(30976 + 2048, [128, 512], F32)]
        rsd_r = [Res("rsd0"), Res("rsd1")]
        qnf = [view(35072, [128, 512], F32), view(35072 + 2048, [128, 512], F32)]
        qnf_r = [Res("qnf0"), Res("qnf1")]
        qnb = [view(39168, [128, 512]), view(39168 + 1024, [128, 512])]
        qnb_r = [Res("qnb0"), Res("qnb1")]
        rt1 = [view(41216, [128, 512], F32), view(41216 + 2048, [128, 512], F32)]
        rt1_r = [Res("rt1_0"), Res("rt1_1")]
        rt2 = [view(45312, [128, 512], F32), view(45312 + 2048, [128, 512], F32)]
        rt2_r = [Res("rt2_0"), Res("rt2_1")]
        EB = [view(49408 + 1024 * i, [128, 512]) for i in range(4)]
        EB_r = [Res("EB%d" % i) for i in range(4)]
        dns = [view(53504, [128, 512], F32), view(53504 + 2048, [128, 512], F32)]
        dns_r = [Res("dns0"), Res("dns1")]
        tpi = view(65536 + 16384, [128, TE], I32)
        tA_ = view(65536 + 16384 + 4608, [128, TE], F32)
        tB_ = view(65536 + 16384 + 9216, [128, TE], F32)
        tmp_r = Res("ropetmp")
        pi_ = tpi[0:32, :]
        A_ = tA_[0:32, :]
        B_ = tB_[0:32, :]
        Bi_ = tB_[0:32, :].bitcast(I32)
        pf_ = tpi[0:32, :].bitcast(F32)
        P.dma("sp", lambda e: e.dma_start(out=pi_, in_=pos_d), P.dsem(), writes=[tmp_r])
        P.op("dve", lambda e: e.tensor_copy(out=A_, in_=pi_), reads=[tmp_r], writes=[tmp_r])
        P.op("dve", lambda e: e.tensor_scalar(out=A_, in0=A_, scalar1=cst[0:32, CC["invf"]:CC["invf"] + 1], scalar2=1.0 / (2 * math.pi),
                                              op0=ALU.mult, op1=ALU.mult), reads=[tmp_r, cst_r], writes=[tmp_r])
        for which, tab in ((0, Stab), (1, Ctab)):
            if which == 1:
                P.op("dve", lambda e: e.tensor_scalar(out=A_, in0=A_, scalar1=0.25, scalar2=None, op0=ALU.add), reads=[tmp_r], writes=[tmp_r])
            P.op("dve", lambda e: e.tensor_copy(out=Bi_, in_=A_), reads=[tmp_r], writes=[tmp_r])
            P.op("dve", lambda e: e.tensor_copy(out=pf_, in_=Bi_), reads=[tmp_r], writes=[tmp_r])
            P.op("dve", lambda e: e.tensor_tensor(out=pf_, in0=A_, in1=pf_, op=ALU.subtract), reads=[tmp_r], writes=[tmp_r])
            P.op("dve", lambda e: e.tensor_scalar(out=B_, in0=pf_, scalar1=0.5, scalar2=None, op0=ALU.is_gt), reads=[tmp_r], writes=[tmp_r])
            P.op("dve", lambda e: e.tensor_tensor(out=pf_, in0=pf_, in1=B_, op=ALU.subtract), reads=[tmp_r], writes=[tmp_r])
            P.op("dve", lambda e: e.tensor_scalar(out=B_, in0=pf_, scalar1=-0.5, scalar2=None, op0=ALU.is_lt), reads=[tmp_r], writes=[tmp_r])
            P.op("dve", lambda e: e.tensor_tensor(out=pf_, in0=pf_, in1=B_, op=ALU.add), reads=[tmp_r], writes=[tmp_r])
            P.op("act", lambda e, tab=tab: e.activation(out=tab[0:32, :], in_=pf_, func=AF.Sin, scale=2 * math.pi), reads=[tmp_r], writes=[tab_r])

        vs = []
        for half in range(2):
            def vspec(inp, half=half):
                wv = inp["w_in"][0][half * 1024:(half + 1) * 1024, SPL[1]:SPL[2]]
                return wv.reshape(8, 128, 512).transpose(1, 0, 2).reshape(128, 4096)
            vs.append(wload(4096, vspec))
        for t in range(9):
            bank = t % 2
            for k in range(KC):
                slab, slab_r = vs[k // 8]
                mm(psum[bank][:, :], xn[:, k, t * 128:(t + 1) * 128], slab[:, (k % 8) * 512:(k % 8 + 1) * 512],
                   k == 0, k == KC - 1, [slab_r, xn_r[k][t]], ps_r[bank])
            P.op("act", lambda e, t=t, bank=bank: e.activation(out=VT[:, t, :], in_=psum[bank][:, :], func=AF.Copy),
                 reads=[ps_r[bank]], writes=[VT_r[t]])

        def ph_A(it, i):
            pbk = 4 + i % 2
            c0, n = it["c0"], it["n"]
            rd = xn_reads(c0, n)
            for k in range(KC):
                mm(psum[pbk][:, 0:n], it["lhs"][:, k, :], xn[:, k, c0:c0 + n], k == 0, k == KC - 1, [it["lr"]] + rd(k), ps_r[pbk])
            P.op("act", lambda e: e.activation(out=sqb[i % 2][:, 0:n], in_=psum[pbk][:, 0:n], func=AF.Square), reads=[ps_r[pbk]], writes=[sqb_r[i % 2]])

        def ph_B(it, i):
            pbk = 4 + i % 2
            j = i % 2
            n = it["n"]
            pq = psum[pbk][:, 0:n]
            mm(psum[6][:, 0:n], ones_b, sqb[j][:, 0:n], True, True, [cmat_r, sqb_r[j]], ps_r[6])
            P.op("act", lambda e: e.activation(out=rsd[j][:, 0:n], in_=psum[6][:, 0:n], func=AF.Ln, scale=1.0 / HD, bias=cst[:, CC["eps"]:CC["eps"] + 1]),
                 reads=[ps_r[6], cst_r], writes=[rsd_r[j]])
            P.op("act", lambda e: e.activation(out=rsd[j][:, 0:n], in_=rsd[j][:, 0:n], func=AF.Exp, scale=-0.5), reads=[rsd_r[j]], writes=[rsd_r[j]])
            P.op("dve", lambda e: e.scalar_tensor_tensor(out=qnf[j][:, 0:n], in0=pq, scalar=cc(it["gname"]), in1=rsd[j][:, 0:n],
                                                         op0=ALU.mult, op1=ALU.mult),
                 reads=[ps_r[pbk], cst_r, rsd_r[j]], writes=[qnf_r[j]])
            P.op("act", lambda e: e.activation(out=qnb[j][:, 0:n], in_=qnf[j][:, 0:n], func=AF.Copy), reads=[qnf_r[j]], writes=[qnb_r[j]])

        def ph_C(it, i):
            j = i % 2
            c0, n = it["c0"], it["n"]
            mm(psum[7][0:32, 0:n], rotm, qnb[j][0:32, 0:n], True, True, [cmat_r, qnb_r[j]], ps_r[7])
            P.op("dve", lambda e: e.tensor_tensor(out=rt1[j][0:32, 0:n], in0=psum[7][0:32, 0:n], in1=Stab[0:32, c0:c0 + n], op=ALU.mult),
                 reads=[ps_r[7], tab_r], writes=[rt1_r[j]])
            P.op("dve", lambda e: e.tensor_tensor(out=rt2[j][0:32, 0:n], in0=qnf[j][0:32, 0:n], in1=Ctab[0:32, c0:c0 + n], op=ALU.mult),
                 reads=[qnf_r[j], tab_r], writes=[rt2_r[j]])
            P.op("dve", lambda e: e.tensor_tensor(out=qnb[j][0:32, 0:n], in0=rt1[j][0:32, 0:n], in1=rt2[j][0:32, 0:n], op=ALU.add),
                 reads=[rt1_r[j], rt2_r[j], qnb_r[j]], writes=[qnb_r[j]])
            dst_ap = it["dst"]
            srcv = qnb[j][:, 0:n]
            if len(dst_ap.shape) == 3:
                srcv = srcv.rearrange("p (a b) -> p a b", a=dst_ap.shape[1])
            P.op("act", lambda e: e.activation(out=dst_ap, in_=srcv, func=AF.Copy), reads=[qnb_r[j]], writes=it["dres"])

        gctr = {"i": 0}

        def run_pipeline(items):
            base = gctr["i"]
            nI = len(items)
            for st_ in range(nI + 2):
                if st_ < nI:
                    ph_A(items[st_], base + st_)
                if 0 <= st_ - 1 < nI:
                    ph_B(items[st_ - 1], base + st_ - 1)
                if 0 <= st_ - 2 < nI:
                    ph_C(items[st_ - 2], base + st_ - 2)
            gctr["i"] = base + nI

        scale = 1.0 / math.sqrt(HD)
        for g in range(NKV):
            items = []
            slab, slab_r = wload(4096, spec_k16([("w_in", SPL[0] + g * 128), ("w_in", g * 4 * 128)]))
            sv = slab[:, :].rearrange("p (g k m) -> p g k m", g=2, k=KC)
            for (c0, n) in ((0, 128), (128, 512), (640, 512)):
                items.append(dict(lhs=sv[:, 0], lr=slab_r, c0=c0, n=n, gname="kg", dst=KT[:, c0:c0 + n],
                                  dres=[KT_r[t] for t in range(c0 // 128, (c0 + n) // 128)]))
            for hh in range(4):
                h = g * 4 + hh
                if hh == 0:
                    qv, q_r = sv[:, 1], slab_r
                elif hh == 1:
                    slab2, slab2_r = wload(4096, spec_k16([("w_in", (h) * 128), ("w_in", (h + 1) * 128)]))
                    sv2 = slab2[:, :].rearrange("p (g k m) -> p g k m", g=2, k=KC)
                    qv, q_r = sv2[:, 0], slab2_r
                elif hh == 2:
                    qv, q_r = sv2[:, 1], slab2_r
                else:
                    slab3, slab3_r = wload(2048, spec_k16([("w_in", h * 128)]))
                    qv = slab3[:, 0:2048].rearrange("p (k m) -> p k m", k=KC)
                    q_r = slab3_r
                for hf in range(2):
                    items.append(dict(lhs=qv, lr=q_r, c0=128 + hf * 512, n=512, gname="qg", dst=QT[:, hf * 4:(hf + 1) * 4, hh, :],
                                      dres=[QT_r[qb] for qb in range(hf * 4, hf * 4 + 4)]))
            run_pipeline(items)
            def att_A(qb, g=g):
                eb = (qb % 2) * 2
                sb = (0, 1) if qb % 2 == 0 else (4, 5)
                for which in range(2):
                    kt = qb + which
                    bank = sb[which]
                    mm(psum[bank][:, :], KT[:, kt * 128:(kt + 1) * 128], QT[:, qb].rearrange("p a b -> p (a b)"), True, True,
                       [KT_r[kt], QT_r[qb]], ps_r[bank])
                    P.op("act", lambda e, bank=bank, which=which, eb=eb: e.activation(out=EB[eb + which], in_=psum[bank][:, :], func=AF.Exp, scale=scale),
                         reads=[ps_r[bank]], writes=[EB_r[eb + which]])
                    msk = mask_cur if which == 1 else (mask_prev0 if qb == 0 else mask_prev)
                    P.op("dve", lambda e, which=which, eb=eb, msk=msk: e.tensor_tensor(out=EB[eb + which], in0=EB[eb + which], in1=msk, op=ALU.mult),
                         reads=[EB_r[eb + which], cmat_r], writes=[EB_r[eb + which]])

            def att_B(qb, g=g):
                eb = (qb % 2) * 2
                for which in range(2):
                    kt = qb + which
                    mm(psum[2][:, :], VT[:, kt, g * 128:(g + 1) * 128], EB[eb + which], which == 0, which == 1, [VT_r[kt], EB_r[eb + which]], ps_r[2])
                for which in range(2):
                    mm(psum[3][:, :], ones_b, EB[eb + which], which == 0, which == 1, [cmat_r, EB_r[eb + which]], ps_r[3])
                d = qb % 2
                for hh in range(4):
                    h = g * 4 + hh
                    P.op("act", lambda e, d=d, hh=hh, h=h: e.activation(out=dns[d][:, hh * 128:(hh + 1) * 128], in_=psum[3][:, hh * 128:(hh + 1) * 128],
                                                                       func=AF.Ln, bias=sml[:, ESINK + h:ESINK + h + 1]),
                         reads=[ps_r[3], sml_r], writes=[dns_r[d]])
                P.op("act", lambda e, d=d: e.activation(out=dns[d], in_=dns[d], func=AF.Exp, scale=-1.0), reads=[dns_r[d]], writes=[dns_r[d]])
                extra = [tmp_r] if g >= 2 else []
                P.op("dve", lambda e, d=d, qb=qb, g=g: e.tensor_tensor(out=AT[:, g * 4:(g + 1) * 4, qb * 128:(qb + 1) * 128],
                                                                       in0=psum[2][:, :].rearrange("p (a b) -> p a b", a=4),
                                                                       in1=dns[d].rearrange("p (a b) -> p a b", a=4), op=ALU.mult),
                     reads=[ps_r[2], dns_r[d]], writes=[AT_r[g * 4 + hh] for hh in range(4)] + extra)
            for st_ in range(9):
                if st_ < 8:
                    att_A(st_)
                if st_ >= 1:
                    att_B(st_ - 1)
        dbg_dump("att", AT.rearrange("p a b -> p (a b)"), AT_r, KC * T)
        if stop_after == "AT":
            return finish(nc, P, W, specs, out_d, None)

        P.barrier()
        sg = [view(0, [128, 512], F32), view(2048, [128, 512], F32)]
        sg_r = [Res("sg0b"), Res("sg1b")]
        tA = [view(4096, [128, 512], F32), view(6144, [128, 512], F32)]
        tA_r = [Res("tA0b"), Res("tA1b")]

        def ap_spec(mp):
            return spec_k16([("w_attn_proj", (2 * mp) * 128), ("w_attn_proj", (2 * mp + 1) * 128)])
        proj_gate_stage(KC, lambda k, hf: AT[:, k, hf * 512:(hf + 1) * 512], lambda k: AT_r[k], ap_spec, SPL[4], "bga", False, sg, sg_r, tA, tA_r)
        dbg_dump("mrg", MG.rearrange("p a b -> p (a b)"), MG_r, KC * T)
        if stop_after == "MA":
            return finish(nc, P, W, specs, out_d, None)

        P.barrier()
        H = view(0, [128, 8, D], F32)
        H_r = [[Res("H%d_%d" % (t, n)) for n in range(4)] for t in range(8)]
        h_s = P.dsem()
        for t in range(8):
            P.dma("sp", lambda e, t=t: e.dma_start(out=H[:, t, :], in_=x_own[t * 128:(t + 1) * 128, :]), P.dsem(), writes=H_r[t])
        it_ = 0
        for n in range(4):
            ws = []
            for half in range(2):
                def wospec(inp, half=half, n=n):
                    w = inp["w_out"][0][half * 1024:(half + 1) * 1024, n * 512:(n + 1) * 512]
                    return w.reshape(8, 128, 512).transpose(1, 0, 2).reshape(128, 4096)
                ws.append(wload(4096, wospec))
            for t in range(8):
                bank = it_ % 4
                it_ += 1
                for k in range(KC):
                    slab, slab_r = ws[k // 8]
                    mm(psum[bank][:, :], MG[:, k, t * 128:(t + 1) * 128], slab[:, (k % 8) * 512:(k % 8 + 1) * 512],
                       k == 0, k == KC - 1, [slab_r, MG_r[k]], ps_r[bank])
                P.op("dve", lambda e, t=t, n=n, bank=bank: e.tensor_tensor(out=H[:, t, n * 512:(n + 1) * 512], in0=psum[bank][:, :],
                                                                            in1=H[:, t, n * 512:(n + 1) * 512], op=ALU.add),
                     reads=[ps_r[bank], H_r[t][n]], writes=[H_r[t][n]])
        dbg_dump("hh", H.rearrange("p a b -> p (a b)"), [r for t in range(8) for r in H_r[t]], 8 * D)
        if stop_after == "C":
            return finish(nc, P, W, specs, out_d, (H, H_r))
        P.barrier()
        hn = xn_t[:, 0:KC * T].rearrange("p (k t) -> p k t", k=KC)
        hn_r = [[Res("hn%d_%d" % (k, t)) for t in range(8)] for k in range(KC)]

        class _AllH:
            pass

        def c_src(i):
            r = Res("Hall%d" % i)
            r.w = H_r[i][3].w
            return H[:, i, :], r
        norm_transpose(c_src, 8, 1, hn, hn_r, 0, 65536)

        P.barrier()
        ACTB = [view(65536, [128, 4, T]), view(65536 + 8192, [128, 4, T])]
        ACTB_r = [[Res("act%d_%d" % (b, cc_)) for cc_ in range(4)] for b in range(2)]
        slb = [view(81920, [128, 512], F32), view(81920 + 2048, [128, 512], F32)]
        slb_r = [Res("sl0"), Res("sl1")]

        def hn_reads(hf):
            return lambda kc: [hn_r[kc][t] for t in range(hf * 4, hf * 4 + 4)]
        it2 = 0
        for gi in range(FC // 4):
            ab = gi % 2
            for pr in range(2):
                c0 = 4 * gi + 2 * pr
                gs, gs_r = wload(4096, spec_k16([("w_ffn_gate", c0 * 128), ("w_ffn_gate", (c0 + 1) * 128)]))
                us, us_r = wload(4096, spec_k16([("w_ffn_up", c0 * 128), ("w_ffn_up", (c0 + 1) * 128)]))
                gv = gs[:, :].rearrange("p (g k m) -> p g k m", g=2, k=KC)
                uv = us[:, :].rearrange("p (g k m) -> p g k m", g=2, k=KC)
                for ci in range(2):
                    cj = 2 * pr + ci
                    for hf in range(2):
                        bg_, bu_ = hf, 2 + hf
                        rd = hn_reads(hf)
                        for k in range(KC):
                            mm(psum[bg_][:, :], gv[:, ci, k, :], hn[:, k, hf * 512:(hf + 1) * 512], k == 0, k == KC - 1, [gs_r] + rd(k), ps_r[bg_])
                        for k in range(KC):
                            mm(psum[bu_][:, :], uv[:, ci, k, :], hn[:, k, hf * 512:(hf + 1) * 512], k == 0, k == KC - 1, [us_r] + rd(k), ps_r[bu_])
                        P.op("act", lambda e, hf=hf, bg_=bg_: e.activation(out=slb[hf], in_=psum[bg_][:, :], func=AF.Silu), reads=[ps_r[bg_]], writes=[slb_r[hf]])
                        P.op("dve", lambda e, hf=hf, bu_=bu_, ab=ab, cj=cj: e.tensor_tensor(out=ACTB[ab][:, cj, hf * 512:(hf + 1) * 512], in0=psum[bu_][:, :],
                                                                                           in1=slb[hf], op=ALU.mult),
                             reads=[ps_r[bu_], slb_r[hf]], writes=[ACTB_r[ab][cj]])
            dsl = []
            for pr in range(2):
                def dspec(inp, r0=(4 * gi + 2 * pr) * 128):
                    w = inp["w_ffn_down"][0][r0:r0 + 256, :]
                    return w.reshape(2, 128, D).transpose(1, 0, 2).reshape(128, 4096)
                ds_, ds_r = wload(4096, dspec)
                dsl.append((ds_[:, :].rearrange("p (c n) -> p c n", c=2), ds_r))
            for t in range(8):
                for n in range(4):
                    bank = 4 + it2 % 4
                    it2 += 1
                    for cj in range(4):
                        dv, ds_r = dsl[cj // 2]
                        mm(psum[bank][:, :], ACTB[ab][:, cj, t * 128:(t + 1) * 128], dv[:, cj % 2, n * 512:(n + 1) * 512], cj == 0, cj == 3,
                           [ACTB_r[ab][cj], ds_r], ps_r[bank])
                    P.op("dve", lambda e, t=t, n=n, bank=bank: e.tensor_tensor(out=H[:, t, n * 512:(n + 1) * 512], in0=psum[bank][:, :],
                                                                                in1=H[:, t, n * 512:(n + 1) * 512], op=ALU.add),
                         reads=[ps_r[bank], H_r[t][n]], writes=[H_r[t][n]])
        return finish(nc, P, W, specs, out_d, (H, H_r))


def finish(nc, P, W, specs, out_d, Hinfo):
    evs = []
    if Hinfo is not None:
        H, H_r = Hinfo
        for t in range(8):
            evs.append(P.dma("sp", lambda e, t=t: e.dma_start(out=out_d[t * 128:(t + 1) * 128, :], in_=H[:, t, :]), P.dsem(), reads=H_r[t]))
    P.barrier()
    P.wait("sp", evs)
    W["ap"] = nc.dram_tensor("wstream", [max(W["off"], 128)], F32, kind="ExternalInput").ap()
    P.emit()
    return nc, specs


def pack_weights(specs, inp):
    parts = []
    for n_el, fn in specs:
        a = np.ascontiguousarray(fn(inp), dtype=np.float32)
        assert a.shape == (128, n_el), (a.shape, n_el)
        parts.append(a.reshape(-1))
    if not parts:
        return np.zeros(128, np.float32)
    return np.concatenate(parts)


def make_in_maps(inp, specs):
    inp = {k: np.asarray(v) for k, v in inp.items()}
    wst = pack_weights(specs, inp)
    g12 = np.stack([np.broadcast_to(inp["norm1_g"][0][None, :], (128, D)),
                    np.broadcast_to(inp["norm2_g"][0][None, :], (128, D))]).astype(np.float32)
    maps = []
    for core in range(8):
        b, j = core // 4, core % 4
        t0 = j * T
        x_own = np.ascontiguousarray(inp["x"][b, t0:t0 + T])
        if j > 0:
            x_halo = np.ascontiguousarray(inp["x"][b, t0 - 128:t0])
            pos = inp["positions"][b, t0 - 128:t0 + T]
        else:
            x_halo = np.zeros((128, D), np.float32)
            pos = np.concatenate([np.zeros(128, np.int32), inp["positions"][b, 0:T]])
        maps.append({
            "x_own": x_own, "x_halo": x_halo,
            "pos": np.ascontiguousarray(np.broadcast_to(pos[None, :].astype(np.int32), (32, TE))),
            "consts": host_consts(inp, core), "cmat": host_cmat(core), "g12": g12, "wstream": wst,
        })
    return maps


_CACHE = {}


def kernel(**inputs):
    if "nc" not in _CACHE:
        _CACHE["nc"] = build_nc()
    nc, specs = _CACHE["nc"]
    maps = make_in_maps(inputs, specs)
    res = run_bass_kernel_spmd(nc, maps, core_ids=list(range(8)))
    out = np.zeros((2, S, D), np.float32)
    for core in range(8):
        b, j = core // 4, core % 4
        out[b, j * T:(j + 1) * T] = res.results[core]["out"]
    return out
```
